# Optimizing a Trainium2 kernel written in Bass

```python
import math
import jax
import jax.numpy as jnp
from jax import lax
import numpy as np

D_MODEL = 1024
BATCH = 32
SEQ = 2048
DEPTH = 4

GRID_W = 64
CTX_LEN = 256
D_FF = 4 * D_MODEL
NORM_EPS = 1e-6
ROPE_THETA = 10000.0

NA_HEADS = 4
NA_HEAD_DIM = 64
NA_WIDTH = NA_HEADS * NA_HEAD_DIM
NA_WIN_ROWS = 8
NA_WIN_COLS = 16

SSD_HEADS = 8
SSD_HEAD_DIM = 64
SSD_WIDTH = SSD_HEADS * SSD_HEAD_DIM
SSD_GROUPS = 2
SSD_STATE = 128
SSD_CONV = 5
SSD_CHUNK = 128
SSD_CONV_CH = SSD_WIDTH + 2 * SSD_GROUPS * SSD_STATE

MLA_HEADS = 4
MLA_NOPE = 64
MLA_ROPE = 32
MLA_V = 64
MLA_QK = MLA_NOPE + MLA_ROPE
MLA_Q_RANK = 256
MLA_KV_RANK = 128
MLA_WIDTH = MLA_HEADS * MLA_V
MLA_Q_BLOCK = 128

MIX_WIDTH = NA_WIDTH + SSD_WIDTH + MLA_WIDTH
IN_SIZES = (NA_WIDTH, NA_WIDTH, NA_WIDTH, SSD_WIDTH, SSD_CONV_CH, 2 * SSD_HEADS, MLA_Q_RANK, MLA_KV_RANK, MLA_ROPE)
IN_WIDTH = sum(IN_SIZES)
IN_SPLITS = tuple(int(s) for s in np.cumsum(IN_SIZES)[:-1])

kernel_name = 'hybrid_na_ssd_mla_dit_block'


def rmsnorm(t, w):
    tf = t.astype(jnp.float32)
    tf = tf * lax.rsqrt(jnp.mean(tf * tf, axis=-1, keepdims=True) + NORM_EPS)
    return tf.astype(t.dtype) * w


def modulate(t, shift, scale):
    return t * (1 + scale) + shift


def split_heads(t, n_heads):
    return t.reshape(t.shape[0], t.shape[1], n_heads, t.shape[-1] // n_heads)


def squared_relu_mlp(h, w1, w2):
    return jnp.square(jax.nn.relu(h @ w1)) @ w2


def full_attention(q, k, v):
    s = jnp.einsum('bqhd,bkhd->bhqk', q, k).astype(jnp.float32) * (q.shape[-1] ** -0.5)
    p = jax.nn.softmax(s, axis=-1).astype(v.dtype)
    o = jnp.einsum('bhqk,bkhd->bqhd', p, v)
    return o.reshape(o.shape[0], o.shape[1], -1)


def blocked_attention(q, k, v):
    b, l, h, d = q.shape
    blk = min(MLA_Q_BLOCK, l)
    qb = q.reshape(b, l // blk, blk, h, d).swapaxes(0, 1)
    o = lax.map(lambda qi: full_attention(qi, k, v), qb)
    return o.swapaxes(0, 1).reshape(b, l, -1)


def axial_rope_tables(n_tokens):
    pos = jnp.arange(n_tokens)
    axes = jnp.stack([pos // GRID_W, pos % GRID_W], axis=-1).astype(jnp.float32)
    n_freq = MLA_ROPE // 4
    inv_freq = ROPE_THETA ** (-jnp.arange(n_freq, dtype=jnp.float32) / n_freq)
    ang = axes[:, :, None] * inv_freq
    return jnp.cos(ang), jnp.sin(ang)


def axial_rope(t, cos, sin):
    shp = t.shape
    t = t.reshape(shp[:-1] + (2, 2, shp[-1] // 4))
    t1, t2 = t[..., 0, :], t[..., 1, :]
    cos = cos[:, None].astype(t.dtype)
    sin = sin[:, None].astype(t.dtype)
    return jnp.stack([t1 * cos - t2 * sin, t2 * cos + t1 * sin], axis=-2).reshape(shp)


def rotate_rope_dims(t, cos, sin):
    if cos is None:
        return t
    return jnp.concatenate([t[..., :MLA_NOPE], axial_rope(t[..., MLA_NOPE:], cos, sin)], axis=-1)


def neighbourhood_attention(q, k, v, kc, vc, rpb, n_rows):
    b, l, h, d = q.shape
    wr = min(NA_WIN_ROWS, n_rows)
    wc = NA_WIN_COLS
    n_win = wr * GRID_W
    scale = d ** -0.5
    qg = q.reshape(b, n_rows, GRID_W, h, d)
    kg = k.reshape(b, n_rows, GRID_W, h, d)
    vg = v.reshape(b, n_rows, GRID_W, h, d)
    cols = jnp.arange(GRID_W)
    col_start = jnp.clip(cols - wc // 2, 0, GRID_W - wc)
    col_in = (cols[None, :] >= col_start[:, None]) & (cols[None, :] < col_start[:, None] + wc)
    col_idx = jnp.clip(cols[None, :] - cols[:, None], -(wc - 1), wc - 1) + (wc - 1)

    def row_block(r):
        rs = jnp.clip(r - wr // 2, 0, n_rows - wr)
        kw = lax.dynamic_slice_in_dim(kg, rs, wr, axis=1)
        vw = lax.dynamic_slice_in_dim(vg, rs, wr, axis=1)
        qr = lax.dynamic_index_in_dim(qg, r, axis=1, keepdims=False)
        row_idx = rs + jnp.arange(wr) - r + (NA_WIN_ROWS - 1)
        bias = rpb[:, row_idx[None, :, None], col_idx[:, None, :]]
        s_win = jnp.einsum('bqhd,bwkhd->bhqwk', qr, kw).astype(jnp.float32) * scale + bias.astype(jnp.float32)
        s_win = jnp.where(col_in[:, None, :], s_win, -jnp.inf).reshape(b, h, GRID_W, n_win)
        s_ctx = jnp.einsum('bqhd,bchd->bhqc', qr, kc).astype(jnp.float32) * scale
        p = jax.nn.softmax(jnp.concatenate([s_win, s_ctx], axis=-1), axis=-1).astype(v.dtype)
        p_win = p[..., :n_win].reshape(b, h, GRID_W, wr, GRID_W)
        p_ctx = p[..., n_win:]
        return jnp.einsum('bhqwk,bwkhd->bqhd', p_win, vw) + jnp.einsum('bhqc,bchd->bqhd', p_ctx, vc)

    out = lax.map(row_block, jnp.arange(n_rows))
    return out.transpose(1, 0, 2, 3, 4).reshape(b, l, h * d)


def depthwise_conv_centred(t, w, bias):
    pad = w.shape[0] // 2
    y = lax.conv_general_dilated(t, w[:, None, :], window_strides=(1,), padding=[(pad, pad)],
                                 dimension_numbers=('NWC', 'WIO', 'NWC'), feature_group_count=t.shape[-1])
    return y + bias


def segsum(t):
    n = t.shape[-1]
    tr = jnp.broadcast_to(t[..., :, None], t.shape + (n,))
    tr = jnp.where(jnp.tril(jnp.ones((n, n), bool), -1), tr, 0)
    cs = jnp.cumsum(tr, axis=-2)
    return jnp.where(jnp.tril(jnp.ones((n, n), bool)), cs, -jnp.inf)


def ssd_scan(x, dt, a, bm, cm, h0, return_y):
    b, l, h, p = x.shape
    g, n = bm.shape[-2], bm.shape[-1]
    e = h // g
    q = min(SSD_CHUNK, l)
    nc = l // q
    xd = (x * dt[..., None]).reshape(b, nc, q, g, e, p)
    adt = (dt * a).reshape(b, nc, q, g, e).transpose(0, 3, 4, 1, 2)
    bm = bm.reshape(b, nc, q, g, n)
    cm = cm.reshape(b, nc, q, g, n)
    a_cum = jnp.cumsum(adt, axis=-1)
    decay_to_end = jnp.exp(a_cum[..., -1:] - a_cum)
    chunk_states = jnp.einsum('bcsgn,bgecs,bcsgep->bcgepn', bm, decay_to_end, xd)
    states = jnp.concatenate([h0.reshape(b, 1, g, e, p, n), chunk_states], axis=1)
    chunk_decay = jnp.exp(segsum(jnp.pad(a_cum[..., -1], ((0, 0), (0, 0), (0, 0), (1, 0)))))
    states = jnp.einsum('bgezc,bcgepn->bzgepn', chunk_decay, states)
    final = states[:, -1].reshape(b, h, p, n)
    if not return_y:
        return None, final
    lmat = jnp.exp(segsum(adt))
    cb = jnp.einsum('bclgn,bcsgn->bgcls', cm, bm)
    y_diag = jnp.einsum('bgcls,bgecls,bcsgep->bclgep', cb, lmat, xd)
    y_off = jnp.einsum('bclgn,bcgepn,bgecl->bclgep', cm, states[:, :-1], jnp.exp(a_cum))
    return (y_diag + y_off).reshape(b, l, h, p), final


def ssd_mixer(z, xbc, dtr, zc, xbcc, dtrc, conv_w, conv_b, dt_bias, a_log, d_skip, norm_w, need_ctx):
    def conv_split(t):
        t = jax.nn.silu(depthwise_conv_centred(t, conv_w, conv_b))
        xs, bs, cs = jnp.split(t, [SSD_WIDTH, SSD_WIDTH + SSD_GROUPS * SSD_STATE], axis=-1)
        bb, ll = t.shape[0], t.shape[1]
        return (xs.reshape(bb, ll, SSD_HEADS, SSD_HEAD_DIM),
                bs.reshape(bb, ll, SSD_GROUPS, SSD_STATE),
                cs.reshape(bb, ll, SSD_GROUPS, SSD_STATE))

    def step_sizes(t):
        return jax.nn.softplus(t.reshape(t.shape[0], t.shape[1], 2, SSD_HEADS) + dt_bias)

    def flip(t):
        return jnp.flip(t, axis=1)

    xl, bl, cl = conv_split(xbc)
    xc, bc, cc = conv_split(xbcc)
    dtl = step_sizes(dtr)
    dtc = step_sizes(dtrc)
    a = -jnp.exp(a_log)
    h0 = jnp.zeros((xc.shape[0], SSD_HEADS, SSD_HEAD_DIM, SSD_STATE), xc.dtype)
    yc_f, hc_f = ssd_scan(xc, dtc[:, :, 0], a[0], bc, cc, h0, need_ctx)
    yl_f, _ = ssd_scan(xl, dtl[:, :, 0], a[0], bl, cl, hc_f, True)
    yc_b, hc_b = ssd_scan(flip(xc), flip(dtc[:, :, 1]), a[1], flip(bc), flip(cc), h0, need_ctx)
    yl_b, _ = ssd_scan(flip(xl), flip(dtl[:, :, 1]), a[1], flip(bl), flip(cl), hc_b, True)

    def gate_norm(y_f, y_b, xs, zz):
        y = y_f + flip(y_b) + d_skip[:, None] * xs
        y = y.reshape(zz.shape) * jax.nn.silu(zz)
        return rmsnorm(y, norm_w)

    out = gate_norm(yl_f, yl_b, xl, z)
    out_c = gate_norm(yc_f, yc_b, xc, zc) if need_ctx else None
    return out, out_c


def mla_queries(cq, cq_norm_w, w_uq, qn_w, cos, sin):
    q = split_heads(rmsnorm(cq, cq_norm_w) @ w_uq, MLA_HEADS)
    return rotate_rope_dims(rmsnorm(q, qn_w), cos, sin)


def mla_keys_values(ckv, k_rope, ckv_norm_w, w_ukv, kn_w, cos, sin):
    kv = split_heads(rmsnorm(ckv, ckv_norm_w) @ w_ukv, MLA_HEADS)
    k_nope, v = jnp.split(kv, [MLA_NOPE], axis=-1)
    b, l = k_rope.shape[0], k_rope.shape[1]
    k_r = jnp.broadcast_to(k_rope[:, :, None, :], (b, l, MLA_HEADS, MLA_ROPE))
    k = jnp.concatenate([k_nope, k_r], axis=-1)
    return rotate_rope_dims(rmsnorm(k, kn_w), cos, sin), v


def token_mixers(u, uc, need_ctx, n_rows, rope_cos, rope_sin,
                 na_qn_w, na_kn_w, na_rpb,
                 conv_w, conv_b, dt_bias, a_log, d_skip, ssd_norm_w,
                 cq_norm_w, ckv_norm_w, w_uq, w_ukv, mla_qn_w, mla_kn_w):
    q_na, k_na, v_na, z, xbc, dtr, cq, ckv, kr = jnp.split(u, IN_SPLITS, axis=-1)
    q_nac, k_nac, v_nac, zc, xbcc, dtrc, cqc, ckvc, krc = jnp.split(uc, IN_SPLITS, axis=-1)

    k_nac_h = rmsnorm(split_heads(k_nac, NA_HEADS), na_kn_w)
    v_nac_h = split_heads(v_nac, NA_HEADS)
    out_na = neighbourhood_attention(rmsnorm(split_heads(q_na, NA_HEADS), na_qn_w),
                                     rmsnorm(split_heads(k_na, NA_HEADS), na_kn_w),
                                     split_heads(v_na, NA_HEADS), k_nac_h, v_nac_h, na_rpb, n_rows)

    out_ssd, out_ssd_c = ssd_mixer(z, xbc, dtr, zc, xbcc, dtrc, conv_w, conv_b, dt_bias, a_log, d_skip,
                                   ssd_norm_w, need_ctx)

    k_mc, v_mc = mla_keys_values(ckvc, krc, ckv_norm_w, w_ukv, mla_kn_w, None, None)
    k_m, v_m = mla_keys_values(ckv, kr, ckv_norm_w, w_ukv, mla_kn_w, rope_cos, rope_sin)
    q_m = mla_queries(cq, cq_norm_w, w_uq, mla_qn_w, rope_cos, rope_sin)
    out_mla = blocked_attention(q_m, jnp.concatenate([k_m, k_mc], axis=1), jnp.concatenate([v_m, v_mc], axis=1))

    mix = jnp.concatenate([out_na, out_ssd, out_mla], axis=-1)
    if not need_ctx:
        return mix, None
    out_na_c = full_attention(rmsnorm(split_heads(q_nac, NA_HEADS), na_qn_w), k_nac_h, v_nac_h)
    out_mla_c = full_attention(mla_queries(cqc, cq_norm_w, w_uq, mla_qn_w, None, None), k_mc, v_mc)
    mix_c = jnp.concatenate([out_na_c, out_ssd_c, out_mla_c], axis=-1)
    return mix, mix_c


def setup_inputs(seed: int = 0) -> dict:
    key = jax.random.key(seed)
    k = jax.random.split(key, 27)
    f32 = jnp.float32
    L = DEPTH

    def normal(i, shape, scale):
        return jax.random.normal(k[i], shape, f32) * scale

    def gain(i, shape):
        return 1.0 + 0.05 * jax.random.normal(k[i], shape, f32)

    u_dt = jax.random.uniform(k[15], (L, 2, SSD_HEADS), f32)
    dt0 = jnp.exp(math.log(1e-3) + u_dt * (math.log(1e-1) - math.log(1e-3)))
    dt_bias = dt0 + jnp.log(-jnp.expm1(-dt0))
    a_log = jnp.log(jax.random.uniform(k[16], (L, 2, SSD_HEADS), f32, minval=1.0, maxval=16.0))
    return {
        'x': normal(0, (BATCH, SEQ, D_MODEL), 1.0),
        'c': normal(1, (BATCH, D_MODEL), 1.0),
        'ctx': normal(2, (BATCH, CTX_LEN, D_MODEL), 1.0),
        'c_ctx': normal(3, (D_MODEL,), 1.0),
        'w_ada': normal(4, (L, D_MODEL, 6 * D_MODEL), 0.5 * D_MODEL ** -0.5),
        'b_ada': normal(5, (L, 6 * D_MODEL), 0.02),
        'norm1_w': gain(6, (L, D_MODEL)),
        'norm2_w': gain(7, (L, D_MODEL)),
        'w_in': normal(8, (L, D_MODEL, IN_WIDTH), D_MODEL ** -0.5),
        'w_out': normal(9, (L, MIX_WIDTH, D_MODEL), MIX_WIDTH ** -0.5),
        'na_qn_w': gain(10, (L, NA_HEAD_DIM)),
        'na_kn_w': gain(11, (L, NA_HEAD_DIM)),
        'na_rpb': normal(12, (L, NA_HEADS, 2 * NA_WIN_ROWS - 1, 2 * NA_WIN_COLS - 1), 0.1),
        'ssd_conv_w': normal(13, (L, SSD_CONV, SSD_CONV_CH), SSD_CONV ** -0.5),
        'ssd_conv_b': normal(14, (L, SSD_CONV_CH), 0.02),
        'ssd_dt_bias': dt_bias,
        'ssd_a_log': a_log,
        'ssd_d': 1.0 + 0.1 * jax.random.normal(k[17], (L, SSD_HEADS), f32),
        'ssd_norm_w': gain(18, (L, SSD_WIDTH)),
        'mla_cq_norm_w': gain(19, (L, MLA_Q_RANK)),
        'mla_ckv_norm_w': gain(20, (L, MLA_KV_RANK)),
        'mla_w_uq': normal(21, (L, MLA_Q_RANK, MLA_HEADS * MLA_QK), MLA_Q_RANK ** -0.5),
        'mla_w_ukv': normal(22, (L, MLA_KV_RANK, MLA_HEADS * (MLA_NOPE + MLA_V)), MLA_KV_RANK ** -0.5),
        'mla_qn_w': gain(23, (L, MLA_QK)),
        'mla_kn_w': gain(24, (L, MLA_QK)),
        'w_ff1': normal(25, (L, D_MODEL, D_FF), D_MODEL ** -0.5),
        'w_ff2': normal(26, (L, D_FF, D_MODEL), D_FF ** -0.5),
    }


def reference(x, c, ctx, c_ctx, w_ada, b_ada, norm1_w, norm2_w, w_in, w_out,
              na_qn_w, na_kn_w, na_rpb, ssd_conv_w, ssd_conv_b, ssd_dt_bias, ssd_a_log, ssd_d, ssd_norm_w,
              mla_cq_norm_w, mla_ckv_norm_w, mla_w_uq, mla_w_ukv, mla_qn_w, mla_kn_w, w_ff1, w_ff2):
    n_tokens = x.shape[1]
    n_rows = n_tokens // GRID_W
    rope_cos, rope_sin = axial_rope_tables(n_tokens)
    c_act = jax.nn.silu(c)
    c_ctx_act = jax.nn.silu(c_ctx)
    for i in range(DEPTH):
        need_ctx = i < DEPTH - 1
        sh1, sc1, g1, sh2, sc2, g2 = jnp.split((c_act @ w_ada[i] + b_ada[i])[:, None, :], 6, axis=-1)
        sh1c, sc1c, g1c, sh2c, sc2c, g2c = jnp.split(c_ctx_act @ w_ada[i] + b_ada[i], 6, axis=-1)
        u = modulate(rmsnorm(x, norm1_w[i]), sh1, sc1) @ w_in[i]
        uc = modulate(rmsnorm(ctx, norm1_w[i]), sh1c, sc1c) @ w_in[i]
        mix, mix_c = token_mixers(u, uc, need_ctx, n_rows, rope_cos, rope_sin,
                                  na_qn_w[i], na_kn_w[i], na_rpb[i],
                                  ssd_conv_w[i], ssd_conv_b[i], ssd_dt_bias[i], ssd_a_log[i], ssd_d[i], ssd_norm_w[i],
                                  mla_cq_norm_w[i], mla_ckv_norm_w[i], mla_w_uq[i], mla_w_ukv[i], mla_qn_w[i], mla_kn_w[i])
        x = x + g1 * (mix @ w_out[i])
        x = x + g2 * squared_relu_mlp(modulate(rmsnorm(x, norm2_w[i]), sh2, sc2), w_ff1[i], w_ff2[i])
        if need_ctx:
            ctx = ctx + g1c * (mix_c @ w_out[i])
            ctx = ctx + g2c * squared_relu_mlp(modulate(rmsnorm(ctx, norm2_w[i]), sh2c, sc2c), w_ff1[i], w_ff2[i])
    return x
```

```python
import numpy as np
from contextlib import ExitStack
import concourse.bass as bass
import concourse.mybir as mybir
from concourse.bass_utils import run_bass_kernel_spmd

F32 = mybir.dt.float32
BF16 = mybir.dt.bfloat16
AF = mybir.ActivationFunctionType
ALU = mybir.AluOpType

D = 1024
SEQ = 2048
CTX = 256
T = SEQ + CTX
DFF = 4096
EPS = 1e-6
NEG = -30000.0
NCORES = 8


class Buf:
    __slots__ = ("w", "r")

    def __init__(self):
        self.w = None
        self.r = []


class TT:
    __slots__ = ("t", "b", "ps")

    def __init__(self, t, ps=False):
        self.t = t
        self.b = Buf()
        self.ps = ps


class Sched:
    ND = 40

    def __init__(self, nc, es):
        self.nc = nc
        self.E = dict(pe=nc.tensor, dve=nc.vector, act=nc.scalar, pool=nc.gpsimd, sp=nc.sync)
        self.csem = {e: es.enter_context(nc.semaphore("c_" + e)) for e in ("pe", "dve", "act", "pool")}
        self.ccnt = {e: 0 for e in self.csem}
        self.dsems = [es.enter_context(nc.semaphore("d%d" % i)) for i in range(self.ND)]
        self.dcnt = [0] * self.ND
        self.dnext = 0
        self.dnext_pool = 0
        self.waited = {e: {} for e in self.E}
        self.n_inst = 0

    def _wait(self, eng, ev):
        key, sem, val = ev[1], ev[2], ev[3]
        if self.waited[eng].get(key, 0) >= val:
            return
        self.E[eng].wait_ge(sem, val)
        self.waited[eng][key] = val

    def _deps(self, eng, reads, writes):
        for b in reads:
            if b.w is not None and not (b.w[0] == eng == "pe"):
                self._wait(eng, b.w)
        for b in writes:
            if b.w is not None and not (b.w[0] == eng == "pe"):
                self._wait(eng, b.w)
            for ev in b.r:
                if not (ev[0] == eng == "pe"):
                    self._wait(eng, ev)

    def _update(self, ev, reads, writes):
        for b in reads:
            b.r = [e for e in b.r if e[1] != ev[1]] + [ev]
        for b in writes:
            b.w = ev
            b.r = []

    def op(self, eng, fn, reads=(), writes=()):
        writes = list(writes) + [x for x in reads if isinstance(x, TT) and x.ps]
        reads = [x.b if isinstance(x, TT) else x for x in reads if not (isinstance(x, TT) and x.ps)]
        writes = [x.b if isinstance(x, TT) else x for x in writes]
        self._deps(eng, reads, writes)
        inst = fn(self.E[eng])
        self.ccnt[eng] += 1
        inst.then_inc(self.csem[eng], 1)
        ev = (eng, "c_" + eng, self.csem[eng], self.ccnt[eng])
        self._update(ev, reads, writes)
        self.n_inst += 1

    def dma(self, q, out, in_, reads=(), writes=()):
        reads = [x.b if isinstance(x, TT) else x for x in reads]
        writes = [x.b if isinstance(x, TT) else x for x in writes]
        half = self.ND // 2
        if q == "pool":
            i = half + self.dnext_pool
            self.dnext_pool = (self.dnext_pool + 1) % half
        else:
            i = self.dnext
            self.dnext = (self.dnext + 1) % half
        key = "d%d" % i
        if self.dcnt[i] > 0:
            self._wait(q, ("dma", key, self.dsems[i], self.dcnt[i]))
        self._deps(q, reads, writes)
        inst = self.E[q].dma_start(out=out, in_=in_)
        self.dcnt[i] += 16
        inst.then_inc(self.dsems[i], 16)
        ev = ("dma", key, self.dsems[i], self.dcnt[i])
        self._update(ev, reads, writes)
        self.n_inst += 1

    def barrier(self):
        evs = [(e, "c_" + e, self.csem[e], self.ccnt[e]) for e in self.csem if self.ccnt[e] > 0]
        evs += [("dma", "d%d" % i, self.dsems[i], self.dcnt[i]) for i in range(self.ND) if self.dcnt[i] > 0]
        for eng in self.E:
            for ev in evs:
                if ev[0] != eng:
                    self._wait(eng, ev)


def _lhsT_layout(W, cols_list, mc=128):
    K = W.shape[0]
    nk = K // 128
    out = np.zeros((len(cols_list), 128, nk, mc), np.float32)
    Wr = W.reshape(nk, 128, W.shape[1])
    for i, cols in enumerate(cols_list):
        cols = np.asarray(cols)
        out[i, :, :, :len(cols)] = Wr[:, :, cols].transpose(1, 0, 2)
    return out


def _na_bias_tables(rpb):
    H = 4
    kc = np.arange(64)
    qc = np.arange(64)
    col_start = np.clip(qc - 8, 0, 48)
    col_in = (kc[:, None] >= col_start[None, :]) & (kc[:, None] < col_start[None, :] + 16)
    col_idx = np.clip(kc[:, None] - qc[None, :], -15, 15) + 15
    tiles = []

    def tile_for(qr0, kr0):
        tl = np.full((H, 2, 64, 8, 64), NEG, np.float32)
        for p in range(2):
            kr = kr0 + p
            for j in range(8):
                qr = qr0 + j
                rs = min(max(qr - 4, 0), 24)
                if rs <= kr < rs + 8:
                    ridx = kr - qr + 7
                    blk = rpb[:, ridx][:, col_idx]
                    blk = np.where(col_in[None], blk, np.float32(NEG))
                    tl[:, p, :, j, :] = blk
        return tl.reshape(H, 128, 512)

    for c in range(6):
        tiles.append(tile_for(0, 2 * c))
    for c in range(8):
        tiles.append(tile_for(8, 4 + 2 * c))
    for c in range(6):
        tiles.append(tile_for(24, 20 + 2 * c))
    return np.stack(tiles, axis=1)


NA_GROUPS = [
    (0, 6, 0), (256, 8, 6), (768, 8, 6), (1280, 6, 14)]


def _consts():
    c = {}
    c["ident"] = np.eye(128, dtype=np.float32)
    c["ones"] = np.ones((128, 128), np.float32)
    b = np.zeros((128, 128), np.float32)
    b[:64, :64] = 1
    b[64:, 64:] = 1
    c["blk64"] = b
    pos = np.arange(SEQ)
    axes = np.stack([pos // 64, pos % 64], -1).astype(np.float32)
    inv = (10000.0 ** (-np.arange(8, dtype=np.float32) / 8)).astype(np.float32)
    ang = axes[:, :, None] * inv
    cos = np.cos(ang).astype(np.float32)
    sin = np.sin(ang).astype(np.float32)
    cosT = np.ones((128, T), np.float32)
    sinT = np.zeros((128, T), np.float32)
    for ax in range(2):
        base = 64 + ax * 16
        cosT[base:base + 8, :SEQ] = cos[:, ax].T
        cosT[base + 8:base + 16, :SEQ] = cos[:, ax].T
        sinT[base:base + 8, :SEQ] = sin[:, ax].T
        sinT[base + 8:base + 16, :SEQ] = sin[:, ax].T
    c["cosT"] = cosT
    c["sinT"] = sinT
    P = np.zeros((128, 128), np.float32)
    for ax in range(2):
        base = 64 + ax * 16
        for i in range(8):
            P[base + i, base + 8 + i] = -1.0
            P[base + 8 + i, base + i] = 1.0
    c["prot"] = np.ascontiguousarray(P.T)
    k = np.arange(128)
    c["tri_f"] = (k[:, None] <= k[None, :]).astype(np.float32)
    c["tri_b"] = (k[:, None] >= k[None, :]).astype(np.float32)
    c["negm_f"] = np.where(k[:, None] <= k[None, :], 0.0, NEG).astype(np.float32)
    c["negm_b"] = np.where(k[:, None] >= k[None, :], 0.0, NEG).astype(np.float32)
    sel = np.zeros((128, 64), np.float32)
    sel[64, :] = 1.0
    c["sel64"] = sel
    return c


def _prep_weights(inp):
    L = inp["w_ada"].shape[0]
    f32 = np.float32
    w = {}
    wa = np.asarray(inp["w_ada"], f32)
    w["wada"] = np.ascontiguousarray(wa.reshape(L, 8, 128, 48, 128).transpose(0, 3, 2, 1, 4))
    w["bada"] = np.ascontiguousarray(np.asarray(inp["b_ada"], f32).reshape(L, 48, 128).transpose(0, 2, 1))
    w["nw1"] = np.ascontiguousarray(np.asarray(inp["norm1_w"], f32).reshape(L, 8, 128).transpose(0, 2, 1))
    w["nw2"] = np.ascontiguousarray(np.asarray(inp["norm2_w"], f32).reshape(L, 8, 128).transpose(0, 2, 1))
    win = np.asarray(inp["w_in"], f32)
    fm_cols = [np.arange(0, 128), np.arange(128, 256), np.arange(256, 384), np.arange(384, 512)]
    fm_cols += [np.arange(1280 + 128 * i, 1280 + 128 * (i + 1)) for i in range(8)]
    fm_cols += [np.arange(2320, 2448), np.arange(2448, 2576), np.arange(2576, 2704)]
    winfm = np.zeros((L, 16, 128, 8, 128), f32)
    for l in range(L):
        winfm[l, :15] = _lhsT_layout(win[l], fm_cols)
        kr = win[l][:, 2704:2736].reshape(8, 128, 32).transpose(1, 0, 2)
        winfm[l, 15, :, :, 64:96] = kr
    w["winfm"] = winfm
    w["winz"] = np.ascontiguousarray(win[:, :, 768:1280].reshape(L, 8, 128, 512).transpose(0, 2, 1, 3))
    vdt = np.zeros((L, 128, 8, 272), f32)
    vdt[..., :256] = win[:, :, 512:768].reshape(L, 8, 128, 256).transpose(0, 2, 1, 3)
    vdt[..., 256:272] = win[:, :, 2304:2320].reshape(L, 8, 128, 16).transpose(0, 2, 1, 3)
    w["winvdt"] = vdt
    wuq = np.asarray(inp["mla_w_uq"], f32)
    w["wuq"] = np.ascontiguousarray(wuq.reshape(L, 2, 128, 4, 96).transpose(0, 2, 3, 1, 4))
    wukv = np.asarray(inp["mla_w_ukv"], f32).reshape(L, 128, 4, 128)
    wk = np.zeros((L, 128, 4, 96), f32)
    wk[..., :64] = wukv[..., :64]
    w["wukvk"] = wk
    w["wukvv"] = np.ascontiguousarray(wukv[..., 64:].reshape(L, 128, 256))
    wo = np.asarray(inp["w_out"], f32)
    w["wout"] = np.ascontiguousarray(wo.reshape(L, 8, 128, 8, 128).transpose(0, 3, 2, 1, 4))
    w1 = np.asarray(inp["w_ff1"], f32)
    w["wff1"] = np.ascontiguousarray(w1.reshape(L, 8, 128, 32, 128).transpose(0, 3, 2, 1, 4))
    w2 = np.asarray(inp["w_ff2"], f32)
    w["wff2"] = np.ascontiguousarray(w2.reshape(L, 32, 128, 8, 128).transpose(0, 3, 2, 1, 4))
    vec = np.zeros((L, 128, 32), f32)
    vec[:, :, 0] = np.tile(np.asarray(inp["na_qn_w"], f32), (1, 2))
    vec[:, :, 1] = np.tile(np.asarray(inp["na_kn_w"], f32), (1, 2))
    vec[:, :, 2:4] = np.asarray(inp["mla_cq_norm_w"], f32).reshape(L, 2, 128).transpose(0, 2, 1)
    vec[:, :, 4] = np.asarray(inp["mla_ckv_norm_w"], f32)
    vec[:, :96, 5] = np.asarray(inp["mla_qn_w"], f32)
    vec[:, :96, 6] = np.asarray(inp["mla_kn_w"], f32)
    vec[:, :, 8:16] = np.asarray(inp["ssd_conv_b"], f32).reshape(L, 8, 128).transpose(0, 2, 1)
    w["vec"] = vec
    w["convw"] = np.ascontiguousarray(np.asarray(inp["ssd_conv_w"], f32).reshape(L, 5, 8, 128).transpose(0, 3, 2, 1))
    row = np.zeros((L, 552), f32)
    row[:, 0:16] = np.asarray(inp["ssd_dt_bias"], f32).reshape(L, 16)
    row[:, 16:32] = np.asarray(inp["ssd_a_log"], f32).reshape(L, 16)
    row[:, 32:40] = np.asarray(inp["ssd_d"], f32)
    row[:, 40:552] = np.asarray(inp["ssd_norm_w"], f32)
    w["row"] = row
    rpb = np.asarray(inp["na_rpb"], f32)
    w["nab"] = np.stack([_na_bias_tables(rpb[l]) for l in range(L)], 0)
    return w


BF_W = ["winfm", "winz", "winvdt", "wuq", "wukvk", "wukvv", "wout", "wff1", "wff2", "nab"]


TILES = [(0, 512), (512, 512), (1024, 512), (1536, 512), (2048, 256)]


class Prog:
    def __init__(self, NB, L, dbg=()):
        self.NB, self.L, self.J = NB, L, NB + 1
        self.dbg = set(dbg)
        self.nc = bass.Bass("TRN2", target_bir_lowering=False)
        self.es = ExitStack()
        self.s = Sched(self.nc, self.es)
        self.dr = {}
        self.drb = {}

    def din(self, name, shape, dt=F32):
        self.dr[name] = self.nc.dram_tensor(name, list(shape), dt, kind="ExternalInput").ap()
        self.drb[name] = Buf()

    def dscr(self, name, shape, dt):
        kind = "ExternalOutput" if name in self.dbg else "Internal"
        self.dr[name] = self.nc.dram_tensor(name, list(shape), dt, kind=kind).ap()
        self.drb[name] = Buf()

    def sb(self, es, name, shape, dt):
        self.uid = getattr(self, "uid", 0) + 1
        return TT(es.enter_context(self.nc.sbuf_tensor("sb%d_%s" % (self.uid, name), list(shape), dt)))

    def wb(self, name, l):
        return self.drb.setdefault((name, l), Buf())

    def psum(self):
        p = self.ps[self.psi % 7]
        self.psi += 1
        return p

    def warm(self, n=1, cols=512):
        for _ in range(n):
            self.s.op("pe", lambda e: e.matmul(self.ps[7].t[:, 0:cols], self.ident.t[:], self.wrhs.t[:, 0:cols], start=True, stop=True))

    def rot(self, lst, key):
        i = self.rr.get(key, 0)
        self.rr[key] = i + 1
        return lst[i % len(lst)]

    def build(self):
        nc, s, es, NB, L, J = self.nc, self.s, self.es, self.NB, self.L, self.J
        self.rr = {}
        self.din("xT", [NB, D, T])
        self.din("cT", [128, 8, J])
        wshapes = dict(wada=[48, 128, 8, 128], bada=[128, 48], nw1=[128, 8], nw2=[128, 8],
                       winfm=[16, 128, 8, 128], winz=[128, 8, 512], winvdt=[128, 8, 272],
                       wuq=[128, 4, 2, 96], wukvk=[128, 4, 96], wukvv=[128, 256],
                       wout=[8, 128, 8, 128], wff1=[32, 128, 8, 128], wff2=[8, 128, 32, 128],
                       vec=[128, 32], convw=[128, 8, 5], row=[552], nab=[4, 20, 128, 512])
        self.wshapes = wshapes
        for k, shp in wshapes.items():
            self.din(k, [L] + shp)
        cshapes = dict(ident=[128, 128], ones=[128, 128], blk64=[128, 128], cosT=[128, T], sinT=[128, T],
                       prot=[128, 128], tri_f=[128, 128], tri_b=[128, 128], negm_f=[128, 128], negm_b=[128, 128],
                       sel64=[128, 64])
        for k, shp in cshapes.items():
            self.din(k, shp)
        self.dr["out"] = nc.dram_tensor("out", [NB, D, SEQ], F32, kind="ExternalOutput").ap()
        self.drb["out"] = Buf()
        for k in BF_W:
            self.dscr(k + "_bf", [L] + wshapes[k], BF16)
        self.dscr("XS", [NB, D, T], F32)
        self.dscr("QKNA", [NB, 512, T], BF16)
        self.dscr("VNA", [NB, T, 256], BF16)
        self.dscr("ZS", [NB, T, 512], BF16)
        self.dscr("XBC", [NB, D, T], BF16)
        self.dscr("DTS", [NB, T, 16], F32)
        self.dscr("QM", [NB, 4, 96, T], BF16)
        self.dscr("KM", [NB, 4, 96, T], BF16)
        self.dscr("VM", [NB, T, 256], BF16)
        self.dscr("MIXT", [NB, D, T], BF16)
        dr, drb = self.dr, self.drb

        pairs = [es.enter_context(nc.psum_tensor("pp%d" % i, [128, 1024], F32)) for i in range(4)]
        self.ps = [TT(pairs[i // 2][:, (i % 2) * 512:(i % 2 + 1) * 512], ps=True) for i in range(8)]
        self.pp = [TT(pairs[i][:, :], ps=True) for i in range(4)]
        self.psi = 0

        def convert(l):
            for k in BF_W:
                shp = wshapes[k]
                src, dst = dr[k][l], dr[k + "_bf"][l]
                if len(shp) == 4 and (shp[1] == 128 or k == "nab"):
                    step = max(1, (1 << 20) // (shp[1] * shp[2] * shp[3]))
                    if k == "nab":
                        for h in range(4):
                            for i in range(0, shp[1], 5):
                                s.dma("pool", dst[h, i:i + 5], src[h, i:i + 5], writes=[self.wb(k + "_bf", l)])
                    else:
                        for i in range(0, shp[0], step):
                            s.dma("pool", dst[i:i + step], src[i:i + step], writes=[self.wb(k + "_bf", l)])
                else:
                    s.dma("pool", dst, src, writes=[self.wb(k + "_bf", l)])
        self.convert = convert
        convert(0)

        def cload(name, dt, rows=128):
            t = self.sb(es, "c_" + name, cshapes[name], dt)
            s.dma("pool", t.t[:], dr[name], writes=[t])
            return t
        self.ident = cload("ident", BF16)
        self.ones = cload("ones", BF16)
        self.blk64 = cload("blk64", BF16)
        self.prot = cload("prot", BF16)
        self.onesf = cload("ones", F32) if False else None
        self.tri = [cload("tri_f", F32), cload("tri_b", F32)]
        self.negmb = [cload("negm_f", BF16), cload("negm_b", BF16)]
        self.sel64 = cload("sel64", F32)
        self.onesf = self.sb(es, "onesf", [128, 128], F32)
        s.dma("sp", self.onesf.t[:], dr["ones"], writes=[self.onesf])
        self.wrhs = self.sb(es, "wrhs", [128, 512], BF16)
        s.op("dve", lambda e: e.memset(self.wrhs.t[:], 1.0), writes=[self.wrhs])
        self.cst = self.sb(es, "cst", [128, 8], F32)
        import math
        for i, v in enumerate([EPS, math.log(0.125), math.log(96 ** -0.5), 1.0, 0.0]):
            s.op("dve", lambda e: e.memset(self.cst.t[:, i:i + 1], v), writes=[self.cst])
        self.sc = self.sb(es, "silu_c", [128, 8, J], F32)
        s.dma("sp", self.sc.t[:], dr["cT"], writes=[self.sc])
        s.op("act", lambda e: e.activation(out=self.sc.t[:], in_=self.sc.t[:], func=AF.Silu), reads=[self.sc], writes=[self.sc])
        self.mod = self.sb(es, "mod", [128, 48, J], F32)
        self.A1 = self.sb(es, "A1", [128, 8, J], F32)
        self.A2 = self.sb(es, "A2", [128, 8, J], F32)
        self.vec = self.sb(es, "vec", [128, 32], F32)
        self.nw = self.sb(es, "nw", [128, 16], F32)
        self.bada = self.sb(es, "bada", [128, 48], F32)
        self.rowbc = self.sb(es, "rowbc", [128, 552], F32)
        self.convw = self.sb(es, "convw", [128, 8, 5], F32)
        s.barrier()

        for l in range(L):
            last = (l == L - 1)
            import os
            stop = int(os.environ.get("KSTOP", "9"))
            if stop >= 1:
                self.phase_A(l)
                s.barrier()
            if l + 1 < L:
                self.convert(l + 1)
            if stop >= 2:
                self.phase_B(l)
                s.barrier()
            if stop >= 3:
                self.phase_att(l, last)
                s.barrier()
            if stop >= 4:
                self.phase_ssd(l, last)
                s.barrier()
            if stop >= 5:
                self.phase_FG(l, last)
                s.barrier()
        s.barrier()
        self.es.close()
        return nc

    def phase_A(self, l):
        nc, s, J = self.nc, self.s, self.J
        dr, drb = self.dr, self.drb
        with ExitStack() as es:
            wts = [self.sb(es, "wada%d" % i, [128, 6, 8, 128], F32) for i in range(2)]
            s.dma("sp", self.vec.t[:], dr["vec"][l], writes=[self.vec])
            s.dma("sp", self.nw.t[:, 0:8], dr["nw1"][l], writes=[self.nw])
            s.dma("sp", self.nw.t[:, 8:16], dr["nw2"][l], writes=[self.nw])
            s.dma("sp", self.bada.t[:], dr["bada"][l], writes=[self.bada])
            s.dma("sp", self.rowbc.t[:], dr["row"][l].partition_broadcast(128), writes=[self.rowbc])
            s.dma("sp", self.convw.t[:], dr["convw"][l], writes=[self.convw])
            ps = self.psum()
            for g in range(8):
                wt = wts[g % 2]
                s.dma("sp", wt.t[:], dr["wada"][l, g * 6:(g + 1) * 6].rearrange("c p k f -> p c k f"), writes=[wt])
                for c in range(6):
                    fc = g * 6 + c
                    for k in range(8):
                        s.op("pe", lambda e: e.matmul(ps.t[:, fc * J:(fc + 1) * J], wt.t[:, c, k, :], self.sc.t[:, k, :],
                                                      start=(k == 0), stop=(k == 7)), reads=[wt, self.sc], writes=[ps])
            s.op("dve", lambda e: e.tensor_tensor(out=self.mod.t[:], in0=ps.t[:, 0:48 * J].rearrange("p (c j) -> p c j", j=J),
                                                  in1=self.bada.t[:].unsqueeze(2).to_broadcast([128, 48, J]), op=ALU.add),
                 reads=[ps, self.bada], writes=[self.mod])
            for (A, off, nwo) in ((self.A1, 8, 0), (self.A2, 32, 8)):
                s.op("dve", lambda e: e.scalar_tensor_tensor(out=A.t[:], in0=self.mod.t[:, off:off + 8, :], scalar=1.0,
                                                             in1=self.nw.t[:, nwo:nwo + 8].unsqueeze(2).to_broadcast([128, 8, J]),
                                                             op0=ALU.add, op1=ALU.mult),
                     reads=[self.mod, self.nw], writes=[A])

    def norm_mod(self, xt, n, A, sh_off, j, xm, sq, rstd, tmp):
        s = self.s
        s.op("act", lambda e: e.activation(out=sq.t[:, :, 0:n], in_=xt.t[:, :, 0:n], func=AF.Square), reads=[xt], writes=[sq])
        ps = self.psum()
        for k in range(8):
            s.op("pe", lambda e: e.matmul(ps.t[:, 0:n], self.ones.t[:], sq.t[:, k, 0:n], start=(k == 0), stop=(k == 7)),
                 reads=[sq, self.ones], writes=[ps])
        s.op("act", lambda e: e.activation(out=rstd.t[:, 0:n], in_=ps.t[:, 0:n], func=AF.Ln, scale=1.0 / D, bias=self.cst.t[:, 0:1]),
             reads=[ps, self.cst], writes=[rstd])
        s.op("act", lambda e: e.activation(out=rstd.t[:, 0:n], in_=rstd.t[:, 0:n], func=AF.Exp, scale=-0.5), reads=[rstd], writes=[rstd])
        for k in range(8):
            tm = self.rot(tmp, "nm_tmp")
            s.op("dve", lambda e: e.scalar_tensor_tensor(out=tm.t[:, 0:n], in0=xt.t[:, k, 0:n], scalar=A.t[:, k, j:j + 1],
                                                         in1=rstd.t[:, 0:n], op0=ALU.mult, op1=ALU.mult),
                 reads=[xt, A, rstd], writes=[tm])
            s.op("act", lambda e: e.activation(out=xm.t[:, k, 0:n], in_=tm.t[:, 0:n], func=AF.Identity,
                                               bias=self.mod.t[:, sh_off + k, j:j + 1]),
                 reads=[tm, self.mod], writes=[xm])

    def fm_rstd(self, sqs, ones_ap, ones_tt, P, n, cnt, rstd, lnb=None):
        s = self.s
        ps = self.psum()
        for i, (tt, ap) in enumerate(sqs):
            s.op("pe", lambda e: e.matmul(ps.t[0:P, 0:n], ones_ap, ap, start=(i == 0), stop=(i == len(sqs) - 1)),
                 reads=[tt, ones_tt], writes=[ps])
        s.op("act", lambda e: e.activation(out=rstd.t[0:P, 0:n], in_=ps.t[0:P, 0:n], func=AF.Ln, scale=1.0 / cnt, bias=self.cst.t[0:P, 0:1]),
             reads=[ps, self.cst], writes=[rstd])
        if lnb is None:
            s.op("act", lambda e: e.activation(out=rstd.t[0:P, 0:n], in_=rstd.t[0:P, 0:n], func=AF.Exp, scale=-0.5),
                 reads=[rstd], writes=[rstd])
        else:
            s.op("act", lambda e: e.activation(out=rstd.t[0:P, 0:n], in_=rstd.t[0:P, 0:n], func=AF.Exp, scale=-0.5,
                                               bias=self.cst.t[0:P, lnb:lnb + 1]),
                 reads=[rstd, self.cst], writes=[rstd])

    def phase_B(self, l):
        nc, s, J, NB = self.nc, self.s, self.J, self.NB
        dr, drb = self.dr, self.drb
        with ExitStack() as es:
            winfm = self.sb(es, "winfm", [128, 16, 8, 128], BF16)
            winz = self.sb(es, "winz", [128, 8, 512], BF16)
            winvdt = self.sb(es, "winvdt", [128, 8, 272], BF16)
            wuq = self.sb(es, "wuq", [128, 4, 2, 96], BF16)
            wukvk = self.sb(es, "wukvk", [128, 4, 96], BF16)
            wukvv = self.sb(es, "wukvv", [128, 256], BF16)
            cosb = [self.sb(es, "cosb%d" % i, [128, 512], F32) for i in range(2)]
            sinb = [self.sb(es, "sinb%d" % i, [128, 512], F32) for i in range(2)]
            for i in range(0, 16, 4):
                s.dma("sp", winfm.t[:, i:i + 4], dr["winfm_bf"][l, i:i + 4].rearrange("c p k f -> p c k f"),
                      reads=[self.wb("winfm_bf", l)], writes=[winfm])
            for (tt, nm) in ((winz, "winz_bf"), (winvdt, "winvdt_bf"), (wuq, "wuq_bf"), (wukvk, "wukvk_bf"), (wukvv, "wukvv_bf")):
                s.dma("sp", tt.t[:], dr[nm][l], reads=[self.wb(nm, l)], writes=[tt])
            xts = [self.sb(es, "xt%d" % i, [128, 8, 512], F32) for i in range(2)]
            sq = self.sb(es, "sq", [128, 8, 512], BF16)
            xm = self.sb(es, "xm", [128, 8, 512], BF16)
            rstd = self.sb(es, "rstd", [128, 512], F32)
            tmp = [self.sb(es, "tmp%d" % i, [128, 512], F32) for i in range(8)]
            ubuf = [self.sb(es, "u%d" % i, [128, 512], F32) for i in range(5)]
            sqb = [self.sb(es, "sqb%d" % i, [128, 512], BF16) for i in range(5)]
            rs2 = [self.sb(es, "rs2_%d" % i, [128, 512], F32) for i in range(4)]
            obf = [self.sb(es, "obf%d" % i, [128, 512], BF16) for i in range(8)]
            cqn = self.sb(es, "cqn", [128, 2, 512], BF16)
            ckvn = self.sb(es, "ckvn", [128, 512], BF16)
            krp = self.sb(es, "krp", [128, 512], F32)
            vst = [self.sb(es, "vst%d" % i, [128, 4, 256], BF16) for i in range(2)]
            dst_ = [self.sb(es, "dst%d" % i, [128, 4, 16], F32) for i in range(2)]
            zst = [self.sb(es, "zst%d" % i, [128, 4, 512], BF16) for i in range(2)]
            vmst = [self.sb(es, "vmst%d" % i, [128, 4, 256], BF16) for i in range(2)]
            xsrc = "xT" if l == 0 else "XS"

            def head_group(items, n, t0, cs_t):
                for it in items:
                    it["ps"] = self.psum()
                    it["mm"](it["ps"])
                for it in items:
                    P = it["P"]
                    u = it["u"] = self.rot(ubuf, "u")
                    q = it["q"] = self.rot(sqb, "sqb")
                    src_ps = it["ps"]
                    if it["extra"] is None:
                        s.op("act", lambda e: e.activation(out=u.t[0:P, 0:n], in_=src_ps.t[0:P, 0:n], func=AF.Copy), reads=[src_ps], writes=[u])
                    else:
                        s.op("dve", lambda e: e.tensor_tensor(out=u.t[0:P, 0:n], in0=src_ps.t[0:P, 0:n], in1=it["extra"].t[0:P, 0:n], op=ALU.add),
                             reads=[src_ps, it["extra"]], writes=[u])
                for it in items:
                    P, u, q = it["P"], it["u"], it["q"]
                    s.op("act", lambda e: e.activation(out=q.t[0:P, 0:n], in_=u.t[0:P, 0:n], func=AF.Square), reads=[u], writes=[q])
                for it in items:
                    P, q = it["P"], it["q"]
                    ps = it["ps2"] = self.psum()
                    if P == 128:
                        s.op("pe", lambda e: e.matmul(ps.t[:, 0:n], self.blk64.t[:], q.t[:, 0:n], start=True, stop=True), reads=[q, self.blk64], writes=[ps])
                    else:
                        s.op("pe", lambda e: e.matmul(ps.t[0:P, 0:n], self.ones.t[0:P, 0:P], q.t[0:P, 0:n], start=True, stop=True), reads=[q, self.ones], writes=[ps])
                for it in items:
                    P, ps = it["P"], it["ps2"]
                    r = it["r"] = self.rot(rs2, "rs2")
                    cnt = 64 if P == 128 else P
                    s.op("act", lambda e: e.activation(out=r.t[0:P, 0:n], in_=ps.t[0:P, 0:n], func=AF.Ln, scale=1.0 / cnt, bias=self.cst.t[0:P, 0:1]),
                         reads=[ps, self.cst], writes=[r])
                for it in items:
                    P, r, lnb = it["P"], it["r"], it["lnb"]
                    if lnb is None:
                        s.op("act", lambda e: e.activation(out=r.t[0:P, 0:n], in_=r.t[0:P, 0:n], func=AF.Exp, scale=-0.5), reads=[r], writes=[r])
                    else:
                        s.op("act", lambda e: e.activation(out=r.t[0:P, 0:n], in_=r.t[0:P, 0:n], func=AF.Exp, scale=-0.5, bias=self.cst.t[0:P, lnb:lnb + 1]),
                             reads=[r, self.cst], writes=[r])
                for it in items:
                    P, u, r, wcol = it["P"], it["u"], it["r"], it["wcol"]
                    o = it["o"] = self.rot(obf, "obf")
                    s.op("dve", lambda e: e.scalar_tensor_tensor(out=o.t[0:P, 0:n], in0=u.t[0:P, 0:n], scalar=self.vec.t[0:P, wcol:wcol + 1],
                                                                 in1=r.t[0:P, 0:n], op0=ALU.mult, op1=ALU.mult),
                         reads=[u, self.vec, r], writes=[o])
                if items[0]["P"] == 96:
                    P = 96
                    cT_, sT_ = cs_t
                    for it in items:
                        o = it["o"]
                        pr = it["pr"] = self.psum()
                        s.op("pe", lambda e: e.matmul(pr.t[0:P, 0:n], self.prot.t[0:P, 0:P], o.t[0:P, 0:n], start=True, stop=True),
                             reads=[o, self.prot], writes=[pr])
                    for it in items:
                        o, pr = it["o"], it["pr"]
                        t1 = it["t1"] = self.rot(tmp, "nm_tmp")
                        t2 = it["t2"] = self.rot(tmp, "nm_tmp")
                        s.op("dve", lambda e: e.tensor_tensor(out=t1.t[0:P, 0:n], in0=pr.t[0:P, 0:n], in1=sT_.t[0:P, 0:n], op=ALU.mult),
                             reads=[pr, sT_], writes=[t1])
                        s.op("pool", lambda e: e.tensor_tensor(out=t2.t[0:P, 0:n], in0=o.t[0:P, 0:n], in1=cT_.t[0:P, 0:n], op=ALU.mult),
                             reads=[o, cT_], writes=[t2])
                    for it in items:
                        t1, t2 = it["t1"], it["t2"]
                        o2 = self.rot(obf, "obf")
                        s.op("dve", lambda e: e.tensor_tensor(out=o2.t[0:P, 0:n], in0=t1.t[0:P, 0:n], in1=t2.t[0:P, 0:n], op=ALU.add),
                             reads=[t1, t2], writes=[o2])
                        it["o"] = o2
                for it in items:
                    P, o = it["P"], it["o"]
                    s.dma("pool", it["dst"], o.t[0:P, 0:n], reads=[o], writes=[drb[it["dname"]]])

            for b in range(NB):
                for ti, (t0, n) in enumerate(TILES):
                    j = NB if ti == 4 else b
                    xt = self.rot(xts, "xt")
                    s.dma("sp", xt.t[:, :, 0:n], dr[xsrc][b].rearrange("(k p) t -> p k t", p=128)[:, :, t0:t0 + n],
                          reads=[drb[xsrc]], writes=[xt])
                    cT_, sT_ = self.rot(cosb, "cosb"), self.rot(sinb, "sinb")
                    s.dma("sp", cT_.t[:, 0:n], dr["cosT"][:, t0:t0 + n], writes=[cT_])
                    s.dma("sp", sT_.t[:, 0:n], dr["sinT"][:, t0:t0 + n], writes=[sT_])
                    self.norm_mod(xt, n, self.A1, 0, j, xm, sq, rstd, tmp)
                    nt = n // 128

                    def proj_mm(fc, P=128):
                        def f(ps):
                            for k in range(8):
                                s.op("pe", lambda e: e.matmul(ps.t[0:P, 0:n], winfm.t[:, fc, k, 0:P], xm.t[:, k, 0:n], start=(k == 0), stop=(k == 7)),
                                     reads=[winfm, xm], writes=[ps])
                        return f

                    def proj(fc, P=128):
                        ps = self.psum()
                        proj_mm(fc, P)(ps)
                        return ps
                    head_group([dict(mm=proj_mm(fc), P=128, wcol=0 if fc < 2 else 1, lnb=1 if fc < 2 else None,
                                     dst=dr["QKNA"][b, fc * 128:(fc + 1) * 128, t0:t0 + n], dname="QKNA", extra=None) for fc in range(4)], n, t0, None)
                    for c in range(8):
                        ps = proj(4 + c)
                        o = self.rot(obf, "obf")
                        if c % 2 == 0:
                            s.op("act", lambda e: e.activation(out=o.t[:, 0:n], in_=ps.t[:, 0:n], func=AF.Copy), reads=[ps], writes=[o])
                        else:
                            s.op("dve", lambda e: e.tensor_copy(out=o.t[:, 0:n], in_=ps.t[:, 0:n]), reads=[ps], writes=[o])
                        s.dma("pool", dr["XBC"][b, c * 128:(c + 1) * 128, t0:t0 + n], o.t[:, 0:n], reads=[o], writes=[drb["XBC"]])
                    vs, ds, zs = self.rot(vst, "vst"), self.rot(dst_, "dst"), self.rot(zst, "zst")
                    for tc in range(nt):
                        ps = self.psum()
                        for k in range(8):
                            s.op("pe", lambda e: e.matmul(ps.t[:, 0:272], xm.t[:, k, tc * 128:(tc + 1) * 128], winvdt.t[:, k, :], start=(k == 0), stop=(k == 7)),
                                 reads=[winvdt, xm], writes=[ps])
                        s.op("act", lambda e: e.activation(out=vs.t[:, tc, :], in_=ps.t[:, 0:256], func=AF.Copy), reads=[ps], writes=[vs])
                        s.op("act", lambda e: e.activation(out=ds.t[:, tc, :], in_=ps.t[:, 256:272], func=AF.Copy), reads=[ps], writes=[ds])
                        ps2 = self.psum()
                        for k in range(8):
                            s.op("pe", lambda e: e.matmul(ps2.t[:, :], xm.t[:, k, tc * 128:(tc + 1) * 128], winz.t[:, k, :], start=(k == 0), stop=(k == 7)),
                                 reads=[winz, xm], writes=[ps2])
                        s.op("dve", lambda e: e.tensor_copy(out=zs.t[:, tc, :], in_=ps2.t[:, :]), reads=[ps2], writes=[zs])
                    s.dma("pool", dr["VNA"][b, t0:t0 + n, :].rearrange("(c p) f -> p c f", p=128), vs.t[:, 0:nt, :], reads=[vs], writes=[drb["VNA"]])
                    s.dma("pool", dr["DTS"][b, t0:t0 + n, :].rearrange("(c p) f -> p c f", p=128), ds.t[:, 0:nt, :], reads=[ds], writes=[drb["DTS"]])
                    s.dma("pool", dr["ZS"][b, t0:t0 + n, :].rearrange("(c p) f -> p c f", p=128), zs.t[:, 0:nt, :], reads=[zs], writes=[drb["ZS"]])
                    us = []
                    for c in range(2):
                        ps = proj(12 + c)
                        u = self.rot(ubuf, "u")
                        s.op("act", lambda e: e.activation(out=u.t[:, 0:n], in_=ps.t[:, 0:n], func=AF.Copy), reads=[ps], writes=[u])
                        q = self.rot(sqb, "sqb")
                        s.op("act", lambda e: e.activation(out=q.t[:, 0:n], in_=ps.t[:, 0:n], func=AF.Square), reads=[ps], writes=[q])
                        us.append((u, q))
                    ps = proj(14)
                    ukv = self.rot(ubuf, "u")
                    s.op("act", lambda e: e.activation(out=ukv.t[:, 0:n], in_=ps.t[:, 0:n], func=AF.Copy), reads=[ps], writes=[ukv])
                    qkv = self.rot(sqb, "sqb")
                    s.op("act", lambda e: e.activation(out=qkv.t[:, 0:n], in_=ps.t[:, 0:n], func=AF.Square), reads=[ps], writes=[qkv])
                    ps = proj(15, 96)
                    s.op("dve", lambda e: e.tensor_copy(out=krp.t[0:96, 0:n], in_=ps.t[0:96, 0:n]), reads=[ps], writes=[krp])
                    r = self.rot(rs2, "rs2")
                    self.fm_rstd([(q, q.t[:, 0:n]) for (u, q) in us], self.ones.t[:], self.ones, 128, n, 256, r)
                    r2 = self.rot(rs2, "rs2")
                    self.fm_rstd([(qkv, qkv.t[:, 0:n])], self.ones.t[:], self.ones, 128, n, 128, r2)
                    for c in range(2):
                        s.op("dve", lambda e: e.scalar_tensor_tensor(out=cqn.t[:, c, 0:n], in0=us[c][0].t[:, 0:n], scalar=self.vec.t[:, 2 + c:3 + c],
                                                                     in1=r.t[:, 0:n], op0=ALU.mult, op1=ALU.mult),
                             reads=[us[c][0], self.vec, r], writes=[cqn])
                    s.op("dve", lambda e: e.scalar_tensor_tensor(out=ckvn.t[:, 0:n], in0=ukv.t[:, 0:n], scalar=self.vec.t[:, 4:5],
                                                                 in1=r2.t[:, 0:n], op0=ALU.mult, op1=ALU.mult),
                         reads=[ukv, self.vec, r2], writes=[ckvn])

                    def uq_mm(h):
                        def f(ps):
                            for c in range(2):
                                s.op("pe", lambda e: e.matmul(ps.t[0:96, 0:n], wuq.t[:, h, c, :], cqn.t[:, c, 0:n], start=(c == 0), stop=(c == 1)),
                                     reads=[wuq, cqn], writes=[ps])
                        return f

                    def uk_mm(h):
                        def f(ps):
                            s.op("pe", lambda e: e.matmul(ps.t[0:96, 0:n], wukvk.t[:, h, :], ckvn.t[:, 0:n], start=True, stop=True),
                                 reads=[wukvk, ckvn], writes=[ps])
                        return f
                    head_group([dict(mm=uq_mm(h), P=96, wcol=5, lnb=2, dst=dr["QM"][b, h, :, t0:t0 + n], dname="QM", extra=None) for h in range(4)],
                               n, t0, (cT_, sT_))
                    head_group([dict(mm=uk_mm(h), P=96, wcol=6, lnb=None, dst=dr["KM"][b, h, :, t0:t0 + n], dname="KM", extra=krp) for h in range(4)],
                               n, t0, (cT_, sT_))
                    vm = self.rot(vmst, "vmst")
                    for tc in range(nt):
                        ps = self.psum()
                        s.op("pe", lambda e: e.matmul(ps.t[:, 0:256], ckvn.t[:, tc * 128:(tc + 1) * 128], wukvv.t[:], start=True, stop=True),
                             reads=[wukvv, ckvn], writes=[ps])
                        s.op("act", lambda e: e.activation(out=vm.t[:, tc, :], in_=ps.t[:, 0:256], func=AF.Copy), reads=[ps], writes=[vm])
                    s.dma("pool", dr["VM"][b, t0:t0 + n, :].rearrange("(c p) f -> p c f", p=128), vm.t[:, 0:nt, :], reads=[vm], writes=[drb["VM"]])

    def phase_att(self, l, last):
        nc, s, NB = self.nc, self.s, self.NB
        dr, drb = self.dr, self.drb
        with ExitStack() as es:
            kT = [self.sb(es, "kT%d" % i, [128, T], BF16) for i in range(2)]
            va = [self.sb(es, "va%d" % i, [128, 18, 128], BF16) for i in range(2)]
            qT = [self.sb(es, "qT%d" % i, [128, 512], BF16) for i in range(3)]
            bias = [self.sb(es, "bias%d" % i, [128, 8, 512], BF16) for i in range(2)]
            pT = [self.sb(es, "pT%d" % i, [128, 2, 512], BF16) for i in range(3)]
            sbias = [self.sb(es, "sbias%d" % i, [128, 2, 512], F32) for i in range(2)]
            osb = [self.sb(es, "osb%d" % i, [128, 512], F32) for i in range(2)]
            rrow = [self.sb(es, "rrow%d" % i, [128, 512], F32) for i in range(2)]
            omix = [self.sb(es, "omix%d" % i, [64, 512], BF16) for i in range(2)]
            for i in range(2):
                s.op("dve", lambda e: e.memset(va[i].t[:, :, 64:128], 1.0), writes=[va[i]])
            accs = [self.ps[4], self.ps[5]]
            wp = self.pp[0:2]
            pbank = [self.ps[6]]
            self.warm(24)
            self.att_tail = []

            def attend(b, d, k_t, v_t, q_src_ap, nq, chunks, bias_t, mix_rows, q_tok0):
                q = self.rot(qT, "qT")
                s.dma("sp", q.t[0:d, 0:nq], q_src_ap, reads=[drb["QKNA"], drb["QM"]], writes=[q])
                acc = self.rot(accs, "acc")
                npair = len(chunks) // 2
                pend = []

                def pv(item):
                    pi, p = item
                    for jj in range(2):
                        ci = 2 * pi + jj
                        kc = chunks[ci][0]
                        s.op("pe", lambda e: e.matmul(acc.t[:, 0:nq], v_t.t[:, kc, :], p.t[:, jj, 0:nq], start=(ci == 0), stop=(ci == len(chunks) - 1)),
                             reads=[v_t, p], writes=[acc])
                for pi in range(npair):
                    ps = self.rot(wp, "wp")
                    psv = ps.t.rearrange("p (j n) -> p j n", j=2)
                    for jj in range(2):
                        kc = chunks[2 * pi + jj][0]
                        s.op("pe", lambda e: e.matmul(ps.t[:, jj * 512:jj * 512 + nq], k_t.t[0:d, kc * 128:(kc + 1) * 128], q.t[0:d, 0:nq], start=True, stop=True),
                             reads=[k_t, q], writes=[ps])
                    bs = chunks[2 * pi][1]
                    p = self.rot(pT, "pT")
                    if bs is not None:
                        sb_ = self.rot(sbias, "sbias")
                        s.op("dve", lambda e: e.tensor_tensor(out=sb_.t[:, :, 0:nq], in0=psv[:, :, 0:nq], in1=bias_t.t[:, bs:bs + 2, 0:nq], op=ALU.add),
                             reads=[ps, bias_t], writes=[sb_])
                        s.op("act", lambda e: e.activation(out=p.t[:, :, 0:nq], in_=sb_.t[:, :, 0:nq], func=AF.Exp), reads=[sb_], writes=[p])
                    else:
                        s.op("act", lambda e: e.activation(out=p.t[:, :, 0:nq], in_=psv[:, :, 0:nq], func=AF.Exp), reads=[ps], writes=[p])
                    pend.append((pi, p))
                    if len(pend) > 1:
                        pv(pend.pop(0))
                    if pi in (0, 3) and self.att_tail:
                        self.att_tail.pop(0)()
                while pend:
                    pv(pend.pop(0))
                while self.att_tail:
                    self.att_tail.pop(0)()
                st = {}
                self.att_tail = [lambda: _tail1(st, nq, acc), lambda: _tail2(st, b, nq, mix_rows, q_tok0)]

            def _tail1(st, nq, acc):
                rr = st["rr"] = self.rot(rrow, "rrow")
                s.op("dve", lambda e: e.reciprocal(out=rr.t[64:128, 0:nq], in_=acc.t[64:128, 0:nq]), reads=[acc], writes=[rr])
                st["acc"] = acc

            def _tail2(st, b, nq, mix_rows, q_tok0):
                rr, acc = st["rr"], st["acc"]
                om = self.rot(omix, "omix")
                s.op("dve", lambda e: e.tensor_tensor(out=om.t[:, 0:nq], in0=acc.t[0:64, 0:nq], in1=rr.t[64:128, 0:nq], op=ALU.mult),
                     reads=[acc, rr], writes=[om])
                s.dma("pool", dr["MIXT"][b, mix_rows:mix_rows + 64, q_tok0:q_tok0 + nq], om.t[:, 0:nq], reads=[om], writes=[drb["MIXT"]])

            for b in range(NB):
                for h in range(4):
                    k_t, v_t = self.rot(kT, "kT"), self.rot(va, "va")
                    s.dma("sp", k_t.t[0:64, :], dr["QKNA"][b, 256 + h * 64:256 + (h + 1) * 64, :], reads=[drb["QKNA"]], writes=[k_t])
                    s.dma("sp", v_t.t[:, :, 0:64], dr["VNA"][b, :, h * 64:(h + 1) * 64].rearrange("(c p) f -> p c f", p=128),
                          reads=[drb["VNA"]], writes=[v_t])
                    for g in range(4):
                        tok0, nch, bt0 = NA_GROUPS[g]
                        bias_t = self.rot(bias, "bias")
                        s.dma("sp", bias_t.t[:, 0:nch, :], dr["nab_bf"][l, h, bt0:bt0 + nch].rearrange("c p q -> p c q"),
                              reads=[self.wb("nab_bf", l)], writes=[bias_t])
                        chunks = [(tok0 // 128 + i, i) for i in range(nch)] + [(16, None), (17, None)]
                        attend(b, 64, k_t, v_t, dr["QKNA"][b, h * 64:(h + 1) * 64, g * 512:(g + 1) * 512], 512, chunks, bias_t, h * 64, g * 512)
                    if not last:
                        attend(b, 64, k_t, v_t, dr["QKNA"][b, h * 64:(h + 1) * 64, SEQ:T], 256, [(16, None), (17, None)], None, h * 64, SEQ)
                for h in range(4):
                    k_t, v_t = self.rot(kT, "kT"), self.rot(va, "va")
                    s.dma("sp", k_t.t[0:96, :], dr["KM"][b, h], reads=[drb["KM"]], writes=[k_t])
                    s.dma("sp", v_t.t[:, :, 0:64], dr["VM"][b, :, h * 64:(h + 1) * 64].rearrange("(c p) f -> p c f", p=128),
                          reads=[drb["VM"]], writes=[v_t])
                    for g in range(4):
                        attend(b, 96, k_t, v_t, dr["QM"][b, h, :, g * 512:(g + 1) * 512], 512, [(i, None) for i in range(18)], None, 768 + h * 64, g * 512)
                    if not last:
                        attend(b, 96, k_t, v_t, dr["QM"][b, h, :, SEQ:T], 256, [(16, None), (17, None)], None, 768 + h * 64, SEQ)
            while self.att_tail:
                self.att_tail.pop(0)()

    def phase_ssd(self, l, last):
        nc, s, NB = self.nc, self.s, self.NB
        dr, drb = self.dr, self.drb
        W = 2312
        import os
        WA = int(os.environ.get("KWA", "0"))
        WB = int(os.environ.get("KWB", "0"))
        with ExitStack() as es:
            xp = [self.sb(es, "xp%d" % i, [128, W], BF16) for i in range(2)]
            xsfm = self.sb(es, "xsfm", [128, 4, T], BF16)
            BT = self.sb(es, "BT", [128, 2, T], BF16)
            CT = self.sb(es, "CT", [128, 2, T], BF16)
            xstm = self.sb(es, "xstm", [128, 18, 512], BF16)
            Btm = self.sb(es, "Btm", [128, 18, 256], BF16)
            raw = self.sb(es, "dtraw", [128, 18, 16], F32)
            dt = self.sb(es, "dt", [128, 18, 16], F32)
            a = self.sb(es, "a", [128, 18, 16], F32)
            acum = self.sb(es, "acum", [128, 18, 16], F32)
            nacum = self.sb(es, "nacum", [128, 18, 16], F32)
            atot = self.sb(es, "atot", [128, 18, 16], F32)
            eac = self.sb(es, "eac", [128, 18, 16], F32)
            dte = self.sb(es, "dte", [128, 18, 16], F32)
            etot = self.sb(es, "etot", [128, 18, 16], F32)
            aneg = self.sb(es, "aneg", [128, 16], F32)
            dsk = self.sb(es, "dsk", [128, 8, 64], F32)
            yacc = self.sb(es, "yacc", [128, 18, 512], F32)
            zts = [self.sb(es, "zt%d" % i, [128, 512], BF16) for i in range(3)]
            ymix = xsfm
            H = [self.sb(es, "H%d" % i, [128, 512], F32) for i in range(2)]
            Hbf = [self.sb(es, "Hbf%d" % i, [128, 512], BF16) for i in range(2)]
            xd = [self.sb(es, "xd%d" % i, [128, 512], BF16) for i in range(6)]
            xdd = [self.sb(es, "xdd%d" % i, [128, 512], BF16) for i in range(6)]
            abc = [self.sb(es, "abc%d" % i, [128, 8, 128], F32) for i in range(2)]
            T1 = [self.sb(es, "T1_%d" % i, [128, 8, 128], BF16) for i in range(4)]
            MT = [self.sb(es, "MT%d" % i, [128, 8, 128], BF16) for i in range(6)]
            tb = [self.sb(es, "tb%d" % i, [128, 512], F32) for i in range(4)]
            ssq = self.sb(es, "ssq", [128, 16], F32)
            obf = [self.sb(es, "sobf%d" % i, [128, 512], BF16) for i in range(3)]
            szb = [self.sb(es, "szb%d" % i, [128, 512], BF16) for i in range(3)]
            junk = self.sb(es, "junk", [128, 512], BF16)
            for i in range(2):
                s.op("dve", lambda e: e.memset(xp[i].t[:, 0:2], 0.0), writes=[xp[i]])
                s.op("dve", lambda e: e.memset(xp[i].t[:, 2050:2054], 0.0), writes=[xp[i]])
                s.op("dve", lambda e: e.memset(xp[i].t[:, 2310:2312], 0.0), writes=[xp[i]])
            s.op("act", lambda e: e.activation(out=aneg.t[:], in_=self.rowbc.t[:, 16:32], func=AF.Exp), reads=[self.rowbc], writes=[aneg])
            s.op("dve", lambda e: e.tensor_scalar(out=aneg.t[:], in0=aneg.t[:], scalar1=-1.0, scalar2=None, op0=ALU.mult), reads=[aneg], writes=[aneg])
            s.op("dve", lambda e: e.tensor_copy(out=dsk.t[:], in_=self.rowbc.t[:, 32:40].unsqueeze(2).to_broadcast([128, 8, 64])),
                 reads=[self.rowbc], writes=[dsk])
            nwbc = self.rowbc.t[:, 40:552]
            negm4 = [self.sb(es, "negm4_%d" % i, [128, 4, 128], BF16) for i in range(2)]
            for i in range(2):
                s.op("dve", lambda e: e.tensor_copy(out=negm4[i].t[:], in_=self.negmb[i].t[:].unsqueeze(1).to_broadcast([128, 4, 128])),
                     reads=[self.negmb[i]], writes=[negm4[i]])

            def bf16view(ps):
                return ps.t[:].bitcast(BF16)

            for b in range(NB):
                for c in range(8):
                    x_ = self.rot(xp, "xp")
                    s.dma("sp", x_.t[:, 2:2050], dr["XBC"][b, c * 128:(c + 1) * 128, 0:SEQ], reads=[drb["XBC"]], writes=[x_])
                    s.dma("sp", x_.t[:, 2054:2310], dr["XBC"][b, c * 128:(c + 1) * 128, SEQ:T], reads=[drb["XBC"]], writes=[x_])
                    v = yacc
                    vt = yacc.t[:, 0:5, :].rearrange("p c f -> p (c f)")[:, 0:2308]
                    s.op("dve", lambda e: e.tensor_scalar(out=vt, in0=x_.t[:, 0:2308], scalar1=self.convw.t[:, c, 0:1], scalar2=None, op0=ALU.mult),
                         reads=[x_, self.convw], writes=[v])
                    for jj in range(1, 5):
                        s.op("dve", lambda e: e.scalar_tensor_tensor(out=vt, in0=x_.t[:, jj:jj + 2308], scalar=self.convw.t[:, c, jj:jj + 1],
                                                                     in1=vt, op0=ALU.mult, op1=ALU.add),
                             reads=[x_, self.convw, v], writes=[v])
                    if c < 4:
                        dst, dl, dc = xsfm, xsfm.t[:, c, 0:SEQ], xsfm.t[:, c, SEQ:T]
                    elif c < 6:
                        dst, dl, dc = BT, BT.t[:, c - 4, 0:SEQ], BT.t[:, c - 4, SEQ:T]
                    else:
                        dst, dl, dc = CT, CT.t[:, c - 6, 0:SEQ], CT.t[:, c - 6, SEQ:T]
                    s.op("act", lambda e: e.activation(out=dl, in_=vt[:, 0:SEQ], func=AF.Silu, bias=self.vec.t[:, 8 + c:9 + c]),
                         reads=[v, self.vec], writes=[dst])
                    s.op("act", lambda e: e.activation(out=dc, in_=vt[:, 2052:2308], func=AF.Silu, bias=self.vec.t[:, 8 + c:9 + c]),
                         reads=[v, self.vec], writes=[dst])
                for tc in range(18):
                    ps = self.psum()
                    pv = bf16view(ps)
                    for ci in range(4):
                        s.op("pe", lambda e: e.transpose(pv[:, ci * 128:(ci + 1) * 128], xsfm.t[:, ci, tc * 128:(tc + 1) * 128], self.ident.t[:]),
                             reads=[xsfm, self.ident], writes=[ps])
                    for ci in range(2):
                        s.op("pe", lambda e: e.transpose(pv[:, 512 + ci * 128:512 + (ci + 1) * 128], BT.t[:, ci, tc * 128:(tc + 1) * 128], self.ident.t[:]),
                             reads=[BT, self.ident], writes=[ps])
                    s.op("act", lambda e: e.activation(out=xstm.t[:, tc, :], in_=pv[:, 0:512], func=AF.Copy), reads=[ps], writes=[xstm])
                    s.op("dve", lambda e: e.tensor_copy(out=Btm.t[:, tc, :], in_=pv[:, 512:768]), reads=[ps], writes=[Btm])
                s.dma("sp", raw.t[:], dr["DTS"][b].rearrange("(c p) f -> p c f", p=128), reads=[drb["DTS"]], writes=[raw])
                s.op("dve", lambda e: e.tensor_tensor(out=dt.t[:], in0=raw.t[:], in1=self.rowbc.t[:, 0:16].unsqueeze(1).to_broadcast([128, 18, 16]), op=ALU.add),
                     reads=[raw, self.rowbc], writes=[dt])
                s.op("dve", lambda e: e.tensor_scalar(out=dt.t[:], in0=dt.t[:], scalar1=30.0, scalar2=None, op0=ALU.min), reads=[dt], writes=[dt])
                s.op("act", lambda e: e.activation(out=dt.t[:], in_=dt.t[:], func=AF.Exp), reads=[dt], writes=[dt])
                s.op("act", lambda e: e.activation(out=dt.t[:], in_=dt.t[:], func=AF.Ln, bias=self.cst.t[:, 3:4]), reads=[dt, self.cst], writes=[dt])
                s.op("dve", lambda e: e.tensor_tensor(out=a.t[:], in0=dt.t[:], in1=aneg.t[:].unsqueeze(1).to_broadcast([128, 18, 16]), op=ALU.mult),
                     reads=[dt, aneg], writes=[a])
                for d in range(2):
                    ps = self.psum()
                    s.op("pe", lambda e: e.matmul(ps.t[:, 0:144], self.tri[d].t[:], a.t[:, :, d * 8:(d + 1) * 8], start=True, stop=True),
                         reads=[self.tri[d], a], writes=[ps])
                    s.op("dve", lambda e: e.tensor_copy(out=acum.t[:, :, d * 8:(d + 1) * 8], in_=ps.t[:, 0:144].rearrange("p (c h) -> p c h", h=8)),
                         reads=[ps], writes=[acum])
                    ps = self.psum()
                    s.op("pe", lambda e: e.matmul(ps.t[:, 0:144], self.onesf.t[:], a.t[:, :, d * 8:(d + 1) * 8], start=True, stop=True),
                         reads=[self.onesf, a], writes=[ps])
                    s.op("dve", lambda e: e.tensor_copy(out=atot.t[:, :, d * 8:(d + 1) * 8], in_=ps.t[:, 0:144].rearrange("p (c h) -> p c h", h=8)),
                         reads=[ps], writes=[atot])
                s.op("dve", lambda e: e.tensor_scalar(out=nacum.t[:], in0=acum.t[:], scalar1=-1.0, scalar2=None, op0=ALU.mult), reads=[acum], writes=[nacum])
                s.op("act", lambda e: e.activation(out=eac.t[:], in_=acum.t[:], func=AF.Exp), reads=[acum], writes=[eac])
                s.op("act", lambda e: e.activation(out=etot.t[:], in_=atot.t[:], func=AF.Exp), reads=[atot], writes=[etot])
                s.op("dve", lambda e: e.tensor_tensor(out=dte.t[:], in0=atot.t[:], in1=acum.t[:], op=ALU.subtract), reads=[atot, acum], writes=[dte])
                s.op("act", lambda e: e.activation(out=dte.t[:], in_=dte.t[:], func=AF.Exp), reads=[dte], writes=[dte])
                orders = [[16, 17] + list(range(16)), [17, 16] + list(range(15, -1, -1))]
                touched = set()
                for d in range(2):
                    s.op("dve", lambda e: e.memset(H[d].t[:], 0.0), writes=[H[d]])
                    s.op("dve", lambda e: e.memset(Hbf[d].t[:], 0.0), writes=[Hbf[d]])

                def stage_a(oi, d):
                    c = orders[d][oi]
                    hs = slice(d * 8, (d + 1) * 8)
                    cs_ = slice(c * 128, (c + 1) * 128)
                    x1, x2 = self.rot(xd, "xd"), self.rot(xdd, "xdd")
                    s.op("pool", lambda e: e.tensor_tensor(out=x1.t[:].rearrange("p (h q) -> p h q", q=64), in0=xstm.t[:, c, :].rearrange("p (h q) -> p h q", q=64),
                                                           in1=dt.t[:, c, hs].unsqueeze(2).to_broadcast([128, 8, 64]), op=ALU.mult),
                         reads=[xstm, dt], writes=[x1])
                    s.op("pool", lambda e: e.tensor_tensor(out=x2.t[:].rearrange("p (h q) -> p h q", q=64), in0=x1.t[:].rearrange("p (h q) -> p h q", q=64),
                                                           in1=dte.t[:, c, hs].unsqueeze(2).to_broadcast([128, 8, 64]), op=ALU.mult),
                         reads=[x1, dte], writes=[x2])
                    ctx_ = dict(oi=oi, d=d, c=c, x1=x1, x2=x2, mt=None)
                    if last and c >= 16:
                        return ctx_
                    pcb = self.psum()
                    for g in range(2):
                        s.op("pe", lambda e: e.matmul(pcb.t[:, g * 128:(g + 1) * 128], BT.t[:, g, cs_], CT.t[:, g, cs_], start=True, stop=True),
                             reads=[BT, CT], writes=[pcb])
                    ab = self.rot(abc, "abc")
                    s.op("dve", lambda e: e.tensor_tensor(out=ab.t[:], in0=self.tri[d].t[:].unsqueeze(1).to_broadcast([128, 8, 128]),
                                                          in1=a.t[:, c, hs].unsqueeze(2).to_broadcast([128, 8, 128]), op=ALU.mult),
                         reads=[a, self.tri[d]], writes=[ab])
                    pd = [self.psum(), self.psum()]
                    t1 = self.rot(T1, "T1")
                    for half in range(2):
                        reg = pd[half].t[:, :]
                        s.op("pe", lambda e: e.matmul(reg, self.ident.t[:], negm4[d].t[:].rearrange("p r l -> p (r l)"), start=True, stop=False),
                             reads=[self.ident, negm4[d]], writes=[pd[half]])
                        s.op("pe", lambda e: e.matmul(reg, self.onesf.t[:], ab.t[:, half * 4:(half + 1) * 4, :].rearrange("p h l -> p (h l)"), start=False, stop=True),
                             reads=[self.onesf, ab], writes=[pd[half]])
                    self.warm(WA)
                    for hh in range(8):
                        reg = pd[hh // 4].t[:, (hh % 4) * 128:(hh % 4 + 1) * 128]
                        s.op("act", lambda e: e.activation(out=t1.t[:, hh, :], in_=reg, func=AF.Exp, bias=nacum.t[:, c, d * 8 + hh:d * 8 + hh + 1]),
                             reads=[pd[hh // 4], nacum], writes=[t1])
                    mt = self.rot(MT, "MT")
                    for g in range(2):
                        s.op("dve", lambda e: e.tensor_tensor(out=mt.t[:, g * 4:(g + 1) * 4, :], in0=t1.t[:, g * 4:(g + 1) * 4, :],
                                                              in1=pcb.t[:, g * 128:(g + 1) * 128].unsqueeze(1).to_broadcast([128, 4, 128]), op=ALU.mult),
                             reads=[t1, pcb], writes=[mt])
                    ctx_["mt"] = mt
                    return ctx_

                def stage_b(cx):
                    oi, d, c, x1, x2, mt = cx["oi"], cx["d"], cx["c"], cx["x1"], cx["x2"], cx["mt"]
                    hs = slice(d * 8, (d + 1) * 8)
                    cs_ = slice(c * 128, (c + 1) * 128)
                    Hd, Hb = H[d], Hbf[d]
                    if mt is not None:
                        py = self.psum()
                        for hh in range(8):
                            s.op("pe", lambda e: e.matmul(py.t[:, hh * 64:(hh + 1) * 64], mt.t[:, hh, :], x1.t[:, hh * 64:(hh + 1) * 64], start=True, stop=True),
                                 reads=[mt, x1], writes=[py])
                        self.warm(WB)
                        pyo = self.psum()
                        for g in range(2):
                            s.op("pe", lambda e: e.matmul(pyo.t[:, g * 256:(g + 1) * 256], CT.t[:, g, cs_], Hb.t[:, g * 256:(g + 1) * 256], start=True, stop=True),
                                 reads=[CT, Hb], writes=[pyo])
                        t_ = self.rot(tb, "tb")
                        s.op("dve", lambda e: e.tensor_tensor(out=t_.t[:].rearrange("p (h q) -> p h q", q=64), in0=pyo.t[:].rearrange("p (h q) -> p h q", q=64),
                                                              in1=eac.t[:, c, hs].unsqueeze(2).to_broadcast([128, 8, 64]), op=ALU.mult),
                             reads=[pyo, eac], writes=[t_])
                        if c not in touched:
                            touched.add(c)
                            s.op("dve", lambda e: e.tensor_tensor(out=yacc.t[:, c, :], in0=t_.t[:], in1=py.t[:], op=ALU.add), reads=[t_, py], writes=[yacc])
                        else:
                            s.op("dve", lambda e: e.tensor_tensor(out=t_.t[:], in0=t_.t[:], in1=py.t[:], op=ALU.add), reads=[t_, py], writes=[t_])
                            s.op("pool", lambda e: e.tensor_tensor(out=yacc.t[:, c, :], in0=yacc.t[:, c, :], in1=t_.t[:], op=ALU.add), reads=[t_, yacc], writes=[yacc])
                    if oi < 17:
                        pcs = self.psum()
                        for g in range(2):
                            s.op("pe", lambda e: e.matmul(pcs.t[:, g * 256:(g + 1) * 256], Btm.t[:, c, g * 128:(g + 1) * 128], x2.t[:, g * 256:(g + 1) * 256], start=True, stop=True),
                                 reads=[Btm, x2], writes=[pcs])
                        s.op("pool", lambda e: e.tensor_tensor(out=Hd.t[:].rearrange("p (h q) -> p h q", q=64), in0=Hd.t[:].rearrange("p (h q) -> p h q", q=64),
                                                               in1=etot.t[:, c, hs].unsqueeze(2).to_broadcast([128, 8, 64]), op=ALU.mult),
                             reads=[Hd, etot], writes=[Hd])
                        s.op("dve", lambda e: e.tensor_tensor(out=Hd.t[:], in0=Hd.t[:], in1=pcs.t[:], op=ALU.add), reads=[Hd, pcs], writes=[Hd])
                        s.op("act", lambda e: e.activation(out=Hb.t[:], in_=Hd.t[:], func=AF.Copy), reads=[Hd], writes=[Hb])

                AHEAD = 2
                inflight = {}
                for oi in range(18 + AHEAD):
                    if oi < 18:
                        inflight[oi] = [stage_a(oi, d) for d in range(2)]
                    if oi >= AHEAD:
                        for cx in inflight.pop(oi - AHEAD):
                            stage_b(cx)
                nch = 16 if last else 18
                G = 3
                for c0 in range(0, nch, G):
                    grp = list(range(c0, min(nch, c0 + G)))
                    tt_, szs, zz = {}, {}, {}
                    for c in grp:
                        t_ = tt_[c] = self.rot(tb, "tb")
                        s.op("pool", lambda e: e.tensor_tensor(out=t_.t[:], in0=xstm.t[:, c, :], in1=dsk.t[:].rearrange("p h q -> p (h q)"), op=ALU.mult),
                             reads=[xstm, dsk], writes=[t_])
                        zt = zz[c] = self.rot(zts, "zt")
                        s.dma("sp", zt.t[:], dr["ZS"][b, c * 128:(c + 1) * 128, :], reads=[drb["ZS"]], writes=[zt])
                    for c in grp:
                        t_ = tt_[c]
                        s.op("pool", lambda e: e.tensor_tensor(out=t_.t[:], in0=t_.t[:], in1=yacc.t[:, c, :], op=ALU.add), reads=[t_, yacc], writes=[t_])
                        sz = szs[c] = self.rot(szb, "szb")
                        s.op("act", lambda e: e.activation(out=sz.t[:], in_=zz[c].t[:], func=AF.Silu), reads=[zz[c]], writes=[sz])
                    for c in grp:
                        t_, sz = tt_[c], szs[c]
                        s.op("dve", lambda e: e.tensor_tensor(out=t_.t[:], in0=t_.t[:], in1=sz.t[:], op=ALU.mult), reads=[t_, sz], writes=[t_])
                    for gi, c in enumerate(grp):
                        s.op("act", lambda e: e.activation(out=junk.t[:], in_=tt_[c].t[:], func=AF.Square, accum_out=ssq.t[:, gi:gi + 1]),
                             reads=[tt_[c]], writes=[junk, ssq])
                    ng = len(grp)
                    s.op("act", lambda e: e.activation(out=ssq.t[:, 4:4 + ng], in_=ssq.t[:, 0:ng], func=AF.Ln, scale=1.0 / 512, bias=self.cst.t[:, 0:1]),
                         reads=[ssq, self.cst], writes=[ssq])
                    s.op("act", lambda e: e.activation(out=ssq.t[:, 8:8 + ng], in_=ssq.t[:, 4:4 + ng], func=AF.Exp, scale=-0.5), reads=[ssq], writes=[ssq])
                    oo = {}
                    for gi, c in enumerate(grp):
                        o = oo[c] = self.rot(obf, "sobf")
                        s.op("dve", lambda e: e.scalar_tensor_tensor(out=o.t[:], in0=tt_[c].t[:], scalar=ssq.t[:, 8 + gi:9 + gi], in1=nwbc, op0=ALU.mult, op1=ALU.mult),
                             reads=[tt_[c], ssq, self.rowbc], writes=[o])
                    pss = {}
                    for c in grp:
                        o = oo[c]
                        ps = pss[c] = self.psum()
                        pv = bf16view(ps)
                        for ci in range(4):
                            s.op("pe", lambda e: e.transpose(pv[:, ci * 128:(ci + 1) * 128], o.t[:, ci * 128:(ci + 1) * 128], self.ident.t[:]),
                                 reads=[o, self.ident], writes=[ps])
                    for gi, c in enumerate(grp):
                        ps = pss[c]
                        pv = bf16view(ps)
                        eng = "act" if gi % 2 == 0 else "dve"
                        if eng == "act":
                            s.op("act", lambda e: e.activation(out=ymix.t[:, :, c * 128:(c + 1) * 128], in_=pv[:, 0:512].rearrange("p (k t) -> p k t", t=128), func=AF.Copy),
                                 reads=[ps], writes=[ymix])
                        else:
                            s.op("dve", lambda e: e.tensor_copy(out=ymix.t[:, :, c * 128:(c + 1) * 128], in_=pv[:, 0:512].rearrange("p (k t) -> p k t", t=128)),
                                 reads=[ps], writes=[ymix])
                ntok = SEQ if last else T
                s.dma("pool", dr["MIXT"][b, 256:768, 0:ntok].rearrange("(k p) t -> p k t", p=128), ymix.t[:, :, 0:ntok], reads=[ymix], writes=[drb["MIXT"]])

    def phase_FG(self, l, last):
        nc, s, NB, J = self.nc, self.s, self.NB, self.J
        dr, drb = self.dr, self.drb
        with ExitStack() as es:
            wout = self.sb(es, "wout", [128, 8, 8, 128], BF16)
            s.dma("sp", wout.t[:], dr["wout_bf"][l].rearrange("c p k f -> p c k f"), reads=[self.wb("wout_bf", l)], writes=[wout])
            xts = [self.sb(es, "fxt%d" % i, [128, 8, 512], F32) for i in range(2)]
            mixs = [self.sb(es, "fmix%d" % i, [128, 8, 512], BF16) for i in range(2)]
            sq = self.sb(es, "fsq", [128, 8, 512], BF16)
            xm = self.sb(es, "fxm", [128, 8, 512], BF16)
            rstd = self.sb(es, "frstd", [128, 512], F32)
            tmp = [self.sb(es, "ftmp%d" % i, [128, 512], F32) for i in range(3)]
            hT = self.sb(es, "hT", [128, 32, 512], BF16)
            rl = [self.sb(es, "rl%d" % i, [128, 512], BF16) for i in range(3)]
            w1 = [self.sb(es, "w1_%d" % i, [128, 4, 8, 128], BF16) for i in range(2)]
            w2 = [self.sb(es, "w2_%d" % i, [128, 32, 128], BF16) for i in range(2)]
            xsrc = "xT" if l == 0 else "XS"
            tiles = TILES[:4] if last else TILES
            for b in range(NB):
                for ti, (t0, n) in enumerate(tiles):
                    j = NB if ti == 4 else b
                    xt = self.rot(xts, "fxt")
                    s.dma("sp", xt.t[:, :, 0:n], dr[xsrc][b].rearrange("(k p) t -> p k t", p=128)[:, :, t0:t0 + n], reads=[drb[xsrc]], writes=[xt])
                    mx = self.rot(mixs, "fmix")
                    s.dma("sp", mx.t[:, :, 0:n], dr["MIXT"][b].rearrange("(k p) t -> p k t", p=128)[:, :, t0:t0 + n], reads=[drb["MIXT"]], writes=[mx])
                    for fc in range(8):
                        ps = self.psum()
                        for k in range(8):
                            s.op("pe", lambda e: e.matmul(ps.t[:, 0:n], wout.t[:, fc, k, :], mx.t[:, k, 0:n], start=(k == 0), stop=(k == 7)),
                                 reads=[wout, mx], writes=[ps])
                        s.op("dve", lambda e: e.scalar_tensor_tensor(out=xt.t[:, fc, 0:n], in0=ps.t[:, 0:n], scalar=self.mod.t[:, 16 + fc, j:j + 1],
                                                                     in1=xt.t[:, fc, 0:n], op0=ALU.mult, op1=ALU.add),
                             reads=[ps, self.mod, xt], writes=[xt])
                    self.norm_mod(xt, n, self.A2, 24, j, xm, sq, rstd, tmp)
                    for g in range(8):
                        w = self.rot(w1, "w1")
                        s.dma("sp", w.t[:], dr["wff1_bf"][l, g * 4:(g + 1) * 4].rearrange("c p k f -> p c k f"), reads=[self.wb("wff1_bf", l)], writes=[w])
                        for c in range(4):
                            fc = g * 4 + c
                            ps = self.psum()
                            for k in range(8):
                                s.op("pe", lambda e: e.matmul(ps.t[:, 0:n], w.t[:, c, k, :], xm.t[:, k, 0:n], start=(k == 0), stop=(k == 7)),
                                     reads=[w, xm], writes=[ps])
                            r = self.rot(rl, "rl")
                            s.op("act", lambda e: e.activation(out=r.t[:, 0:n], in_=ps.t[:, 0:n], func=AF.Relu), reads=[ps], writes=[r])
                            s.op("pool", lambda e: e.tensor_tensor(out=hT.t[:, fc, 0:n], in0=r.t[:, 0:n], in1=r.t[:, 0:n], op=ALU.mult), reads=[r], writes=[hT])
                    for fc in range(8):
                        w = self.rot(w2, "w2")
                        s.dma("sp", w.t[:], dr["wff2_bf"][l, fc], reads=[self.wb("wff2_bf", l)], writes=[w])
                        ps = self.psum()
                        for k in range(32):
                            s.op("pe", lambda e: e.matmul(ps.t[:, 0:n], w.t[:, k, :], hT.t[:, k, 0:n], start=(k == 0), stop=(k == 31)),
                                 reads=[w, hT], writes=[ps])
                        s.op("dve", lambda e: e.scalar_tensor_tensor(out=xt.t[:, fc, 0:n], in0=ps.t[:, 0:n], scalar=self.mod.t[:, 40 + fc, j:j + 1],
                                                                     in1=xt.t[:, fc, 0:n], op0=ALU.mult, op1=ALU.add),
                             reads=[ps, self.mod, xt], writes=[xt])
                    if last:
                        s.dma("pool", dr["out"][b].rearrange("(k p) t -> p k t", p=128)[:, :, t0:t0 + n], xt.t[:, :, 0:n], reads=[xt], writes=[drb["out"]])
                    else:
                        s.dma("pool", dr["XS"][b].rearrange("(k p) t -> p k t", p=128)[:, :, t0:t0 + n], xt.t[:, :, 0:n], reads=[xt], writes=[drb["XS"]])


_CACHE = {}


def _get_prog(NB, L, dbg=()):
    key = (NB, L, tuple(sorted(dbg)))
    if key not in _CACHE:
        p = Prog(NB, L, dbg)
        p.build()
        _CACHE[key] = p
    return _CACHE[key]


def run(inputs, ncores=NCORES, L=None, dbg=(), trace=False):
    x = np.asarray(inputs["x"], np.float32)
    ctx = np.asarray(inputs["ctx"], np.float32)
    c = np.asarray(inputs["c"], np.float32)
    c_ctx = np.asarray(inputs["c_ctx"], np.float32)
    B = x.shape[0]
    NB = B // ncores
    Lw = inputs["w_ada"].shape[0]
    L = Lw if L is None else L
    w = _prep_weights({k: (np.asarray(v)[:L] if np.asarray(v).ndim >= 1 and np.asarray(v).shape[0] == Lw and k not in ("x", "c", "ctx", "c_ctx") else v)
                       for k, v in inputs.items()})
    cst = _consts()
    prog = _get_prog(NB, L, dbg)
    in_maps = []
    for i in range(ncores):
        sl = slice(i * NB, (i + 1) * NB)
        xT = np.concatenate([x[sl].transpose(0, 2, 1), ctx[sl].transpose(0, 2, 1)], axis=2)
        cc = np.concatenate([c[sl], c_ctx[None]], axis=0)
        cT = np.ascontiguousarray(cc.reshape(NB + 1, 8, 128).transpose(2, 1, 0))
        m = {"xT": np.ascontiguousarray(xT), "cT": cT}
        m.update(w)
        m.update(cst)
        in_maps.append(m)
    res = run_bass_kernel_spmd(prog.nc, in_maps, core_ids=list(range(ncores)), **({"trace": True} if trace else {}))
    out = np.concatenate([r["out"].transpose(0, 2, 1) for r in res.results], axis=0)
    return np.ascontiguousarray(out), res


def kernel(**inputs):
    out, _ = run(inputs)
    return out.astype(np.float32)
```

```python
import numpy as np
from contextlib import ExitStack
import concourse.bass as bass
import concourse.mybir as mybir
from concourse.bass_utils import run_bass_kernel_spmd

F32 = mybir.dt.float32
BF16 = mybir.dt.bfloat16
AF = mybir.ActivationFunctionType
ALU = mybir.AluOpType

D = 1024
SEQ = 2048
CTX = 256
T = SEQ + CTX
DFF = 4096
EPS = 1e-6
NEG = -30000.0
NCORES = 8


class Buf:
    __slots__ = ("w", "r")

    def __init__(self):
        self.w = None
        self.r = []


class TT:
    __slots__ = ("t", "b", "ps")

    def __init__(self, t, ps=False):
        self.t = t
        self.b = Buf()
        self.ps = ps


class Sched:
    ND = 40

    def __init__(self, nc, es):
        self.nc = nc
        self.E = dict(pe=nc.tensor, dve=nc.vector, act=nc.scalar, pool=nc.gpsimd, sp=nc.sync)
        self.csem = {e: es.enter_context(nc.semaphore("c_" + e)) for e in ("pe", "dve", "act", "pool")}
        self.ccnt = {e: 0 for e in self.csem}
        self.dsems = [es.enter_context(nc.semaphore("d%d" % i)) for i in range(self.ND)]
        self.dcnt = [0] * self.ND
        self.dnext = 0
        self.dnext_pool = 0
        self.waited = {e: {} for e in self.E}
        self.n_inst = 0

    def _wait(self, eng, ev):
        key, sem, val = ev[1], ev[2], ev[3]
        if self.waited[eng].get(key, 0) >= val:
            return
        self.E[eng].wait_ge(sem, val)
        self.waited[eng][key] = val

    def _deps(self, eng, reads, writes):
        for b in reads:
            if b.w is not None and not (b.w[0] == eng == "pe"):
                self._wait(eng, b.w)
        for b in writes:
            if b.w is not None and not (b.w[0] == eng == "pe"):
                self._wait(eng, b.w)
            for ev in b.r:
                if not (ev[0] == eng == "pe"):
                    self._wait(eng, ev)

    def _update(self, ev, reads, writes):
        for b in reads:
            b.r = [e for e in b.r if e[1] != ev[1]] + [ev]
        for b in writes:
            b.w = ev
            b.r = []

    def op(self, eng, fn, reads=(), writes=()):
        writes = list(writes) + [x for x in reads if isinstance(x, TT) and x.ps]
        reads = [x.b if isinstance(x, TT) else x for x in reads if not (isinstance(x, TT) and x.ps)]
        writes = [x.b if isinstance(x, TT) else x for x in writes]
        self._deps(eng, reads, writes)
        inst = fn(self.E[eng])
        self.ccnt[eng] += 1
        inst.then_inc(self.csem[eng], 1)
        ev = (eng, "c_" + eng, self.csem[eng], self.ccnt[eng])
        self._update(ev, reads, writes)
        self.n_inst += 1

    def dma(self, q, out, in_, reads=(), writes=()):
        reads = [x.b if isinstance(x, TT) else x for x in reads]
        writes = [x.b if isinstance(x, TT) else x for x in writes]
        half = self.ND // 2
        if q == "pool":
            i = half + self.dnext_pool
            self.dnext_pool = (self.dnext_pool + 1) % half
        else:
            i = self.dnext
            self.dnext = (self.dnext + 1) % half
        key = "d%d" % i
        if self.dcnt[i] > 0:
            self._wait(q, ("dma", key, self.dsems[i], self.dcnt[i]))
        self._deps(q, reads, writes)
        inst = self.E[q].dma_start(out=out, in_=in_)
        self.dcnt[i] += 16
        inst.then_inc(self.dsems[i], 16)
        ev = ("dma", key, self.dsems[i], self.dcnt[i])
        self._update(ev, reads, writes)
        self.n_inst += 1

    def barrier(self):
        evs = [(e, "c_" + e, self.csem[e], self.ccnt[e]) for e in self.csem if self.ccnt[e] > 0]
        evs += [("dma", "d%d" % i, self.dsems[i], self.dcnt[i]) for i in range(self.ND) if self.dcnt[i] > 0]
        for eng in self.E:
            for ev in evs:
                if ev[0] != eng:
                    self._wait(eng, ev)


def _lhsT_layout(W, cols_list, mc=128):
    K = W.shape[0]
    nk = K // 128
    out = np.zeros((len(cols_list), 128, nk, mc), np.float32)
    Wr = W.reshape(nk, 128, W.shape[1])
    for i, cols in enumerate(cols_list):
        cols = np.asarray(cols)
        out[i, :, :, :len(cols)] = Wr[:, :, cols].transpose(1, 0, 2)
    return out


def _na_bias_tables(rpb):
    H = 4
    kc = np.arange(64)
    qc = np.arange(64)
    col_start = np.clip(qc - 8, 0, 48)
    col_in = (kc[:, None] >= col_start[None, :]) & (kc[:, None] < col_start[None, :] + 16)
    col_idx = np.clip(kc[:, None] - qc[None, :], -15, 15) + 15
    tiles = []

    def tile_for(qr0, kr0):
        tl = np.full((H, 2, 64, 8, 64), NEG, np.float32)
        for p in range(2):
            kr = kr0 + p
            for j in range(8):
                qr = qr0 + j
                rs = min(max(qr - 4, 0), 24)
                if rs <= kr < rs + 8:
                    ridx = kr - qr + 7
                    blk = rpb[:, ridx][:, col_idx]
                    blk = np.where(col_in[None], blk, np.float32(NEG))
                    tl[:, p, :, j, :] = blk
        return tl.reshape(H, 128, 512)

    for c in range(6):
        tiles.append(tile_for(0, 2 * c))
    for c in range(8):
        tiles.append(tile_for(8, 4 + 2 * c))
    for c in range(6):
        tiles.append(tile_for(24, 20 + 2 * c))
    return np.stack(tiles, axis=1)


NA_GROUPS = [
    (0, 6, 0), (256, 8, 6), (768, 8, 6), (1280, 6, 14)]


def _consts():
    c = {}
    c["ident"] = np.eye(128, dtype=np.float32)
    c["ones"] = np.ones((128, 128), np.float32)
    b = np.zeros((128, 128), np.float32)
    b[:64, :64] = 1
    b[64:, 64:] = 1
    c["blk64"] = b
    pos = np.arange(SEQ)
    axes = np.stack([pos // 64, pos % 64], -1).astype(np.float32)
    inv = (10000.0 ** (-np.arange(8, dtype=np.float32) / 8)).astype(np.float32)
    ang = axes[:, :, None] * inv
    cos = np.cos(ang).astype(np.float32)
    sin = np.sin(ang).astype(np.float32)
    cosT = np.ones((128, T), np.float32)
    sinT = np.zeros((128, T), np.float32)
    for ax in range(2):
        base = 64 + ax * 16
        cosT[base:base + 8, :SEQ] = cos[:, ax].T
        cosT[base + 8:base + 16, :SEQ] = cos[:, ax].T
        sinT[base:base + 8, :SEQ] = sin[:, ax].T
        sinT[base + 8:base + 16, :SEQ] = sin[:, ax].T
    c["cosT"] = cosT
    c["sinT"] = sinT
    P = np.zeros((128, 128), np.float32)
    for ax in range(2):
        base = 64 + ax * 16
        for i in range(8):
            P[base + i, base + 8 + i] = -1.0
            P[base + 8 + i, base + i] = 1.0
    c["prot"] = np.ascontiguousarray(P.T)
    k = np.arange(128)
    c["tri_f"] = (k[:, None] <= k[None, :]).astype(np.float32)
    c["tri_b"] = (k[:, None] >= k[None, :]).astype(np.float32)
    c["negm_f"] = np.where(k[:, None] <= k[None, :], 0.0, NEG).astype(np.float32)
    c["negm_b"] = np.where(k[:, None] >= k[None, :], 0.0, NEG).astype(np.float32)
    sel = np.zeros((128, 64), np.float32)
    sel[64, :] = 1.0
    c["sel64"] = sel
    return c


def _prep_weights(inp):
    L = inp["w_ada"].shape[0]
    f32 = np.float32
    w = {}
    wa = np.asarray(inp["w_ada"], f32)
    w["wada"] = np.ascontiguousarray(wa.reshape(L, 8, 128, 48, 128).transpose(0, 3, 2, 1, 4))
    w["bada"] = np.ascontiguousarray(np.asarray(inp["b_ada"], f32).reshape(L, 48, 128).transpose(0, 2, 1))
    w["nw1"] = np.ascontiguousarray(np.asarray(inp["norm1_w"], f32).reshape(L, 8, 128).transpose(0, 2, 1))
    w["nw2"] = np.ascontiguousarray(np.asarray(inp["norm2_w"], f32).reshape(L, 8, 128).transpose(0, 2, 1))
    win = np.asarray(inp["w_in"], f32)
    fm_cols = [np.arange(0, 128), np.arange(128, 256), np.arange(256, 384), np.arange(384, 512)]
    fm_cols += [np.arange(1280 + 128 * i, 1280 + 128 * (i + 1)) for i in range(8)]
    fm_cols += [np.arange(2320, 2448), np.arange(2448, 2576), np.arange(2576, 2704)]
    winfm = np.zeros((L, 16, 128, 8, 128), f32)
    for l in range(L):
        winfm[l, :15] = _lhsT_layout(win[l], fm_cols)
        kr = win[l][:, 2704:2736].reshape(8, 128, 32).transpose(1, 0, 2)
        winfm[l, 15, :, :, 64:96] = kr
    w["winfm"] = winfm
    w["winz"] = np.ascontiguousarray(win[:, :, 768:1280].reshape(L, 8, 128, 512).transpose(0, 2, 1, 3))
    vdt = np.zeros((L, 128, 8, 272), f32)
    vdt[..., :256] = win[:, :, 512:768].reshape(L, 8, 128, 256).transpose(0, 2, 1, 3)
    vdt[..., 256:272] = win[:, :, 2304:2320].reshape(L, 8, 128, 16).transpose(0, 2, 1, 3)
    w["winvdt"] = vdt
    wuq = np.asarray(inp["mla_w_uq"], f32)
    w["wuq"] = np.ascontiguousarray(wuq.reshape(L, 2, 128, 4, 96).transpose(0, 2, 3, 1, 4))
    wukv = np.asarray(inp["mla_w_ukv"], f32).reshape(L, 128, 4, 128)
    wk = np.zeros((L, 128, 4, 96), f32)
    wk[..., :64] = wukv[..., :64]
    w["wukvk"] = wk
    w["wukvv"] = np.ascontiguousarray(wukv[..., 64:].reshape(L, 128, 256))
    wo = np.asarray(inp["w_out"], f32)
    w["wout"] = np.ascontiguousarray(wo.reshape(L, 8, 128, 8, 128).transpose(0, 3, 2, 1, 4))
    w1 = np.asarray(inp["w_ff1"], f32)
    w["wff1"] = np.ascontiguousarray(w1.reshape(L, 8, 128, 32, 128).transpose(0, 3, 2, 1, 4))
    w2 = np.asarray(inp["w_ff2"], f32)
    w["wff2"] = np.ascontiguousarray(w2.reshape(L, 32, 128, 8, 128).transpose(0, 3, 2, 1, 4))
    vec = np.zeros((L, 128, 32), f32)
    vec[:, :, 0] = np.tile(np.asarray(inp["na_qn_w"], f32), (1, 2))
    vec[:, :, 1] = np.tile(np.asarray(inp["na_kn_w"], f32), (1, 2))
    vec[:, :, 2:4] = np.asarray(inp["mla_cq_norm_w"], f32).reshape(L, 2, 128).transpose(0, 2, 1)
    vec[:, :, 4] = np.asarray(inp["mla_ckv_norm_w"], f32)
    vec[:, :96, 5] = np.asarray(inp["mla_qn_w"], f32)
    vec[:, :96, 6] = np.asarray(inp["mla_kn_w"], f32)
    vec[:, :, 8:16] = np.asarray(inp["ssd_conv_b"], f32).reshape(L, 8, 128).transpose(0, 2, 1)
    w["vec"] = vec
    w["convw"] = np.ascontiguousarray(np.asarray(inp["ssd_conv_w"], f32).reshape(L, 5, 8, 128).transpose(0, 3, 2, 1))
    row = np.zeros((L, 552), f32)
    row[:, 0:16] = np.asarray(inp["ssd_dt_bias"], f32).reshape(L, 16)
    row[:, 16:32] = np.asarray(inp["ssd_a_log"], f32).reshape(L, 16)
    row[:, 32:40] = np.asarray(inp["ssd_d"], f32)
    row[:, 40:552] = np.asarray(inp["ssd_norm_w"], f32)
    w["row"] = row
    rpb = np.asarray(inp["na_rpb"], f32)
    w["nab"] = np.stack([_na_bias_tables(rpb[l]) for l in range(L)], 0)
    return w


BF_W = ["winfm", "winz", "winvdt", "wuq", "wukvk", "wukvv", "wout", "wff1", "wff2", "nab"]


TILES = [(0, 512), (512, 512), (1024, 512), (1536, 512), (2048, 256)]


class Prog:
    def __init__(self, NB, L, dbg=()):
        self.NB, self.L, self.J = NB, L, NB + 1
        self.dbg = set(dbg)
        self.nc = bass.Bass("TRN2", target_bir_lowering=False)
        self.es = ExitStack()
        self.s = Sched(self.nc, self.es)
        self.dr = {}
        self.drb = {}

    def din(self, name, shape, dt=F32):
        self.dr[name] = self.nc.dram_tensor(name, list(shape), dt, kind="ExternalInput").ap()
        self.drb[name] = Buf()

    def dscr(self, name, shape, dt):
        kind = "ExternalOutput" if name in self.dbg else "Internal"
        self.dr[name] = self.nc.dram_tensor(name, list(shape), dt, kind=kind).ap()
        self.drb[name] = Buf()

    def sb(self, es, name, shape, dt):
        self.uid = getattr(self, "uid", 0) + 1
        return TT(es.enter_context(self.nc.sbuf_tensor("sb%d_%s" % (self.uid, name), list(shape), dt)))

    def wb(self, name, l):
        return self.drb.setdefault((name, l), Buf())

    def psum(self):
        p = self.ps[self.psi % 7]
        self.psi += 1
        return p

    def warm(self, n=1, cols=512):
        for _ in range(n):
            self.s.op("pe", lambda e: e.matmul(self.ps[7].t[:, 0:cols], self.ident.t[:], self.wrhs.t[:, 0:cols], start=True, stop=True))

    def rot(self, lst, key):
        i = self.rr.get(key, 0)
        self.rr[key] = i + 1
        return lst[i % len(lst)]

    def build(self):
        nc, s, es, NB, L, J = self.nc, self.s, self.es, self.NB, self.L, self.J
        self.rr = {}
        self.din("xT", [NB, D, T])
        self.din("cT", [128, 8, J])
        wshapes = dict(wada=[48, 128, 8, 128], bada=[128, 48], nw1=[128, 8], nw2=[128, 8],
                       winfm=[16, 128, 8, 128], winz=[128, 8, 512], winvdt=[128, 8, 272],
                       wuq=[128, 4, 2, 96], wukvk=[128, 4, 96], wukvv=[128, 256],
                       wout=[8, 128, 8, 128], wff1=[32, 128, 8, 128], wff2=[8, 128, 32, 128],
                       vec=[128, 32], convw=[128, 8, 5], row=[552], nab=[4, 20, 128, 512])
        self.wshapes = wshapes
        for k, shp in wshapes.items():
            self.din(k, [L] + shp)
        cshapes = dict(ident=[128, 128], ones=[128, 128], blk64=[128, 128], cosT=[128, T], sinT=[128, T],
                       prot=[128, 128], tri_f=[128, 128], tri_b=[128, 128], negm_f=[128, 128], negm_b=[128, 128],
                       sel64=[128, 64])
        for k, shp in cshapes.items():
            self.din(k, shp)
        self.dr["out"] = nc.dram_tensor("out", [NB, D, SEQ], F32, kind="ExternalOutput").ap()
        self.drb["out"] = Buf()
        for k in BF_W:
            self.dscr(k + "_bf", [L] + wshapes[k], BF16)
        self.dscr("XS", [NB, D, T], F32)
        self.dscr("QKNA", [NB, 512, T], BF16)
        self.dscr("VNA", [NB, T, 256], BF16)
        self.dscr("ZS", [NB, T, 512], BF16)
        self.dscr("XBC", [NB, D, T], BF16)
        self.dscr("DTS", [NB, T, 16], F32)
        self.dscr("QM", [NB, 4, 96, T], BF16)
        self.dscr("KM", [NB, 4, 96, T], BF16)
        self.dscr("VM", [NB, T, 256], BF16)
        self.dscr("MIXT", [NB, D, T], BF16)
        dr, drb = self.dr, self.drb

        pairs = [es.enter_context(nc.psum_tensor("pp%d" % i, [128, 1024], F32)) for i in range(4)]
        self.ps = [TT(pairs[i // 2][:, (i % 2) * 512:(i % 2 + 1) * 512], ps=True) for i in range(8)]
        self.pp = [TT(pairs[i][:, :], ps=True) for i in range(4)]
        self.psi = 0

        def convert(l):
            for k in BF_W:
                shp = wshapes[k]
                src, dst = dr[k][l], dr[k + "_bf"][l]
                if len(shp) == 4 and (shp[1] == 128 or k == "nab"):
                    step = max(1, (1 << 20) // (shp[1] * shp[2] * shp[3]))
                    if k == "nab":
                        for h in range(4):
                            for i in range(0, shp[1], 5):
                                s.dma("pool", dst[h, i:i + 5], src[h, i:i + 5], writes=[self.wb(k + "_bf", l)])
                    else:
                        for i in range(0, shp[0], step):
                            s.dma("pool", dst[i:i + step], src[i:i + step], writes=[self.wb(k + "_bf", l)])
                else:
                    s.dma("pool", dst, src, writes=[self.wb(k + "_bf", l)])
        self.convert = convert
        convert(0)

        def cload(name, dt, rows=128):
            t = self.sb(es, "c_" + name, cshapes[name], dt)
            s.dma("pool", t.t[:], dr[name], writes=[t])
            return t
        self.ident = cload("ident", BF16)
        self.ones = cload("ones", BF16)
        self.blk64 = cload("blk64", BF16)
        self.prot = cload("prot", BF16)
        self.onesf = cload("ones", F32) if False else None
        self.tri = [cload("tri_f", F32), cload("tri_b", F32)]
        self.negmb = [cload("negm_f", BF16), cload("negm_b", BF16)]
        self.sel64 = cload("sel64", F32)
        self.onesf = self.sb(es, "onesf", [128, 128], F32)
        s.dma("sp", self.onesf.t[:], dr["ones"], writes=[self.onesf])
        self.wrhs = self.sb(es, "wrhs", [128, 512], BF16)
        s.op("dve", lambda e: e.memset(self.wrhs.t[:], 1.0), writes=[self.wrhs])
        self.cst = self.sb(es, "cst", [128, 8], F32)
        import math
        for i, v in enumerate([EPS, math.log(0.125), math.log(96 ** -0.5), 1.0, 0.0]):
            s.op("dve", lambda e: e.memset(self.cst.t[:, i:i + 1], v), writes=[self.cst])
        self.sc = self.sb(es, "silu_c", [128, 8, J], F32)
        s.dma("sp", self.sc.t[:], dr["cT"], writes=[self.sc])
        s.op("act", lambda e: e.activation(out=self.sc.t[:], in_=self.sc.t[:], func=AF.Silu), reads=[self.sc], writes=[self.sc])
        self.mod = self.sb(es, "mod", [128, 48, J], F32)
        self.A1 = self.sb(es, "A1", [128, 8, J], F32)
        self.A2 = self.sb(es, "A2", [128, 8, J], F32)
        self.vec = self.sb(es, "vec", [128, 32], F32)
        self.nw = self.sb(es, "nw", [128, 16], F32)
        self.bada = self.sb(es, "bada", [128, 48], F32)
        self.rowbc = self.sb(es, "rowbc", [128, 552], F32)
        self.convw = self.sb(es, "convw", [128, 8, 5], F32)
        s.barrier()

        for l in range(L):
            last = (l == L - 1)
            import os
            stop = int(os.environ.get("KSTOP", "9"))
            if stop >= 1:
                self.phase_A(l)
                s.barrier()
            if l + 1 < L:
                self.convert(l + 1)
            if stop >= 2:
                self.phase_B(l)
                s.barrier()
            if stop >= 3:
                self.phase_att(l, last)
                s.barrier()
            if stop >= 4:
                self.phase_ssd(l, last)
                s.barrier()
            if stop >= 5:
                self.phase_FG(l, last)
                s.barrier()
        s.barrier()
        self.es.close()
        return nc

    def phase_A(self, l):
        nc, s, J = self.nc, self.s, self.J
        dr, drb = self.dr, self.drb
        with ExitStack() as es:
            wts = [self.sb(es, "wada%d" % i, [128, 6, 8, 128], F32) for i in range(2)]
            s.dma("sp", self.vec.t[:], dr["vec"][l], writes=[self.vec])
            s.dma("sp", self.nw.t[:, 0:8], dr["nw1"][l], writes=[self.nw])
            s.dma("sp", self.nw.t[:, 8:16], dr["nw2"][l], writes=[self.nw])
            s.dma("sp", self.bada.t[:], dr["bada"][l], writes=[self.bada])
            s.dma("sp", self.rowbc.t[:], dr["row"][l].partition_broadcast(128), writes=[self.rowbc])
            s.dma("sp", self.convw.t[:], dr["convw"][l], writes=[self.convw])
            ps = self.psum()
            for g in range(8):
                wt = wts[g % 2]
                s.dma("sp", wt.t[:], dr["wada"][l, g * 6:(g + 1) * 6].rearrange("c p k f -> p c k f"), writes=[wt])
                for c in range(6):
                    fc = g * 6 + c
                    for k in range(8):
                        s.op("pe", lambda e: e.matmul(ps.t[:, fc * J:(fc + 1) * J], wt.t[:, c, k, :], self.sc.t[:, k, :],
                                                      start=(k == 0), stop=(k == 7)), reads=[wt, self.sc], writes=[ps])
            s.op("dve", lambda e: e.tensor_tensor(out=self.mod.t[:], in0=ps.t[:, 0:48 * J].rearrange("p (c j) -> p c j", j=J),
                                                  in1=self.bada.t[:].unsqueeze(2).to_broadcast([128, 48, J]), op=ALU.add),
                 reads=[ps, self.bada], writes=[self.mod])
            for (A, off, nwo) in ((self.A1, 8, 0), (self.A2, 32, 8)):
                s.op("dve", lambda e: e.scalar_tensor_tensor(out=A.t[:], in0=self.mod.t[:, off:off + 8, :], scalar=1.0,
                                                             in1=self.nw.t[:, nwo:nwo + 8].unsqueeze(2).to_broadcast([128, 8, J]),
                                                             op0=ALU.add, op1=ALU.mult),
                     reads=[self.mod, self.nw], writes=[A])

    def norm_mod(self, xt, n, A, sh_off, j, xm, sq, rstd, tmp):
        s = self.s
        s.op("act", lambda e: e.activation(out=sq.t[:, :, 0:n], in_=xt.t[:, :, 0:n], func=AF.Square), reads=[xt], writes=[sq])
        ps = self.psum()
        for k in range(8):
            s.op("pe", lambda e: e.matmul(ps.t[:, 0:n], self.ones.t[:], sq.t[:, k, 0:n], start=(k == 0), stop=(k == 7)),
                 reads=[sq, self.ones], writes=[ps])
        s.op("act", lambda e: e.activation(out=rstd.t[:, 0:n], in_=ps.t[:, 0:n], func=AF.Ln, scale=1.0 / D, bias=self.cst.t[:, 0:1]),
             reads=[ps, self.cst], writes=[rstd])
        s.op("act", lambda e: e.activation(out=rstd.t[:, 0:n], in_=rstd.t[:, 0:n], func=AF.Exp, scale=-0.5), reads=[rstd], writes=[rstd])
        for k in range(8):
            tm = self.rot(tmp, "nm_tmp")
            s.op("dve", lambda e: e.scalar_tensor_tensor(out=tm.t[:, 0:n], in0=xt.t[:, k, 0:n], scalar=A.t[:, k, j:j + 1],
                                                         in1=rstd.t[:, 0:n], op0=ALU.mult, op1=ALU.mult),
                 reads=[xt, A, rstd], writes=[tm])
            s.op("act", lambda e: e.activation(out=xm.t[:, k, 0:n], in_=tm.t[:, 0:n], func=AF.Identity,
                                               bias=self.mod.t[:, sh_off + k, j:j + 1]),
                 reads=[tm, self.mod], writes=[xm])

    def fm_rstd(self, sqs, ones_ap, ones_tt, P, n, cnt, rstd, lnb=None):
        s = self.s
        ps = self.psum()
        for i, (tt, ap) in enumerate(sqs):
            s.op("pe", lambda e: e.matmul(ps.t[0:P, 0:n], ones_ap, ap, start=(i == 0), stop=(i == len(sqs) - 1)),
                 reads=[tt, ones_tt], writes=[ps])
        s.op("act", lambda e: e.activation(out=rstd.t[0:P, 0:n], in_=ps.t[0:P, 0:n], func=AF.Ln, scale=1.0 / cnt, bias=self.cst.t[0:P, 0:1]),
             reads=[ps, self.cst], writes=[rstd])
        if lnb is None:
            s.op("act", lambda e: e.activation(out=rstd.t[0:P, 0:n], in_=rstd.t[0:P, 0:n], func=AF.Exp, scale=-0.5),
                 reads=[rstd], writes=[rstd])
        else:
            s.op("act", lambda e: e.activation(out=rstd.t[0:P, 0:n], in_=rstd.t[0:P, 0:n], func=AF.Exp, scale=-0.5,
                                               bias=self.cst.t[0:P, lnb:lnb + 1]),
                 reads=[rstd, self.cst], writes=[rstd])

    def phase_B(self, l):
        nc, s, J, NB = self.nc, self.s, self.J, self.NB
        dr, drb = self.dr, self.drb
        with ExitStack() as es:
            winfm = self.sb(es, "winfm", [128, 16, 8, 128], BF16)
            winz = self.sb(es, "winz", [128, 8, 512], BF16)
            winvdt = self.sb(es, "winvdt", [128, 8, 272], BF16)
            wuq = self.sb(es, "wuq", [128, 4, 2, 96], BF16)
            wukvk = self.sb(es, "wukvk", [128, 4, 96], BF16)
            wukvv = self.sb(es, "wukvv", [128, 256], BF16)
            cosb = [self.sb(es, "cosb%d" % i, [128, 512], F32) for i in range(2)]
            sinb = [self.sb(es, "sinb%d" % i, [128, 512], F32) for i in range(2)]
            for i in range(0, 16, 4):
                s.dma("sp", winfm.t[:, i:i + 4], dr["winfm_bf"][l, i:i + 4].rearrange("c p k f -> p c k f"),
                      reads=[self.wb("winfm_bf", l)], writes=[winfm])
            for (tt, nm) in ((winz, "winz_bf"), (winvdt, "winvdt_bf"), (wuq, "wuq_bf"), (wukvk, "wukvk_bf"), (wukvv, "wukvv_bf")):
                s.dma("sp", tt.t[:], dr[nm][l], reads=[self.wb(nm, l)], writes=[tt])
            xts = [self.sb(es, "xt%d" % i, [128, 8, 512], F32) for i in range(2)]
            sq = self.sb(es, "sq", [128, 8, 512], BF16)
            xm = self.sb(es, "xm", [128, 8, 512], BF16)
            rstd = self.sb(es, "rstd", [128, 512], F32)
            tmp = [self.sb(es, "tmp%d" % i, [128, 512], F32) for i in range(8)]
            ubuf = [self.sb(es, "u%d" % i, [128, 512], F32) for i in range(5)]
            sqb = [self.sb(es, "sqb%d" % i, [128, 512], BF16) for i in range(5)]
            rs2 = [self.sb(es, "rs2_%d" % i, [128, 512], F32) for i in range(4)]
            obf = [self.sb(es, "obf%d" % i, [128, 512], BF16) for i in range(8)]
            cqn = self.sb(es, "cqn", [128, 2, 512], BF16)
            ckvn = self.sb(es, "ckvn", [128, 512], BF16)
            krp = self.sb(es, "krp", [128, 512], F32)
            vst = [self.sb(es, "vst%d" % i, [128, 4, 256], BF16) for i in range(2)]
            dst_ = [self.sb(es, "dst%d" % i, [128, 4, 16], F32) for i in range(2)]
            zst = [self.sb(es, "zst%d" % i, [128, 4, 512], BF16) for i in range(2)]
            vmst = [self.sb(es, "vmst%d" % i, [128, 4, 256], BF16) for i in range(2)]
            xsrc = "xT" if l == 0 else "XS"

            def head_group(items, n, t0, cs_t):
                for it in items:
                    it["ps"] = self.psum()
                    it["mm"](it["ps"])
                for it in items:
                    P = it["P"]
                    u = it["u"] = self.rot(ubuf, "u")
                    q = it["q"] = self.rot(sqb, "sqb")
                    src_ps = it["ps"]
                    if it["extra"] is None:
                        s.op("act", lambda e: e.activation(out=u.t[0:P, 0:n], in_=src_ps.t[0:P, 0:n], func=AF.Copy), reads=[src_ps], writes=[u])
                    else:
                        s.op("dve", lambda e: e.tensor_tensor(out=u.t[0:P, 0:n], in0=src_ps.t[0:P, 0:n], in1=it["extra"].t[0:P, 0:n], op=ALU.add),
                             reads=[src_ps, it["extra"]], writes=[u])
                for it in items:
                    P, u, q = it["P"], it["u"], it["q"]
                    s.op("act", lambda e: e.activation(out=q.t[0:P, 0:n], in_=u.t[0:P, 0:n], func=AF.Square), reads=[u], writes=[q])
                for it in items:
                    P, q = it["P"], it["q"]
                    ps = it["ps2"] = self.psum()
                    if P == 128:
                        s.op("pe", lambda e: e.matmul(ps.t[:, 0:n], self.blk64.t[:], q.t[:, 0:n], start=True, stop=True), reads=[q, self.blk64], writes=[ps])
                    else:
                        s.op("pe", lambda e: e.matmul(ps.t[0:P, 0:n], self.ones.t[0:P, 0:P], q.t[0:P, 0:n], start=True, stop=True), reads=[q, self.ones], writes=[ps])
                for it in items:
                    P, ps = it["P"], it["ps2"]
                    r = it["r"] = self.rot(rs2, "rs2")
                    cnt = 64 if P == 128 else P
                    s.op("act", lambda e: e.activation(out=r.t[0:P, 0:n], in_=ps.t[0:P, 0:n], func=AF.Ln, scale=1.0 / cnt, bias=self.cst.t[0:P, 0:1]),
                         reads=[ps, self.cst], writes=[r])
                for it in items:
                    P, r, lnb = it["P"], it["r"], it["lnb"]
                    if lnb is None:
                        s.op("act", lambda e: e.activation(out=r.t[0:P, 0:n], in_=r.t[0:P, 0:n], func=AF.Exp, scale=-0.5), reads=[r], writes=[r])
                    else:
                        s.op("act", lambda e: e.activation(out=r.t[0:P, 0:n], in_=r.t[0:P, 0:n], func=AF.Exp, scale=-0.5, bias=self.cst.t[0:P, lnb:lnb + 1]),
                             reads=[r, self.cst], writes=[r])
                for it in items:
                    P, u, r, wcol = it["P"], it["u"], it["r"], it["wcol"]
                    o = it["o"] = self.rot(obf, "obf")
                    s.op("dve", lambda e: e.scalar_tensor_tensor(out=o.t[0:P, 0:n], in0=u.t[0:P, 0:n], scalar=self.vec.t[0:P, wcol:wcol + 1],
                                                                 in1=r.t[0:P, 0:n], op0=ALU.mult, op1=ALU.mult),
                         reads=[u, self.vec, r], writes=[o])
                if items[0]["P"] == 96:
                    P = 96
                    cT_, sT_ = cs_t
                    for it in items:
                        o = it["o"]
                        pr = it["pr"] = self.psum()
                        s.op("pe", lambda e: e.matmul(pr.t[0:P, 0:n], self.prot.t[0:P, 0:P], o.t[0:P, 0:n], start=True, stop=True),
                             reads=[o, self.prot], writes=[pr])
                    for it in items:
                        o, pr = it["o"], it["pr"]
                        t1 = it["t1"] = self.rot(tmp, "nm_tmp")
                        t2 = it["t2"] = self.rot(tmp, "nm_tmp")
                        s.op("dve", lambda e: e.tensor_tensor(out=t1.t[0:P, 0:n], in0=pr.t[0:P, 0:n], in1=sT_.t[0:P, 0:n], op=ALU.mult),
                             reads=[pr, sT_], writes=[t1])
                        s.op("pool", lambda e: e.tensor_tensor(out=t2.t[0:P, 0:n], in0=o.t[0:P, 0:n], in1=cT_.t[0:P, 0:n], op=ALU.mult),
                             reads=[o, cT_], writes=[t2])
                    for it in items:
                        t1, t2 = it["t1"], it["t2"]
                        o2 = self.rot(obf, "obf")
                        s.op("dve", lambda e: e.tensor_tensor(out=o2.t[0:P, 0:n], in0=t1.t[0:P, 0:n], in1=t2.t[0:P, 0:n], op=ALU.add),
                             reads=[t1, t2], writes=[o2])
                        it["o"] = o2
                for it in items:
                    P, o = it["P"], it["o"]
                    s.dma("pool", it["dst"], o.t[0:P, 0:n], reads=[o], writes=[drb[it["dname"]]])

            for b in range(NB):
                for ti, (t0, n) in enumerate(TILES):
                    j = NB if ti == 4 else b
                    xt = self.rot(xts, "xt")
                    s.dma("sp", xt.t[:, :, 0:n], dr[xsrc][b].rearrange("(k p) t -> p k t", p=128)[:, :, t0:t0 + n],
                          reads=[drb[xsrc]], writes=[xt])
                    cT_, sT_ = self.rot(cosb, "cosb"), self.rot(sinb, "sinb")
                    s.dma("sp", cT_.t[:, 0:n], dr["cosT"][:, t0:t0 + n], writes=[cT_])
                    s.dma("sp", sT_.t[:, 0:n], dr["sinT"][:, t0:t0 + n], writes=[sT_])
                    self.norm_mod(xt, n, self.A1, 0, j, xm, sq, rstd, tmp)
                    nt = n // 128

                    def proj_mm(fc, P=128):
                        def f(ps):
                            for k in range(8):
                                s.op("pe", lambda e: e.matmul(ps.t[0:P, 0:n], winfm.t[:, fc, k, 0:P], xm.t[:, k, 0:n], start=(k == 0), stop=(k == 7)),
                                     reads=[winfm, xm], writes=[ps])
                        return f

                    def proj(fc, P=128):
                        ps = self.psum()
                        proj_mm(fc, P)(ps)
                        return ps
                    head_group([dict(mm=proj_mm(fc), P=128, wcol=0 if fc < 2 else 1, lnb=1 if fc < 2 else None,
                                     dst=dr["QKNA"][b, fc * 128:(fc + 1) * 128, t0:t0 + n], dname="QKNA", extra=None) for fc in range(4)], n, t0, None)
                    for c in range(8):
                        ps = proj(4 + c)
                        o = self.rot(obf, "obf")
                        if c % 2 == 0:
                            s.op("act", lambda e: e.activation(out=o.t[:, 0:n], in_=ps.t[:, 0:n], func=AF.Copy), reads=[ps], writes=[o])
                        else:
                            s.op("dve", lambda e: e.tensor_copy(out=o.t[:, 0:n], in_=ps.t[:, 0:n]), reads=[ps], writes=[o])
                        s.dma("pool", dr["XBC"][b, c * 128:(c + 1) * 128, t0:t0 + n], o.t[:, 0:n], reads=[o], writes=[drb["XBC"]])
                    vs, ds, zs = self.rot(vst, "vst"), self.rot(dst_, "dst"), self.rot(zst, "zst")
                    for tc in range(nt):
                        ps = self.psum()
                        for k in range(8):
                            s.op("pe", lambda e: e.matmul(ps.t[:, 0:272], xm.t[:, k, tc * 128:(tc + 1) * 128], winvdt.t[:, k, :], start=(k == 0), stop=(k == 7)),
                                 reads=[winvdt, xm], writes=[ps])
                        s.op("act", lambda e: e.activation(out=vs.t[:, tc, :], in_=ps.t[:, 0:256], func=AF.Copy), reads=[ps], writes=[vs])
                        s.op("act", lambda e: e.activation(out=ds.t[:, tc, :], in_=ps.t[:, 256:272], func=AF.Copy), reads=[ps], writes=[ds])
                        ps2 = self.psum()
                        for k in range(8):
                            s.op("pe", lambda e: e.matmul(ps2.t[:, :], xm.t[:, k, tc * 128:(tc + 1) * 128], winz.t[:, k, :], start=(k == 0), stop=(k == 7)),
                                 reads=[winz, xm], writes=[ps2])
                        s.op("dve", lambda e: e.tensor_copy(out=zs.t[:, tc, :], in_=ps2.t[:, :]), reads=[ps2], writes=[zs])
                    s.dma("pool", dr["VNA"][b, t0:t0 + n, :].rearrange("(c p) f -> p c f", p=128), vs.t[:, 0:nt, :], reads=[vs], writes=[drb["VNA"]])
                    s.dma("pool", dr["DTS"][b, t0:t0 + n, :].rearrange("(c p) f -> p c f", p=128), ds.t[:, 0:nt, :], reads=[ds], writes=[drb["DTS"]])
                    s.dma("pool", dr["ZS"][b, t0:t0 + n, :].rearrange("(c p) f -> p c f", p=128), zs.t[:, 0:nt, :], reads=[zs], writes=[drb["ZS"]])
                    us = []
                    for c in range(2):
                        ps = proj(12 + c)
                        u = self.rot(ubuf, "u")
                        s.op("act", lambda e: e.activation(out=u.t[:, 0:n], in_=ps.t[:, 0:n], func=AF.Copy), reads=[ps], writes=[u])
                        q = self.rot(sqb, "sqb")
                        s.op("act", lambda e: e.activation(out=q.t[:, 0:n], in_=ps.t[:, 0:n], func=AF.Square), reads=[ps], writes=[q])
                        us.append((u, q))
                    ps = proj(14)
                    ukv = self.rot(ubuf, "u")
                    s.op("act", lambda e: e.activation(out=ukv.t[:, 0:n], in_=ps.t[:, 0:n], func=AF.Copy), reads=[ps], writes=[ukv])
                    qkv = self.rot(sqb, "sqb")
                    s.op("act", lambda e: e.activation(out=qkv.t[:, 0:n], in_=ps.t[:, 0:n], func=AF.Square), reads=[ps], writes=[qkv])
                    ps = proj(15, 96)
                    s.op("dve", lambda e: e.tensor_copy(out=krp.t[0:96, 0:n], in_=ps.t[0:96, 0:n]), reads=[ps], writes=[krp])
                    r = self.rot(rs2, "rs2")
                    self.fm_rstd([(q, q.t[:, 0:n]) for (u, q) in us], self.ones.t[:], self.ones, 128, n, 256, r)
                    r2 = self.rot(rs2, "rs2")
                    self.fm_rstd([(qkv, qkv.t[:, 0:n])], self.ones.t[:], self.ones, 128, n, 128, r2)
                    for c in range(2):
                        s.op("dve", lambda e: e.scalar_tensor_tensor(out=cqn.t[:, c, 0:n], in0=us[c][0].t[:, 0:n], scalar=self.vec.t[:, 2 + c:3 + c],
                                                                     in1=r.t[:, 0:n], op0=ALU.mult, op1=ALU.mult),
                             reads=[us[c][0], self.vec, r], writes=[cqn])
                    s.op("dve", lambda e: e.scalar_tensor_tensor(out=ckvn.t[:, 0:n], in0=ukv.t[:, 0:n], scalar=self.vec.t[:, 4:5],
                                                                 in1=r2.t[:, 0:n], op0=ALU.mult, op1=ALU.mult),
                         reads=[ukv, self.vec, r2], writes=[ckvn])

                    def uq_mm(h):
                        def f(ps):
                            for c in range(2):
                                s.op("pe", lambda e: e.matmul(ps.t[0:96, 0:n], wuq.t[:, h, c, :], cqn.t[:, c, 0:n], start=(c == 0), stop=(c == 1)),
                                     reads=[wuq, cqn], writes=[ps])
                        return f

                    def uk_mm(h):
                        def f(ps):
                            s.op("pe", lambda e: e.matmul(ps.t[0:96, 0:n], wukvk.t[:, h, :], ckvn.t[:, 0:n], start=True, stop=True),
                                 reads=[wukvk, ckvn], writes=[ps])
                        return f
                    head_group([dict(mm=uq_mm(h), P=96, wcol=5, lnb=2, dst=dr["QM"][b, h, :, t0:t0 + n], dname="QM", extra=None) for h in range(4)],
                               n, t0, (cT_, sT_))
                    head_group([dict(mm=uk_mm(h), P=96, wcol=6, lnb=None, dst=dr["KM"][b, h, :, t0:t0 + n], dname="KM", extra=krp) for h in range(4)],
                               n, t0, (cT_, sT_))
                    vm = self.rot(vmst, "vmst")
                    for tc in range(nt):
                        ps = self.psum()
                        s.op("pe", lambda e: e.matmul(ps.t[:, 0:256], ckvn.t[:, tc * 128:(tc + 1) * 128], wukvv.t[:], start=True, stop=True),
                             reads=[wukvv, ckvn], writes=[ps])
                        s.op("act", lambda e: e.activation(out=vm.t[:, tc, :], in_=ps.t[:, 0:256], func=AF.Copy), reads=[ps], writes=[vm])
                    s.dma("pool", dr["VM"][b, t0:t0 + n, :].rearrange("(c p) f -> p c f", p=128), vm.t[:, 0:nt, :], reads=[vm], writes=[drb["VM"]])

    def phase_att(self, l, last):
        nc, s, NB = self.nc, self.s, self.NB
        dr, drb = self.dr, self.drb
        with ExitStack() as es:
            kT = [self.sb(es, "kT%d" % i, [128, T], BF16) for i in range(2)]
            va = [self.sb(es, "va%d" % i, [128, 18, 128], BF16) for i in range(2)]
            qT = [self.sb(es, "qT%d" % i, [128, 512], BF16) for i in range(3)]
            bias = [self.sb(es, "bias%d" % i, [128, 8, 512], BF16) for i in range(2)]
            pT = [self.sb(es, "pT%d" % i, [128, 2, 512], BF16) for i in range(3)]
            sbias = [self.sb(es, "sbias%d" % i, [128, 2, 512], F32) for i in range(2)]
            osb = [self.sb(es, "osb%d" % i, [128, 512], F32) for i in range(2)]
            rrow = [self.sb(es, "rrow%d" % i, [128, 512], F32) for i in range(2)]
            omix = [self.sb(es, "omix%d" % i, [64, 512], BF16) for i in range(2)]
            for i in range(2):
                s.op("dve", lambda e: e.memset(va[i].t[:, :, 64:128], 1.0), writes=[va[i]])
            accs = [self.ps[4], self.ps[5]]
            wp = self.pp[0:2]
            pbank = [self.ps[6]]
            self.warm(24)
            self.att_tail = []

            def attend(b, d, k_t, v_t, q_src_ap, nq, chunks, bias_t, mix_rows, q_tok0):
                q = self.rot(qT, "qT")
                s.dma("sp", q.t[0:d, 0:nq], q_src_ap, reads=[drb["QKNA"], drb["QM"]], writes=[q])
                acc = self.rot(accs, "acc")
                npair = len(chunks) // 2
                pend = []

                def pv(item):
                    pi, p = item
                    for jj in range(2):
                        ci = 2 * pi + jj
                        kc = chunks[ci][0]
                        s.op("pe", lambda e: e.matmul(acc.t[:, 0:nq], v_t.t[:, kc, :], p.t[:, jj, 0:nq], start=(ci == 0), stop=(ci == len(chunks) - 1)),
                             reads=[v_t, p], writes=[acc])
                for pi in range(npair):
                    ps = self.rot(wp, "wp")
                    psv = ps.t.rearrange("p (j n) -> p j n", j=2)
                    for jj in range(2):
                        kc = chunks[2 * pi + jj][0]
                        s.op("pe", lambda e: e.matmul(ps.t[:, jj * 512:jj * 512 + nq], k_t.t[0:d, kc * 128:(kc + 1) * 128], q.t[0:d, 0:nq], start=True, stop=True),
                             reads=[k_t, q], writes=[ps])
                    bs = chunks[2 * pi][1]
                    p = self.rot(pT, "pT")
                    if bs is not None:
                        sb_ = self.rot(sbias, "sbias")
                        s.op("dve", lambda e: e.tensor_tensor(out=sb_.t[:, :, 0:nq], in0=psv[:, :, 0:nq], in1=bias_t.t[:, bs:bs + 2, 0:nq], op=ALU.add),
                             reads=[ps, bias_t], writes=[sb_])
                        s.op("act", lambda e: e.activation(out=p.t[:, :, 0:nq], in_=sb_.t[:, :, 0:nq], func=AF.Exp), reads=[sb_], writes=[p])
                    else:
                        s.op("act", lambda e: e.activation(out=p.t[:, :, 0:nq], in_=psv[:, :, 0:nq], func=AF.Exp), reads=[ps], writes=[p])
                    pend.append((pi, p))
                    if len(pend) > 1:
                        pv(pend.pop(0))
                    if pi in (0, 3) and self.att_tail:
                        self.att_tail.pop(0)()
                while pend:
                    pv(pend.pop(0))
                while self.att_tail:
                    self.att_tail.pop(0)()
                st = {}
                self.att_tail = [lambda: _tail1(st, nq, acc, d), lambda: _tail2(st, b, nq, mix_rows, q_tok0)]

            def _tail1(st, nq, acc, d=96):
                rr = st["rr"] = self.rot(rrow, "rrow")
                if d == 64:
                    s.op("act", lambda e: e.activation(out=rr.t[64:128, 0:nq], in_=acc.t[64:128, 0:nq], func=AF.Ln), reads=[acc], writes=[rr])
                    s.op("act", lambda e: e.activation(out=rr.t[64:128, 0:nq], in_=rr.t[64:128, 0:nq], func=AF.Exp, scale=-1.0), reads=[rr], writes=[rr])
                else:
                    s.op("dve", lambda e: e.reciprocal(out=rr.t[64:128, 0:nq], in_=acc.t[64:128, 0:nq]), reads=[acc], writes=[rr])
                st["acc"] = acc

            def _tail2(st, b, nq, mix_rows, q_tok0):
                rr, acc = st["rr"], st["acc"]
                om = self.rot(omix, "omix")
                s.op("dve", lambda e: e.tensor_tensor(out=om.t[:, 0:nq], in0=acc.t[0:64, 0:nq], in1=rr.t[64:128, 0:nq], op=ALU.mult),
                     reads=[acc, rr], writes=[om])
                s.dma("pool", dr["MIXT"][b, mix_rows:mix_rows + 64, q_tok0:q_tok0 + nq], om.t[:, 0:nq], reads=[om], writes=[drb["MIXT"]])

            for b in range(NB):
                for h in range(4):
                    k_t, v_t = self.rot(kT, "kT"), self.rot(va, "va")
                    s.dma("sp", k_t.t[0:64, :], dr["QKNA"][b, 256 + h * 64:256 + (h + 1) * 64, :], reads=[drb["QKNA"]], writes=[k_t])
                    s.dma("sp", v_t.t[:, :, 0:64], dr["VNA"][b, :, h * 64:(h + 1) * 64].rearrange("(c p) f -> p c f", p=128),
                          reads=[drb["VNA"]], writes=[v_t])
                    for g in range(4):
                        tok0, nch, bt0 = NA_GROUPS[g]
                        bias_t = self.rot(bias, "bias")
                        s.dma("sp", bias_t.t[:, 0:nch, :], dr["nab_bf"][l, h, bt0:bt0 + nch].rearrange("c p q -> p c q"),
                              reads=[self.wb("nab_bf", l)], writes=[bias_t])
                        chunks = [(tok0 // 128 + i, i) for i in range(nch)] + [(16, None), (17, None)]
                        attend(b, 64, k_t, v_t, dr["QKNA"][b, h * 64:(h + 1) * 64, g * 512:(g + 1) * 512], 512, chunks, bias_t, h * 64, g * 512)
                    if not last:
                        attend(b, 64, k_t, v_t, dr["QKNA"][b, h * 64:(h + 1) * 64, SEQ:T], 256, [(16, None), (17, None)], None, h * 64, SEQ)
                for h in range(4):
                    k_t, v_t = self.rot(kT, "kT"), self.rot(va, "va")
                    s.dma("sp", k_t.t[0:96, :], dr["KM"][b, h], reads=[drb["KM"]], writes=[k_t])
                    s.dma("sp", v_t.t[:, :, 0:64], dr["VM"][b, :, h * 64:(h + 1) * 64].rearrange("(c p) f -> p c f", p=128),
                          reads=[drb["VM"]], writes=[v_t])
                    for g in range(4):
                        attend(b, 96, k_t, v_t, dr["QM"][b, h, :, g * 512:(g + 1) * 512], 512, [(i, None) for i in range(18)], None, 768 + h * 64, g * 512)
                    if not last:
                        attend(b, 96, k_t, v_t, dr["QM"][b, h, :, SEQ:T], 256, [(16, None), (17, None)], None, 768 + h * 64, SEQ)
            while self.att_tail:
                self.att_tail.pop(0)()

    def phase_ssd(self, l, last):
        nc, s, NB = self.nc, self.s, self.NB
        dr, drb = self.dr, self.drb
        W = 2312
        import os
        WA = int(os.environ.get("KWA", "0"))
        WB = int(os.environ.get("KWB", "0"))
        with ExitStack() as es:
            xp = [self.sb(es, "xp%d" % i, [128, W], BF16) for i in range(2)]
            xsfm = self.sb(es, "xsfm", [128, 4, T], BF16)
            BT = self.sb(es, "BT", [128, 2, T], BF16)
            CT = self.sb(es, "CT", [128, 2, T], BF16)
            xstm = self.sb(es, "xstm", [128, 18, 512], BF16)
            Btm = self.sb(es, "Btm", [128, 18, 256], BF16)
            raw = self.sb(es, "dtraw", [128, 18, 16], F32)
            dt = self.sb(es, "dt", [128, 18, 16], F32)
            a = self.sb(es, "a", [128, 18, 16], F32)
            acum = self.sb(es, "acum", [128, 18, 16], F32)
            nacum = self.sb(es, "nacum", [128, 18, 16], F32)
            atot = self.sb(es, "atot", [128, 18, 16], F32)
            eac = self.sb(es, "eac", [128, 18, 16], F32)
            dte = self.sb(es, "dte", [128, 18, 16], F32)
            etot = self.sb(es, "etot", [128, 18, 16], F32)
            aneg = self.sb(es, "aneg", [128, 16], F32)
            dsk = self.sb(es, "dsk", [128, 8, 64], F32)
            yacc = self.sb(es, "yacc", [128, 18, 512], F32)
            zts = [self.sb(es, "zt%d" % i, [128, 512], BF16) for i in range(3)]
            ymix = xsfm
            H = [self.sb(es, "H%d" % i, [128, 512], F32) for i in range(2)]
            Hbf = [self.sb(es, "Hbf%d" % i, [128, 512], BF16) for i in range(2)]
            xd = [self.sb(es, "xd%d" % i, [128, 512], BF16) for i in range(6)]
            xdd = [self.sb(es, "xdd%d" % i, [128, 512], BF16) for i in range(6)]
            abc = [self.sb(es, "abc%d" % i, [128, 8, 128], F32) for i in range(2)]
            T1 = [self.sb(es, "T1_%d" % i, [128, 8, 128], BF16) for i in range(4)]
            MT = [self.sb(es, "MT%d" % i, [128, 8, 128], BF16) for i in range(6)]
            tb = [self.sb(es, "tb%d" % i, [128, 512], F32) for i in range(4)]
            ssq = self.sb(es, "ssq", [128, 16], F32)
            obf = [self.sb(es, "sobf%d" % i, [128, 512], BF16) for i in range(3)]
            szb = [self.sb(es, "szb%d" % i, [128, 512], BF16) for i in range(3)]
            junk = self.sb(es, "junk", [128, 512], BF16)
            for i in range(2):
                s.op("dve", lambda e: e.memset(xp[i].t[:, 0:2], 0.0), writes=[xp[i]])
                s.op("dve", lambda e: e.memset(xp[i].t[:, 2050:2054], 0.0), writes=[xp[i]])
                s.op("dve", lambda e: e.memset(xp[i].t[:, 2310:2312], 0.0), writes=[xp[i]])
            s.op("act", lambda e: e.activation(out=aneg.t[:], in_=self.rowbc.t[:, 16:32], func=AF.Exp), reads=[self.rowbc], writes=[aneg])
            s.op("dve", lambda e: e.tensor_scalar(out=aneg.t[:], in0=aneg.t[:], scalar1=-1.0, scalar2=None, op0=ALU.mult), reads=[aneg], writes=[aneg])
            s.op("dve", lambda e: e.tensor_copy(out=dsk.t[:], in_=self.rowbc.t[:, 32:40].unsqueeze(2).to_broadcast([128, 8, 64])),
                 reads=[self.rowbc], writes=[dsk])
            nwbc = self.rowbc.t[:, 40:552]
            negm4 = [self.sb(es, "negm4_%d" % i, [128, 4, 128], BF16) for i in range(2)]
            for i in range(2):
                s.op("dve", lambda e: e.tensor_copy(out=negm4[i].t[:], in_=self.negmb[i].t[:].unsqueeze(1).to_broadcast([128, 4, 128])),
                     reads=[self.negmb[i]], writes=[negm4[i]])

            def bf16view(ps):
                return ps.t[:].bitcast(BF16)

            for b in range(NB):
                for c in range(8):
                    x_ = self.rot(xp, "xp")
                    s.dma("sp", x_.t[:, 2:2050], dr["XBC"][b, c * 128:(c + 1) * 128, 0:SEQ], reads=[drb["XBC"]], writes=[x_])
                    s.dma("sp", x_.t[:, 2054:2310], dr["XBC"][b, c * 128:(c + 1) * 128, SEQ:T], reads=[drb["XBC"]], writes=[x_])
                    v = yacc
                    vt = yacc.t[:, 0:5, :].rearrange("p c f -> p (c f)")[:, 0:2308]
                    s.op("dve", lambda e: e.tensor_scalar(out=vt, in0=x_.t[:, 0:2308], scalar1=self.convw.t[:, c, 0:1], scalar2=None, op0=ALU.mult),
                         reads=[x_, self.convw], writes=[v])
                    for jj in range(1, 5):
                        s.op("dve", lambda e: e.scalar_tensor_tensor(out=vt, in0=x_.t[:, jj:jj + 2308], scalar=self.convw.t[:, c, jj:jj + 1],
                                                                     in1=vt, op0=ALU.mult, op1=ALU.add),
                             reads=[x_, self.convw, v], writes=[v])
                    if c < 4:
                        dst, dl, dc = xsfm, xsfm.t[:, c, 0:SEQ], xsfm.t[:, c, SEQ:T]
                    elif c < 6:
                        dst, dl, dc = BT, BT.t[:, c - 4, 0:SEQ], BT.t[:, c - 4, SEQ:T]
                    else:
                        dst, dl, dc = CT, CT.t[:, c - 6, 0:SEQ], CT.t[:, c - 6, SEQ:T]
                    s.op("act", lambda e: e.activation(out=dl, in_=vt[:, 0:SEQ], func=AF.Silu, bias=self.vec.t[:, 8 + c:9 + c]),
                         reads=[v, self.vec], writes=[dst])
                    s.op("act", lambda e: e.activation(out=dc, in_=vt[:, 2052:2308], func=AF.Silu, bias=self.vec.t[:, 8 + c:9 + c]),
                         reads=[v, self.vec], writes=[dst])
                for tc in range(18):
                    ps = self.psum()
                    pv = bf16view(ps)
                    for ci in range(4):
                        s.op("pe", lambda e: e.transpose(pv[:, ci * 128:(ci + 1) * 128], xsfm.t[:, ci, tc * 128:(tc + 1) * 128], self.ident.t[:]),
                             reads=[xsfm, self.ident], writes=[ps])
                    for ci in range(2):
                        s.op("pe", lambda e: e.transpose(pv[:, 512 + ci * 128:512 + (ci + 1) * 128], BT.t[:, ci, tc * 128:(tc + 1) * 128], self.ident.t[:]),
                             reads=[BT, self.ident], writes=[ps])
                    s.op("act", lambda e: e.activation(out=xstm.t[:, tc, :], in_=pv[:, 0:512], func=AF.Copy), reads=[ps], writes=[xstm])
                    s.op("dve", lambda e: e.tensor_copy(out=Btm.t[:, tc, :], in_=pv[:, 512:768]), reads=[ps], writes=[Btm])
                s.dma("sp", raw.t[:], dr["DTS"][b].rearrange("(c p) f -> p c f", p=128), reads=[drb["DTS"]], writes=[raw])
                s.op("dve", lambda e: e.tensor_tensor(out=dt.t[:], in0=raw.t[:], in1=self.rowbc.t[:, 0:16].unsqueeze(1).to_broadcast([128, 18, 16]), op=ALU.add),
                     reads=[raw, self.rowbc], writes=[dt])
                s.op("dve", lambda e: e.tensor_scalar(out=dt.t[:], in0=dt.t[:], scalar1=30.0, scalar2=None, op0=ALU.min), reads=[dt], writes=[dt])
                s.op("act", lambda e: e.activation(out=dt.t[:], in_=dt.t[:], func=AF.Exp), reads=[dt], writes=[dt])
                s.op("act", lambda e: e.activation(out=dt.t[:], in_=dt.t[:], func=AF.Ln, bias=self.cst.t[:, 3:4]), reads=[dt, self.cst], writes=[dt])
                s.op("dve", lambda e: e.tensor_tensor(out=a.t[:], in0=dt.t[:], in1=aneg.t[:].unsqueeze(1).to_broadcast([128, 18, 16]), op=ALU.mult),
                     reads=[dt, aneg], writes=[a])
                for d in range(2):
                    ps = self.psum()
                    s.op("pe", lambda e: e.matmul(ps.t[:, 0:144], self.tri[d].t[:], a.t[:, :, d * 8:(d + 1) * 8], start=True, stop=True),
                         reads=[self.tri[d], a], writes=[ps])
                    s.op("dve", lambda e: e.tensor_copy(out=acum.t[:, :, d * 8:(d + 1) * 8], in_=ps.t[:, 0:144].rearrange("p (c h) -> p c h", h=8)),
                         reads=[ps], writes=[acum])
                    ps = self.psum()
                    s.op("pe", lambda e: e.matmul(ps.t[:, 0:144], self.onesf.t[:], a.t[:, :, d * 8:(d + 1) * 8], start=True, stop=True),
                         reads=[self.onesf, a], writes=[ps])
                    s.op("dve", lambda e: e.tensor_copy(out=atot.t[:, :, d * 8:(d + 1) * 8], in_=ps.t[:, 0:144].rearrange("p (c h) -> p c h", h=8)),
                         reads=[ps], writes=[atot])
                s.op("dve", lambda e: e.tensor_scalar(out=nacum.t[:], in0=acum.t[:], scalar1=-1.0, scalar2=None, op0=ALU.mult), reads=[acum], writes=[nacum])
                s.op("act", lambda e: e.activation(out=eac.t[:], in_=acum.t[:], func=AF.Exp), reads=[acum], writes=[eac])
                s.op("act", lambda e: e.activation(out=etot.t[:], in_=atot.t[:], func=AF.Exp), reads=[atot], writes=[etot])
                s.op("dve", lambda e: e.tensor_tensor(out=dte.t[:], in0=atot.t[:], in1=acum.t[:], op=ALU.subtract), reads=[atot, acum], writes=[dte])
                s.op("act", lambda e: e.activation(out=dte.t[:], in_=dte.t[:], func=AF.Exp), reads=[dte], writes=[dte])
                orders = [[16, 17] + list(range(16)), [17, 16] + list(range(15, -1, -1))]
                touched = set()
                for d in range(2):
                    s.op("dve", lambda e: e.memset(H[d].t[:], 0.0), writes=[H[d]])
                    s.op("dve", lambda e: e.memset(Hbf[d].t[:], 0.0), writes=[Hbf[d]])

                def mk(oi, d):
                    c = orders[d][oi]
                    return dict(oi=oi, d=d, c=c, hs=slice(d * 8, (d + 1) * 8), cs=slice(c * 128, (c + 1) * 128),
                                y=not (last and c >= 16), mt=None)

                def stage_a(cxs):
                    for cx in cxs:
                        c, hs = cx["c"], cx["hs"]
                        x1, x2 = cx["x1"], cx["x2"] = self.rot(xd, "xd"), self.rot(xdd, "xdd")
                        s.op("pool", lambda e: e.tensor_tensor(out=x1.t[:].rearrange("p (h q) -> p h q", q=64), in0=xstm.t[:, c, :].rearrange("p (h q) -> p h q", q=64),
                                                               in1=dt.t[:, c, hs].unsqueeze(2).to_broadcast([128, 8, 64]), op=ALU.mult),
                             reads=[xstm, dt], writes=[x1])
                        s.op("pool", lambda e: e.tensor_tensor(out=x2.t[:].rearrange("p (h q) -> p h q", q=64), in0=x1.t[:].rearrange("p (h q) -> p h q", q=64),
                                                               in1=dte.t[:, c, hs].unsqueeze(2).to_broadcast([128, 8, 64]), op=ALU.mult),
                             reads=[x1, dte], writes=[x2])
                    ys = [cx for cx in cxs if cx["y"]]
                    for cx in ys:
                        c, hs, cs_, d = cx["c"], cx["hs"], cx["cs"], cx["d"]
                        pcb = cx["pcb"] = self.psum()
                        for g in range(2):
                            s.op("pe", lambda e: e.matmul(pcb.t[:, g * 128:(g + 1) * 128], BT.t[:, g, cs_], CT.t[:, g, cs_], start=True, stop=True),
                                 reads=[BT, CT], writes=[pcb])
                        ab = cx["ab"] = self.rot(abc, "abc")
                        s.op("dve", lambda e: e.tensor_tensor(out=ab.t[:], in0=self.tri[d].t[:].unsqueeze(1).to_broadcast([128, 8, 128]),
                                                              in1=a.t[:, c, hs].unsqueeze(2).to_broadcast([128, 8, 128]), op=ALU.mult),
                             reads=[a, self.tri[d]], writes=[ab])
                    for cx in ys:
                        d, ab = cx["d"], cx["ab"]
                        pd = cx["pd"] = [self.psum(), self.psum()]
                        for half in range(2):
                            reg = pd[half].t[:, :]
                            s.op("pe", lambda e: e.matmul(reg, self.ident.t[:], negm4[d].t[:].rearrange("p r l -> p (r l)"), start=True, stop=False),
                                 reads=[self.ident, negm4[d]], writes=[pd[half]])
                            s.op("pe", lambda e: e.matmul(reg, self.onesf.t[:], ab.t[:, half * 4:(half + 1) * 4, :].rearrange("p h l -> p (h l)"), start=False, stop=True),
                                 reads=[self.onesf, ab], writes=[pd[half]])
                    for cx in ys:
                        c, d, pd = cx["c"], cx["d"], cx["pd"]
                        t1 = cx["t1"] = self.rot(T1, "T1")
                        for hh in range(8):
                            reg = pd[hh // 4].t[:, (hh % 4) * 128:(hh % 4 + 1) * 128]
                            s.op("act", lambda e: e.activation(out=t1.t[:, hh, :], in_=reg, func=AF.Exp, bias=nacum.t[:, c, d * 8 + hh:d * 8 + hh + 1]),
                                 reads=[pd[hh // 4], nacum], writes=[t1])
                    for cx in ys:
                        t1, pcb = cx["t1"], cx["pcb"]
                        mt = cx["mt"] = self.rot(MT, "MT")
                        for g in range(2):
                            s.op("dve", lambda e: e.tensor_tensor(out=mt.t[:, g * 4:(g + 1) * 4, :], in0=t1.t[:, g * 4:(g + 1) * 4, :],
                                                                  in1=pcb.t[:, g * 128:(g + 1) * 128].unsqueeze(1).to_broadcast([128, 4, 128]), op=ALU.mult),
                                 reads=[t1, pcb], writes=[mt])
                    return cxs

                def stage_b(cxs):
                    ys = [cx for cx in cxs if cx["mt"] is not None]
                    for cx in ys:
                        d, cs_, x1, mt = cx["d"], cx["cs"], cx["x1"], cx["mt"]
                        py = cx["py"] = self.psum()
                        for hh in range(8):
                            s.op("pe", lambda e: e.matmul(py.t[:, hh * 64:(hh + 1) * 64], mt.t[:, hh, :], x1.t[:, hh * 64:(hh + 1) * 64], start=True, stop=True),
                                 reads=[mt, x1], writes=[py])
                        pyo = cx["pyo"] = self.psum()
                        for g in range(2):
                            s.op("pe", lambda e: e.matmul(pyo.t[:, g * 256:(g + 1) * 256], CT.t[:, g, cs_], Hbf[d].t[:, g * 256:(g + 1) * 256], start=True, stop=True),
                                 reads=[CT, Hbf[d]], writes=[pyo])
                    upd = [cx for cx in cxs if cx["oi"] < 17]
                    for cx in upd:
                        d, c, x2, hs = cx["d"], cx["c"], cx["x2"], cx["hs"]
                        pcs = cx["pcs"] = self.psum()
                        for g in range(2):
                            s.op("pe", lambda e: e.matmul(pcs.t[:, g * 256:(g + 1) * 256], Btm.t[:, c, g * 128:(g + 1) * 128], x2.t[:, g * 256:(g + 1) * 256], start=True, stop=True),
                                 reads=[Btm, x2], writes=[pcs])
                        s.op("pool", lambda e: e.tensor_tensor(out=H[d].t[:].rearrange("p (h q) -> p h q", q=64), in0=H[d].t[:].rearrange("p (h q) -> p h q", q=64),
                                                               in1=etot.t[:, c, hs].unsqueeze(2).to_broadcast([128, 8, 64]), op=ALU.mult),
                             reads=[H[d], etot], writes=[H[d]])
                    for cx in ys:
                        c, hs, pyo = cx["c"], cx["hs"], cx["pyo"]
                        t_ = cx["t_"] = self.rot(tb, "tb")
                        s.op("dve", lambda e: e.tensor_tensor(out=t_.t[:].rearrange("p (h q) -> p h q", q=64), in0=pyo.t[:].rearrange("p (h q) -> p h q", q=64),
                                                              in1=eac.t[:, c, hs].unsqueeze(2).to_broadcast([128, 8, 64]), op=ALU.mult),
                             reads=[pyo, eac], writes=[t_])
                    for cx in upd:
                        d, pcs = cx["d"], cx["pcs"]
                        s.op("dve", lambda e: e.tensor_tensor(out=H[d].t[:], in0=H[d].t[:], in1=pcs.t[:], op=ALU.add), reads=[H[d], pcs], writes=[H[d]])
                        s.op("act", lambda e: e.activation(out=Hbf[d].t[:], in_=H[d].t[:], func=AF.Copy), reads=[H[d]], writes=[Hbf[d]])
                    for cx in ys:
                        c, t_, py = cx["c"], cx["t_"], cx["py"]
                        if c not in touched:
                            touched.add(c)
                            s.op("dve", lambda e: e.tensor_tensor(out=yacc.t[:, c, :], in0=t_.t[:], in1=py.t[:], op=ALU.add), reads=[t_, py], writes=[yacc])
                        else:
                            s.op("dve", lambda e: e.tensor_tensor(out=t_.t[:], in0=t_.t[:], in1=py.t[:], op=ALU.add), reads=[t_, py], writes=[t_])
                            s.op("pool", lambda e: e.tensor_tensor(out=yacc.t[:, c, :], in0=yacc.t[:, c, :], in1=t_.t[:], op=ALU.add), reads=[t_, yacc], writes=[yacc])

                AHEAD = 2
                inflight = {}
                for oi in range(18 + AHEAD):
                    if oi < 18:
                        inflight[oi] = stage_a([mk(oi, d) for d in range(2)])
                    if oi >= AHEAD:
                        stage_b(inflight.pop(oi - AHEAD))
                nch = 16 if last else 18
                G = 3
                for c0 in range(0, nch, G):
                    grp = list(range(c0, min(nch, c0 + G)))
                    tt_, szs, zz = {}, {}, {}
                    for c in grp:
                        t_ = tt_[c] = self.rot(tb, "tb")
                        s.op("pool", lambda e: e.tensor_tensor(out=t_.t[:], in0=xstm.t[:, c, :], in1=dsk.t[:].rearrange("p h q -> p (h q)"), op=ALU.mult),
                             reads=[xstm, dsk], writes=[t_])
                        zt = zz[c] = self.rot(zts, "zt")
                        s.dma("sp", zt.t[:], dr["ZS"][b, c * 128:(c + 1) * 128, :], reads=[drb["ZS"]], writes=[zt])
                    for c in grp:
                        t_ = tt_[c]
                        s.op("pool", lambda e: e.tensor_tensor(out=t_.t[:], in0=t_.t[:], in1=yacc.t[:, c, :], op=ALU.add), reads=[t_, yacc], writes=[t_])
                        sz = szs[c] = self.rot(szb, "szb")
                        s.op("act", lambda e: e.activation(out=sz.t[:], in_=zz[c].t[:], func=AF.Silu), reads=[zz[c]], writes=[sz])
                    for c in grp:
                        t_, sz = tt_[c], szs[c]
                        s.op("dve", lambda e: e.tensor_tensor(out=t_.t[:], in0=t_.t[:], in1=sz.t[:], op=ALU.mult), reads=[t_, sz], writes=[t_])
                    for gi, c in enumerate(grp):
                        s.op("act", lambda e: e.activation(out=junk.t[:], in_=tt_[c].t[:], func=AF.Square, accum_out=ssq.t[:, gi:gi + 1]),
                             reads=[tt_[c]], writes=[junk, ssq])
                    ng = len(grp)
                    s.op("act", lambda e: e.activation(out=ssq.t[:, 4:4 + ng], in_=ssq.t[:, 0:ng], func=AF.Ln, scale=1.0 / 512, bias=self.cst.t[:, 0:1]),
                         reads=[ssq, self.cst], writes=[ssq])
                    s.op("act", lambda e: e.activation(out=ssq.t[:, 8:8 + ng], in_=ssq.t[:, 4:4 + ng], func=AF.Exp, scale=-0.5), reads=[ssq], writes=[ssq])
                    oo = {}
                    for gi, c in enumerate(grp):
                        o = oo[c] = self.rot(obf, "sobf")
                        s.op("dve", lambda e: e.scalar_tensor_tensor(out=o.t[:], in0=tt_[c].t[:], scalar=ssq.t[:, 8 + gi:9 + gi], in1=nwbc, op0=ALU.mult, op1=ALU.mult),
                             reads=[tt_[c], ssq, self.rowbc], writes=[o])
                    pss = {}
                    for c in grp:
                        o = oo[c]
                        ps = pss[c] = self.psum()
                        pv = bf16view(ps)
                        for ci in range(4):
                            s.op("pe", lambda e: e.transpose(pv[:, ci * 128:(ci + 1) * 128], o.t[:, ci * 128:(ci + 1) * 128], self.ident.t[:]),
                                 reads=[o, self.ident], writes=[ps])
                    for gi, c in enumerate(grp):
                        ps = pss[c]
                        pv = bf16view(ps)
                        eng = "act" if gi % 2 == 0 else "dve"
                        if eng == "act":
                            s.op("act", lambda e: e.activation(out=ymix.t[:, :, c * 128:(c + 1) * 128], in_=pv[:, 0:512].rearrange("p (k t) -> p k t", t=128), func=AF.Copy),
                                 reads=[ps], writes=[ymix])
                        else:
                            s.op("dve", lambda e: e.tensor_copy(out=ymix.t[:, :, c * 128:(c + 1) * 128], in_=pv[:, 0:512].rearrange("p (k t) -> p k t", t=128)),
                                 reads=[ps], writes=[ymix])
                ntok = SEQ if last else T
                s.dma("pool", dr["MIXT"][b, 256:768, 0:ntok].rearrange("(k p) t -> p k t", p=128), ymix.t[:, :, 0:ntok], reads=[ymix], writes=[drb["MIXT"]])

    def phase_FG(self, l, last):
        nc, s, NB, J = self.nc, self.s, self.NB, self.J
        dr, drb = self.dr, self.drb
        with ExitStack() as es:
            wout = self.sb(es, "wout", [128, 8, 8, 128], BF16)
            s.dma("sp", wout.t[:], dr["wout_bf"][l].rearrange("c p k f -> p c k f"), reads=[self.wb("wout_bf", l)], writes=[wout])
            xts = [self.sb(es, "fxt%d" % i, [128, 8, 512], F32) for i in range(2)]
            mixs = [self.sb(es, "fmix%d" % i, [128, 8, 512], BF16) for i in range(2)]
            sq = self.sb(es, "fsq", [128, 8, 512], BF16)
            xm = self.sb(es, "fxm", [128, 8, 512], BF16)
            rstd = self.sb(es, "frstd", [128, 512], F32)
            tmp = [self.sb(es, "ftmp%d" % i, [128, 512], F32) for i in range(3)]
            hT = self.sb(es, "hT", [128, 32, 512], BF16)
            rl = [self.sb(es, "rl%d" % i, [128, 512], BF16) for i in range(3)]
            w1 = [self.sb(es, "w1_%d" % i, [128, 4, 8, 128], BF16) for i in range(2)]
            w2 = [self.sb(es, "w2_%d" % i, [128, 32, 128], BF16) for i in range(2)]
            xsrc = "xT" if l == 0 else "XS"
            tiles = TILES[:4] if last else TILES
            for b in range(NB):
                for ti, (t0, n) in enumerate(tiles):
                    j = NB if ti == 4 else b
                    xt = self.rot(xts, "fxt")
                    s.dma("sp", xt.t[:, :, 0:n], dr[xsrc][b].rearrange("(k p) t -> p k t", p=128)[:, :, t0:t0 + n], reads=[drb[xsrc]], writes=[xt])
                    mx = self.rot(mixs, "fmix")
                    s.dma("sp", mx.t[:, :, 0:n], dr["MIXT"][b].rearrange("(k p) t -> p k t", p=128)[:, :, t0:t0 + n], reads=[drb["MIXT"]], writes=[mx])
                    for fc in range(8):
                        ps = self.psum()
                        for k in range(8):
                            s.op("pe", lambda e: e.matmul(ps.t[:, 0:n], wout.t[:, fc, k, :], mx.t[:, k, 0:n], start=(k == 0), stop=(k == 7)),
                                 reads=[wout, mx], writes=[ps])
                        s.op("dve", lambda e: e.scalar_tensor_tensor(out=xt.t[:, fc, 0:n], in0=ps.t[:, 0:n], scalar=self.mod.t[:, 16 + fc, j:j + 1],
                                                                     in1=xt.t[:, fc, 0:n], op0=ALU.mult, op1=ALU.add),
                             reads=[ps, self.mod, xt], writes=[xt])
                    self.norm_mod(xt, n, self.A2, 24, j, xm, sq, rstd, tmp)
                    for g in range(8):
                        w = self.rot(w1, "w1")
                        s.dma("sp", w.t[:], dr["wff1_bf"][l, g * 4:(g + 1) * 4].rearrange("c p k f -> p c k f"), reads=[self.wb("wff1_bf", l)], writes=[w])
                        for c in range(4):
                            fc = g * 4 + c
                            ps = self.psum()
                            for k in range(8):
                                s.op("pe", lambda e: e.matmul(ps.t[:, 0:n], w.t[:, c, k, :], xm.t[:, k, 0:n], start=(k == 0), stop=(k == 7)),
                                     reads=[w, xm], writes=[ps])
                            r = self.rot(rl, "rl")
                            s.op("act", lambda e: e.activation(out=r.t[:, 0:n], in_=ps.t[:, 0:n], func=AF.Relu), reads=[ps], writes=[r])
                            s.op("pool", lambda e: e.tensor_tensor(out=hT.t[:, fc, 0:n], in0=r.t[:, 0:n], in1=r.t[:, 0:n], op=ALU.mult), reads=[r], writes=[hT])
                    for fc in range(8):
                        w = self.rot(w2, "w2")
                        s.dma("sp", w.t[:], dr["wff2_bf"][l, fc], reads=[self.wb("wff2_bf", l)], writes=[w])
                        ps = self.psum()
                        for k in range(32):
                            s.op("pe", lambda e: e.matmul(ps.t[:, 0:n], w.t[:, k, :], hT.t[:, k, 0:n], start=(k == 0), stop=(k == 31)),
                                 reads=[w, hT], writes=[ps])
                        s.op("dve", lambda e: e.scalar_tensor_tensor(out=xt.t[:, fc, 0:n], in0=ps.t[:, 0:n], scalar=self.mod.t[:, 40 + fc, j:j + 1],
                                                                     in1=xt.t[:, fc, 0:n], op0=ALU.mult, op1=ALU.add),
                             reads=[ps, self.mod, xt], writes=[xt])
                    if last:
                        s.dma("pool", dr["out"][b].rearrange("(k p) t -> p k t", p=128)[:, :, t0:t0 + n], xt.t[:, :, 0:n], reads=[xt], writes=[drb["out"]])
                    else:
                        s.dma("pool", dr["XS"][b].rearrange("(k p) t -> p k t", p=128)[:, :, t0:t0 + n], xt.t[:, :, 0:n], reads=[xt], writes=[drb["XS"]])


_CACHE = {}


def _get_prog(NB, L, dbg=()):
    key = (NB, L, tuple(sorted(dbg)))
    if key not in _CACHE:
        p = Prog(NB, L, dbg)
        p.build()
        _CACHE[key] = p
    return _CACHE[key]


def run(inputs, ncores=NCORES, L=None, dbg=(), trace=False):
    x = np.asarray(inputs["x"], np.float32)
    ctx = np.asarray(inputs["ctx"], np.float32)
    c = np.asarray(inputs["c"], np.float32)
    c_ctx = np.asarray(inputs["c_ctx"], np.float32)
    B = x.shape[0]
    NB = B // ncores
    Lw = inputs["w_ada"].shape[0]
    L = Lw if L is None else L
    w = _prep_weights({k: (np.asarray(v)[:L] if np.asarray(v).ndim >= 1 and np.asarray(v).shape[0] == Lw and k not in ("x", "c", "ctx", "c_ctx") else v)
                       for k, v in inputs.items()})
    cst = _consts()
    prog = _get_prog(NB, L, dbg)
    in_maps = []
    for i in range(ncores):
        sl = slice(i * NB, (i + 1) * NB)
        xT = np.concatenate([x[sl].transpose(0, 2, 1), ctx[sl].transpose(0, 2, 1)], axis=2)
        cc = np.concatenate([c[sl], c_ctx[None]], axis=0)
        cT = np.ascontiguousarray(cc.reshape(NB + 1, 8, 128).transpose(2, 1, 0))
        m = {"xT": np.ascontiguousarray(xT), "cT": cT}
        m.update(w)
        m.update(cst)
        in_maps.append(m)
    res = run_bass_kernel_spmd(prog.nc, in_maps, core_ids=list(range(ncores)), **({"trace": True} if trace else {}))
    out = np.concatenate([r["out"].transpose(0, 2, 1) for r in res.results], axis=0)
    return np.ascontiguousarray(out), res


def kernel(**inputs):
    out, _ = run(inputs)
    return out.astype(np.float32)
```

```python
import numpy as np
from contextlib import ExitStack
import concourse.bass as bass
import concourse.mybir as mybir
from concourse.bass_utils import run_bass_kernel_spmd

F32 = mybir.dt.float32
BF16 = mybir.dt.bfloat16
AF = mybir.ActivationFunctionType
ALU = mybir.AluOpType

D = 1024
SEQ = 2048
CTX = 256
T = SEQ + CTX
DFF = 4096
EPS = 1e-6
NEG = -30000.0
NCORES = 8


class Buf:
    __slots__ = ("w", "r")

    def __init__(self):
        self.w = None
        self.r = []


class TT:
    __slots__ = ("t", "b", "ps")

    def __init__(self, t, ps=False):
        self.t = t
        self.b = Buf()
        self.ps = ps


class Sched:
    ND = 40

    def __init__(self, nc, es):
        self.nc = nc
        self.E = dict(pe=nc.tensor, dve=nc.vector, act=nc.scalar, pool=nc.gpsimd, sp=nc.sync)
        self.csem = {e: es.enter_context(nc.semaphore("c_" + e)) for e in ("pe", "dve", "act", "pool")}
        self.ccnt = {e: 0 for e in self.csem}
        self.dsems = [es.enter_context(nc.semaphore("d%d" % i)) for i in range(self.ND)]
        self.dcnt = [0] * self.ND
        self.dnext = 0
        self.dnext_pool = 0
        self.waited = {e: {} for e in self.E}
        self.n_inst = 0

    def _wait(self, eng, ev):
        key, sem, val = ev[1], ev[2], ev[3]
        if self.waited[eng].get(key, 0) >= val:
            return
        self.E[eng].wait_ge(sem, val)
        self.waited[eng][key] = val

    def _deps(self, eng, reads, writes):
        for b in reads:
            if b.w is not None and not (b.w[0] == eng == "pe"):
                self._wait(eng, b.w)
        for b in writes:
            if b.w is not None and not (b.w[0] == eng == "pe"):
                self._wait(eng, b.w)
            for ev in b.r:
                if not (ev[0] == eng == "pe"):
                    self._wait(eng, ev)

    def _update(self, ev, reads, writes):
        for b in reads:
            b.r = [e for e in b.r if e[1] != ev[1]] + [ev]
        for b in writes:
            b.w = ev
            b.r = []

    def op(self, eng, fn, reads=(), writes=()):
        writes = list(writes) + [x for x in reads if isinstance(x, TT) and x.ps]
        reads = [x.b if isinstance(x, TT) else x for x in reads if not (isinstance(x, TT) and x.ps)]
        writes = [x.b if isinstance(x, TT) else x for x in writes]
        self._deps(eng, reads, writes)
        inst = fn(self.E[eng])
        self.ccnt[eng] += 1
        inst.then_inc(self.csem[eng], 1)
        ev = (eng, "c_" + eng, self.csem[eng], self.ccnt[eng])
        self._update(ev, reads, writes)
        self.n_inst += 1

    def dma(self, q, out, in_, reads=(), writes=()):
        reads = [x.b if isinstance(x, TT) else x for x in reads]
        writes = [x.b if isinstance(x, TT) else x for x in writes]
        half = self.ND // 2
        if q == "pool":
            i = half + self.dnext_pool
            self.dnext_pool = (self.dnext_pool + 1) % half
        else:
            i = self.dnext
            self.dnext = (self.dnext + 1) % half
        key = "d%d" % i
        if self.dcnt[i] > 0:
            self._wait(q, ("dma", key, self.dsems[i], self.dcnt[i]))
        self._deps(q, reads, writes)
        inst = self.E[q].dma_start(out=out, in_=in_)
        self.dcnt[i] += 16
        inst.then_inc(self.dsems[i], 16)
        ev = ("dma", key, self.dsems[i], self.dcnt[i])
        self._update(ev, reads, writes)
        self.n_inst += 1

    def barrier(self):
        evs = [(e, "c_" + e, self.csem[e], self.ccnt[e]) for e in self.csem if self.ccnt[e] > 0]
        evs += [("dma", "d%d" % i, self.dsems[i], self.dcnt[i]) for i in range(self.ND) if self.dcnt[i] > 0]
        for eng in self.E:
            for ev in evs:
                if ev[0] != eng:
                    self._wait(eng, ev)


def _lhsT_layout(W, cols_list, mc=128):
    K = W.shape[0]
    nk = K // 128
    out = np.zeros((len(cols_list), 128, nk, mc), np.float32)
    Wr = W.reshape(nk, 128, W.shape[1])
    for i, cols in enumerate(cols_list):
        cols = np.asarray(cols)
        out[i, :, :, :len(cols)] = Wr[:, :, cols].transpose(1, 0, 2)
    return out


def _na_bias_tables(rpb):
    H = 4
    kc = np.arange(64)
    qc = np.arange(64)
    col_start = np.clip(qc - 8, 0, 48)
    col_in = (kc[:, None] >= col_start[None, :]) & (kc[:, None] < col_start[None, :] + 16)
    col_idx = np.clip(kc[:, None] - qc[None, :], -15, 15) + 15
    tiles = []

    def tile_for(qr0, kr0):
        tl = np.full((H, 2, 64, 8, 64), NEG, np.float32)
        for p in range(2):
            kr = kr0 + p
            for j in range(8):
                qr = qr0 + j
                rs = min(max(qr - 4, 0), 24)
                if rs <= kr < rs + 8:
                    ridx = kr - qr + 7
                    blk = rpb[:, ridx][:, col_idx]
                    blk = np.where(col_in[None], blk, np.float32(NEG))
                    tl[:, p, :, j, :] = blk
        return tl.reshape(H, 128, 512)

    for c in range(6):
        tiles.append(tile_for(0, 2 * c))
    for c in range(8):
        tiles.append(tile_for(8, 4 + 2 * c))
    for c in range(6):
        tiles.append(tile_for(24, 20 + 2 * c))
    return np.stack(tiles, axis=1)


NA_GROUPS = [
    (0, 6, 0), (256, 8, 6), (768, 8, 6), (1280, 6, 14)]


def _consts():
    c = {}
    c["ident"] = np.eye(128, dtype=np.float32)
    c["ones"] = np.ones((128, 128), np.float32)
    b = np.zeros((128, 128), np.float32)
    b[:64, :64] = 1
    b[64:, 64:] = 1
    c["blk64"] = b
    pos = np.arange(SEQ)
    axes = np.stack([pos // 64, pos % 64], -1).astype(np.float32)
    inv = (10000.0 ** (-np.arange(8, dtype=np.float32) / 8)).astype(np.float32)
    ang = axes[:, :, None] * inv
    cos = np.cos(ang).astype(np.float32)
    sin = np.sin(ang).astype(np.float32)
    cosT = np.ones((128, T), np.float32)
    sinT = np.zeros((128, T), np.float32)
    for ax in range(2):
        base = 64 + ax * 16
        cosT[base:base + 8, :SEQ] = cos[:, ax].T
        cosT[base + 8:base + 16, :SEQ] = cos[:, ax].T
        sinT[base:base + 8, :SEQ] = sin[:, ax].T
        sinT[base + 8:base + 16, :SEQ] = sin[:, ax].T
    c["cosT"] = cosT
    c["sinT"] = sinT
    P = np.zeros((128, 128), np.float32)
    for ax in range(2):
        base = 64 + ax * 16
        for i in range(8):
            P[base + i, base + 8 + i] = -1.0
            P[base + 8 + i, base + i] = 1.0
    c["prot"] = np.ascontiguousarray(P.T)
    k = np.arange(128)
    c["tri_f"] = (k[:, None] <= k[None, :]).astype(np.float32)
    c["tri_b"] = (k[:, None] >= k[None, :]).astype(np.float32)
    c["negm_f"] = np.where(k[:, None] <= k[None, :], 0.0, NEG).astype(np.float32)
    c["negm_b"] = np.where(k[:, None] >= k[None, :], 0.0, NEG).astype(np.float32)
    sel = np.zeros((128, 64), np.float32)
    sel[64, :] = 1.0
    c["sel64"] = sel
    return c


def _prep_weights(inp):
    L = inp["w_ada"].shape[0]
    f32 = np.float32
    w = {}
    wa = np.asarray(inp["w_ada"], f32)
    w["wada"] = np.ascontiguousarray(wa.reshape(L, 8, 128, 48, 128).transpose(0, 3, 2, 1, 4))
    w["bada"] = np.ascontiguousarray(np.asarray(inp["b_ada"], f32).reshape(L, 48, 128).transpose(0, 2, 1))
    w["nw1"] = np.ascontiguousarray(np.asarray(inp["norm1_w"], f32).reshape(L, 8, 128).transpose(0, 2, 1))
    w["nw2"] = np.ascontiguousarray(np.asarray(inp["norm2_w"], f32).reshape(L, 8, 128).transpose(0, 2, 1))
    win = np.asarray(inp["w_in"], f32)
    fm_cols = [np.arange(0, 128), np.arange(128, 256), np.arange(256, 384), np.arange(384, 512)]
    fm_cols += [np.arange(1280 + 128 * i, 1280 + 128 * (i + 1)) for i in range(8)]
    fm_cols += [np.arange(2320, 2448), np.arange(2448, 2576), np.arange(2576, 2704)]
    winfm = np.zeros((L, 16, 128, 8, 128), f32)
    for l in range(L):
        winfm[l, :15] = _lhsT_layout(win[l], fm_cols)
        kr = win[l][:, 2704:2736].reshape(8, 128, 32).transpose(1, 0, 2)
        winfm[l, 15, :, :, 64:96] = kr
    w["winfm"] = winfm
    w["winz"] = np.ascontiguousarray(win[:, :, 768:1280].reshape(L, 8, 128, 512).transpose(0, 2, 1, 3))
    vdt = np.zeros((L, 128, 8, 272), f32)
    vdt[..., :256] = win[:, :, 512:768].reshape(L, 8, 128, 256).transpose(0, 2, 1, 3)
    vdt[..., 256:272] = win[:, :, 2304:2320].reshape(L, 8, 128, 16).transpose(0, 2, 1, 3)
    w["winvdt"] = vdt
    wuq = np.asarray(inp["mla_w_uq"], f32)
    w["wuq"] = np.ascontiguousarray(wuq.reshape(L, 2, 128, 4, 96).transpose(0, 2, 3, 1, 4))
    wukv = np.asarray(inp["mla_w_ukv"], f32).reshape(L, 128, 4, 128)
    wk = np.zeros((L, 128, 4, 96), f32)
    wk[..., :64] = wukv[..., :64]
    w["wukvk"] = wk
    w["wukvv"] = np.ascontiguousarray(wukv[..., 64:].reshape(L, 128, 256))
    wo = np.asarray(inp["w_out"], f32)
    w["wout"] = np.ascontiguousarray(wo.reshape(L, 8, 128, 8, 128).transpose(0, 3, 2, 1, 4))
    w1 = np.asarray(inp["w_ff1"], f32)
    w["wff1"] = np.ascontiguousarray(w1.reshape(L, 8, 128, 32, 128).transpose(0, 3, 2, 1, 4))
    w2 = np.asarray(inp["w_ff2"], f32)
    w["wff2"] = np.ascontiguousarray(w2.reshape(L, 32, 128, 8, 128).transpose(0, 3, 2, 1, 4))
    vec = np.zeros((L, 128, 32), f32)
    vec[:, :, 0] = np.tile(np.asarray(inp["na_qn_w"], f32), (1, 2))
    vec[:, :, 1] = np.tile(np.asarray(inp["na_kn_w"], f32), (1, 2))
    vec[:, :, 2:4] = np.asarray(inp["mla_cq_norm_w"], f32).reshape(L, 2, 128).transpose(0, 2, 1)
    vec[:, :, 4] = np.asarray(inp["mla_ckv_norm_w"], f32)
    vec[:, :96, 5] = np.asarray(inp["mla_qn_w"], f32)
    vec[:, :96, 6] = np.asarray(inp["mla_kn_w"], f32)
    vec[:, :, 8:16] = np.asarray(inp["ssd_conv_b"], f32).reshape(L, 8, 128).transpose(0, 2, 1)
    w["vec"] = vec
    w["convw"] = np.ascontiguousarray(np.asarray(inp["ssd_conv_w"], f32).reshape(L, 5, 8, 128).transpose(0, 3, 2, 1))
    row = np.zeros((L, 552), f32)
    row[:, 0:16] = np.asarray(inp["ssd_dt_bias"], f32).reshape(L, 16)
    row[:, 16:32] = np.asarray(inp["ssd_a_log"], f32).reshape(L, 16)
    row[:, 32:40] = np.asarray(inp["ssd_d"], f32)
    row[:, 40:552] = np.asarray(inp["ssd_norm_w"], f32)
    w["row"] = row
    rpb = np.asarray(inp["na_rpb"], f32)
    w["nab"] = np.stack([_na_bias_tables(rpb[l]) for l in range(L)], 0)
    return w


BF_W = ["winfm", "winz", "winvdt", "wuq", "wukvk", "wukvv", "wout", "wff1", "wff2", "nab"]


TILES = [(0, 512), (512, 512), (1024, 512), (1536, 512), (2048, 256)]


class Prog:
    def __init__(self, NB, L, dbg=()):
        self.NB, self.L, self.J = NB, L, NB + 1
        self.dbg = set(dbg)
        self.nc = bass.Bass("TRN2", target_bir_lowering=False)
        self.es = ExitStack()
        self.s = Sched(self.nc, self.es)
        self.dr = {}
        self.drb = {}

    def din(self, name, shape, dt=F32):
        self.dr[name] = self.nc.dram_tensor(name, list(shape), dt, kind="ExternalInput").ap()
        self.drb[name] = Buf()

    def dscr(self, name, shape, dt):
        kind = "ExternalOutput" if name in self.dbg else "Internal"
        self.dr[name] = self.nc.dram_tensor(name, list(shape), dt, kind=kind).ap()
        self.drb[name] = Buf()

    def sb(self, es, name, shape, dt):
        self.uid = getattr(self, "uid", 0) + 1
        return TT(es.enter_context(self.nc.sbuf_tensor("sb%d_%s" % (self.uid, name), list(shape), dt)))

    def wb(self, name, l):
        return self.drb.setdefault((name, l), Buf())

    def psum(self):
        p = self.ps[self.psi % 7]
        self.psi += 1
        return p

    def warm(self, n=1, cols=512):
        for _ in range(n):
            self.s.op("pe", lambda e: e.matmul(self.ps[7].t[:, 0:cols], self.ident.t[:], self.wrhs.t[:, 0:cols], start=True, stop=True))

    def rot(self, lst, key):
        i = self.rr.get(key, 0)
        self.rr[key] = i + 1
        return lst[i % len(lst)]

    def build(self):
        nc, s, es, NB, L, J = self.nc, self.s, self.es, self.NB, self.L, self.J
        self.rr = {}
        self.din("xT", [NB, D, T])
        self.din("cT", [128, 8, J])
        wshapes = dict(wada=[48, 128, 8, 128], bada=[128, 48], nw1=[128, 8], nw2=[128, 8],
                       winfm=[16, 128, 8, 128], winz=[128, 8, 512], winvdt=[128, 8, 272],
                       wuq=[128, 4, 2, 96], wukvk=[128, 4, 96], wukvv=[128, 256],
                       wout=[8, 128, 8, 128], wff1=[32, 128, 8, 128], wff2=[8, 128, 32, 128],
                       vec=[128, 32], convw=[128, 8, 5], row=[552], nab=[4, 20, 128, 512])
        self.wshapes = wshapes
        for k, shp in wshapes.items():
            self.din(k, [L] + shp)
        cshapes = dict(ident=[128, 128], ones=[128, 128], blk64=[128, 128], cosT=[128, T], sinT=[128, T],
                       prot=[128, 128], tri_f=[128, 128], tri_b=[128, 128], negm_f=[128, 128], negm_b=[128, 128],
                       sel64=[128, 64])
        for k, shp in cshapes.items():
            self.din(k, shp)
        self.dr["out"] = nc.dram_tensor("out", [NB, D, SEQ], F32, kind="ExternalOutput").ap()
        self.drb["out"] = Buf()
        for k in BF_W:
            self.dscr(k + "_bf", [L] + wshapes[k], BF16)
        self.dscr("XS", [NB, D, T], F32)
        self.dscr("QKNA", [NB, 512, T], BF16)
        self.dscr("VNA", [NB, T, 256], BF16)
        self.dscr("ZS", [NB, T, 512], BF16)
        self.dscr("XBC", [NB, D, T], BF16)
        self.dscr("DTS", [NB, T, 16], F32)
        self.dscr("QM", [NB, 4, 96, T], BF16)
        self.dscr("KM", [NB, 4, 96, T], BF16)
        self.dscr("VM", [NB, T, 256], BF16)
        self.dscr("MIXT", [NB, D, T], BF16)
        dr, drb = self.dr, self.drb

        pairs = [es.enter_context(nc.psum_tensor("pp%d" % i, [128, 1024], F32)) for i in range(4)]
        self.ps = [TT(pairs[i // 2][:, (i % 2) * 512:(i % 2 + 1) * 512], ps=True) for i in range(8)]
        self.pp = [TT(pairs[i][:, :], ps=True) for i in range(4)]
        self.psi = 0

        def convert(l):
            for k in BF_W:
                shp = wshapes[k]
                src, dst = dr[k][l], dr[k + "_bf"][l]
                if len(shp) == 4 and (shp[1] == 128 or k == "nab"):
                    step = max(1, (1 << 20) // (shp[1] * shp[2] * shp[3]))
                    if k == "nab":
                        for h in range(4):
                            for i in range(0, shp[1], 5):
                                s.dma("pool", dst[h, i:i + 5], src[h, i:i + 5], writes=[self.wb(k + "_bf", l)])
                    else:
                        for i in range(0, shp[0], step):
                            s.dma("pool", dst[i:i + step], src[i:i + step], writes=[self.wb(k + "_bf", l)])
                else:
                    s.dma("pool", dst, src, writes=[self.wb(k + "_bf", l)])
        self.convert = convert
        convert(0)

        def cload(name, dt, rows=128):
            t = self.sb(es, "c_" + name, cshapes[name], dt)
            s.dma("pool", t.t[:], dr[name], writes=[t])
            return t
        self.ident = cload("ident", BF16)
        self.ones = cload("ones", BF16)
        self.blk64 = cload("blk64", BF16)
        self.prot = cload("prot", BF16)
        self.onesf = cload("ones", F32) if False else None
        self.tri = [cload("tri_f", F32), cload("tri_b", F32)]
        self.negmb = [cload("negm_f", BF16), cload("negm_b", BF16)]
        self.sel64 = cload("sel64", F32)
        self.onesf = self.sb(es, "onesf", [128, 128], F32)
        s.dma("sp", self.onesf.t[:], dr["ones"], writes=[self.onesf])
        self.wrhs = self.sb(es, "wrhs", [128, 512], BF16)
        s.op("dve", lambda e: e.memset(self.wrhs.t[:], 1.0), writes=[self.wrhs])
        self.cst = self.sb(es, "cst", [128, 8], F32)
        import math
        for i, v in enumerate([EPS, math.log(0.125), math.log(96 ** -0.5), 1.0, 0.0]):
            s.op("dve", lambda e: e.memset(self.cst.t[:, i:i + 1], v), writes=[self.cst])
        self.sc = self.sb(es, "silu_c", [128, 8, J], F32)
        s.dma("sp", self.sc.t[:], dr["cT"], writes=[self.sc])
        s.op("act", lambda e: e.activation(out=self.sc.t[:], in_=self.sc.t[:], func=AF.Silu), reads=[self.sc], writes=[self.sc])
        self.mod = self.sb(es, "mod", [128, 48, J], F32)
        self.A1 = self.sb(es, "A1", [128, 8, J], F32)
        self.A2 = self.sb(es, "A2", [128, 8, J], F32)
        self.vec = self.sb(es, "vec", [128, 32], F32)
        self.nw = self.sb(es, "nw", [128, 16], F32)
        self.bada = self.sb(es, "bada", [128, 48], F32)
        self.rowbc = self.sb(es, "rowbc", [128, 552], F32)
        self.convw = self.sb(es, "convw", [128, 8, 5], F32)
        s.barrier()

        for l in range(L):
            last = (l == L - 1)
            import os
            stop = int(os.environ.get("KSTOP", "9"))
            if stop >= 1:
                self.phase_A(l)
                s.barrier()
            if l + 1 < L:
                self.convert(l + 1)
            if stop >= 2:
                self.phase_B(l)
                s.barrier()
            if stop >= 3:
                self.phase_att(l, last)
                s.barrier()
            if stop >= 4:
                self.phase_ssd(l, last)
                s.barrier()
            if stop >= 5:
                self.phase_FG(l, last)
                s.barrier()
        s.barrier()
        self.es.close()
        return nc

    def phase_A(self, l):
        nc, s, J = self.nc, self.s, self.J
        dr, drb = self.dr, self.drb
        with ExitStack() as es:
            wts = [self.sb(es, "wada%d" % i, [128, 6, 8, 128], F32) for i in range(2)]
            s.dma("sp", self.vec.t[:], dr["vec"][l], writes=[self.vec])
            s.dma("sp", self.nw.t[:, 0:8], dr["nw1"][l], writes=[self.nw])
            s.dma("sp", self.nw.t[:, 8:16], dr["nw2"][l], writes=[self.nw])
            s.dma("sp", self.bada.t[:], dr["bada"][l], writes=[self.bada])
            s.dma("sp", self.rowbc.t[:], dr["row"][l].partition_broadcast(128), writes=[self.rowbc])
            s.dma("sp", self.convw.t[:], dr["convw"][l], writes=[self.convw])
            ps = self.psum()
            for g in range(8):
                wt = wts[g % 2]
                s.dma("sp", wt.t[:], dr["wada"][l, g * 6:(g + 1) * 6].rearrange("c p k f -> p c k f"), writes=[wt])
                for c in range(6):
                    fc = g * 6 + c
                    for k in range(8):
                        s.op("pe", lambda e: e.matmul(ps.t[:, fc * J:(fc + 1) * J], wt.t[:, c, k, :], self.sc.t[:, k, :],
                                                      start=(k == 0), stop=(k == 7)), reads=[wt, self.sc], writes=[ps])
            s.op("dve", lambda e: e.tensor_tensor(out=self.mod.t[:], in0=ps.t[:, 0:48 * J].rearrange("p (c j) -> p c j", j=J),
                                                  in1=self.bada.t[:].unsqueeze(2).to_broadcast([128, 48, J]), op=ALU.add),
                 reads=[ps, self.bada], writes=[self.mod])
            for (A, off, nwo) in ((self.A1, 8, 0), (self.A2, 32, 8)):
                s.op("dve", lambda e: e.scalar_tensor_tensor(out=A.t[:], in0=self.mod.t[:, off:off + 8, :], scalar=1.0,
                                                             in1=self.nw.t[:, nwo:nwo + 8].unsqueeze(2).to_broadcast([128, 8, J]),
                                                             op0=ALU.add, op1=ALU.mult),
                     reads=[self.mod, self.nw], writes=[A])

    def norm_mod(self, xt, n, A, sh_off, j, xm, sq, rstd, tmp):
        s = self.s
        s.op("act", lambda e: e.activation(out=sq.t[:, :, 0:n], in_=xt.t[:, :, 0:n], func=AF.Square), reads=[xt], writes=[sq])
        ps = self.psum()
        for k in range(8):
            s.op("pe", lambda e: e.matmul(ps.t[:, 0:n], self.ones.t[:], sq.t[:, k, 0:n], start=(k == 0), stop=(k == 7)),
                 reads=[sq, self.ones], writes=[ps])
        s.op("act", lambda e: e.activation(out=rstd.t[:, 0:n], in_=ps.t[:, 0:n], func=AF.Ln, scale=1.0 / D, bias=self.cst.t[:, 0:1]),
             reads=[ps, self.cst], writes=[rstd])
        s.op("act", lambda e: e.activation(out=rstd.t[:, 0:n], in_=rstd.t[:, 0:n], func=AF.Exp, scale=-0.5), reads=[rstd], writes=[rstd])
        for k in range(8):
            tm = self.rot(tmp, "nm_tmp")
            s.op("dve", lambda e: e.scalar_tensor_tensor(out=tm.t[:, 0:n], in0=xt.t[:, k, 0:n], scalar=A.t[:, k, j:j + 1],
                                                         in1=rstd.t[:, 0:n], op0=ALU.mult, op1=ALU.mult),
                 reads=[xt, A, rstd], writes=[tm])
            s.op("act", lambda e: e.activation(out=xm.t[:, k, 0:n], in_=tm.t[:, 0:n], func=AF.Identity,
                                               bias=self.mod.t[:, sh_off + k, j:j + 1]),
                 reads=[tm, self.mod], writes=[xm])

    def fm_rstd(self, sqs, ones_ap, ones_tt, P, n, cnt, rstd, lnb=None):
        s = self.s
        ps = self.psum()
        for i, (tt, ap) in enumerate(sqs):
            s.op("pe", lambda e: e.matmul(ps.t[0:P, 0:n], ones_ap, ap, start=(i == 0), stop=(i == len(sqs) - 1)),
                 reads=[tt, ones_tt], writes=[ps])
        s.op("act", lambda e: e.activation(out=rstd.t[0:P, 0:n], in_=ps.t[0:P, 0:n], func=AF.Ln, scale=1.0 / cnt, bias=self.cst.t[0:P, 0:1]),
             reads=[ps, self.cst], writes=[rstd])
        if lnb is None:
            s.op("act", lambda e: e.activation(out=rstd.t[0:P, 0:n], in_=rstd.t[0:P, 0:n], func=AF.Exp, scale=-0.5),
                 reads=[rstd], writes=[rstd])
        else:
            s.op("act", lambda e: e.activation(out=rstd.t[0:P, 0:n], in_=rstd.t[0:P, 0:n], func=AF.Exp, scale=-0.5,
                                               bias=self.cst.t[0:P, lnb:lnb + 1]),
                 reads=[rstd, self.cst], writes=[rstd])

    def phase_B(self, l):
        nc, s, J, NB = self.nc, self.s, self.J, self.NB
        dr, drb = self.dr, self.drb
        with ExitStack() as es:
            winfm = self.sb(es, "winfm", [128, 16, 8, 128], BF16)
            winz = self.sb(es, "winz", [128, 8, 512], BF16)
            winvdt = self.sb(es, "winvdt", [128, 8, 272], BF16)
            wuq = self.sb(es, "wuq", [128, 4, 2, 96], BF16)
            wukvk = self.sb(es, "wukvk", [128, 4, 96], BF16)
            wukvv = self.sb(es, "wukvv", [128, 256], BF16)
            cosb = [self.sb(es, "cosb%d" % i, [128, 512], F32) for i in range(2)]
            sinb = [self.sb(es, "sinb%d" % i, [128, 512], F32) for i in range(2)]
            for i in range(0, 16, 4):
                s.dma("sp", winfm.t[:, i:i + 4], dr["winfm_bf"][l, i:i + 4].rearrange("c p k f -> p c k f"),
                      reads=[self.wb("winfm_bf", l)], writes=[winfm])
            for (tt, nm) in ((winz, "winz_bf"), (winvdt, "winvdt_bf"), (wuq, "wuq_bf"), (wukvk, "wukvk_bf"), (wukvv, "wukvv_bf")):
                s.dma("sp", tt.t[:], dr[nm][l], reads=[self.wb(nm, l)], writes=[tt])
            xts = [self.sb(es, "xt%d" % i, [128, 8, 512], F32) for i in range(2)]
            sq = self.sb(es, "sq", [128, 8, 512], BF16)
            xm = self.sb(es, "xm", [128, 8, 512], BF16)
            rstd = self.sb(es, "rstd", [128, 512], F32)
            tmp = [self.sb(es, "tmp%d" % i, [128, 512], F32) for i in range(8)]
            ubuf = [self.sb(es, "u%d" % i, [128, 512], F32) for i in range(5)]
            sqb = [self.sb(es, "sqb%d" % i, [128, 512], BF16) for i in range(5)]
            rs2 = [self.sb(es, "rs2_%d" % i, [128, 512], F32) for i in range(4)]
            obf = [self.sb(es, "obf%d" % i, [128, 512], BF16) for i in range(8)]
            cqn = self.sb(es, "cqn", [128, 2, 512], BF16)
            ckvn = self.sb(es, "ckvn", [128, 512], BF16)
            krp = self.sb(es, "krp", [128, 512], F32)
            vst = [self.sb(es, "vst%d" % i, [128, 4, 256], BF16) for i in range(2)]
            dst_ = [self.sb(es, "dst%d" % i, [128, 4, 16], F32) for i in range(2)]
            zst = [self.sb(es, "zst%d" % i, [128, 4, 512], BF16) for i in range(2)]
            vmst = [self.sb(es, "vmst%d" % i, [128, 4, 256], BF16) for i in range(2)]
            xsrc = "xT" if l == 0 else "XS"

            def head_group_gen(items, n, t0, cs_t):
                for it in items:
                    it["ps"] = self.psum()
                    it["mm"](it["ps"])
                yield
                for it in items:
                    P = it["P"]
                    u = it["u"] = self.rot(ubuf, "u")
                    q = it["q"] = self.rot(sqb, "sqb")
                    src_ps = it["ps"]
                    if it["extra"] is None:
                        s.op("act", lambda e: e.activation(out=u.t[0:P, 0:n], in_=src_ps.t[0:P, 0:n], func=AF.Copy), reads=[src_ps], writes=[u])
                    else:
                        s.op("dve", lambda e: e.tensor_tensor(out=u.t[0:P, 0:n], in0=src_ps.t[0:P, 0:n], in1=it["extra"].t[0:P, 0:n], op=ALU.add),
                             reads=[src_ps, it["extra"]], writes=[u])
                yield
                for it in items:
                    P, u, q = it["P"], it["u"], it["q"]
                    s.op("act", lambda e: e.activation(out=q.t[0:P, 0:n], in_=u.t[0:P, 0:n], func=AF.Square), reads=[u], writes=[q])
                yield
                for it in items:
                    P, q = it["P"], it["q"]
                    ps = it["ps2"] = self.psum()
                    if P == 128:
                        s.op("pe", lambda e: e.matmul(ps.t[:, 0:n], self.blk64.t[:], q.t[:, 0:n], start=True, stop=True), reads=[q, self.blk64], writes=[ps])
                    else:
                        s.op("pe", lambda e: e.matmul(ps.t[0:P, 0:n], self.ones.t[0:P, 0:P], q.t[0:P, 0:n], start=True, stop=True), reads=[q, self.ones], writes=[ps])
                yield
                for it in items:
                    P, ps = it["P"], it["ps2"]
                    r = it["r"] = self.rot(rs2, "rs2")
                    cnt = 64 if P == 128 else P
                    s.op("act", lambda e: e.activation(out=r.t[0:P, 0:n], in_=ps.t[0:P, 0:n], func=AF.Ln, scale=1.0 / cnt, bias=self.cst.t[0:P, 0:1]),
                         reads=[ps, self.cst], writes=[r])
                yield
                for it in items:
                    P, r, lnb = it["P"], it["r"], it["lnb"]
                    if lnb is None:
                        s.op("act", lambda e: e.activation(out=r.t[0:P, 0:n], in_=r.t[0:P, 0:n], func=AF.Exp, scale=-0.5), reads=[r], writes=[r])
                    else:
                        s.op("act", lambda e: e.activation(out=r.t[0:P, 0:n], in_=r.t[0:P, 0:n], func=AF.Exp, scale=-0.5, bias=self.cst.t[0:P, lnb:lnb + 1]),
                             reads=[r, self.cst], writes=[r])
                yield
                for it in items:
                    P, u, r, wcol = it["P"], it["u"], it["r"], it["wcol"]
                    o = it["o"] = self.rot(obf, "obf")
                    s.op("dve", lambda e: e.scalar_tensor_tensor(out=o.t[0:P, 0:n], in0=u.t[0:P, 0:n], scalar=self.vec.t[0:P, wcol:wcol + 1],
                                                                 in1=r.t[0:P, 0:n], op0=ALU.mult, op1=ALU.mult),
                         reads=[u, self.vec, r], writes=[o])
                if items[0]["P"] == 96:
                    P = 96
                    cT_, sT_ = cs_t
                    yield
                    for it in items:
                        o = it["o"]
                        pr = it["pr"] = self.psum()
                        s.op("pe", lambda e: e.matmul(pr.t[0:P, 0:n], self.prot.t[0:P, 0:P], o.t[0:P, 0:n], start=True, stop=True),
                             reads=[o, self.prot], writes=[pr])
                    yield
                    for it in items:
                        o, pr = it["o"], it["pr"]
                        t1 = it["t1"] = self.rot(tmp, "nm_tmp")
                        t2 = it["t2"] = self.rot(tmp, "nm_tmp")
                        s.op("dve", lambda e: e.tensor_tensor(out=t1.t[0:P, 0:n], in0=pr.t[0:P, 0:n], in1=sT_.t[0:P, 0:n], op=ALU.mult),
                             reads=[pr, sT_], writes=[t1])
                        s.op("pool", lambda e: e.tensor_tensor(out=t2.t[0:P, 0:n], in0=o.t[0:P, 0:n], in1=cT_.t[0:P, 0:n], op=ALU.mult),
                             reads=[o, cT_], writes=[t2])
                    yield
                    for it in items:
                        t1, t2 = it["t1"], it["t2"]
                        o2 = self.rot(obf, "obf")
                        s.op("dve", lambda e: e.tensor_tensor(out=o2.t[0:P, 0:n], in0=t1.t[0:P, 0:n], in1=t2.t[0:P, 0:n], op=ALU.add),
                             reads=[t1, t2], writes=[o2])
                        it["o"] = o2
                yield
                for it in items:
                    P, o = it["P"], it["o"]
                    s.dma("pool", it["dst"], o.t[0:P, 0:n], reads=[o], writes=[drb[it["dname"]]])

            def head_group(items, n, t0, cs_t, fillers=()):
                fillers = list(fillers)
                for _ in head_group_gen(items, n, t0, cs_t):
                    if fillers:
                        fillers.pop(0)()
                return fillers

            for b in range(NB):
                for ti, (t0, n) in enumerate(TILES):
                    j = NB if ti == 4 else b
                    xt = self.rot(xts, "xt")
                    s.dma("sp", xt.t[:, :, 0:n], dr[xsrc][b].rearrange("(k p) t -> p k t", p=128)[:, :, t0:t0 + n],
                          reads=[drb[xsrc]], writes=[xt])
                    cT_, sT_ = self.rot(cosb, "cosb"), self.rot(sinb, "sinb")
                    s.dma("sp", cT_.t[:, 0:n], dr["cosT"][:, t0:t0 + n], writes=[cT_])
                    s.dma("sp", sT_.t[:, 0:n], dr["sinT"][:, t0:t0 + n], writes=[sT_])
                    self.norm_mod(xt, n, self.A1, 0, j, xm, sq, rstd, tmp)
                    nt = n // 128

                    def proj_mm(fc, P=128):
                        def f(ps):
                            for k in range(8):
                                s.op("pe", lambda e: e.matmul(ps.t[0:P, 0:n], winfm.t[:, fc, k, 0:P], xm.t[:, k, 0:n], start=(k == 0), stop=(k == 7)),
                                     reads=[winfm, xm], writes=[ps])
                        return f

                    def proj(fc, P=128):
                        ps = self.psum()
                        proj_mm(fc, P)(ps)
                        return ps
                    head_group([dict(mm=proj_mm(fc), P=128, wcol=0 if fc < 2 else 1, lnb=1 if fc < 2 else None,
                                     dst=dr["QKNA"][b, fc * 128:(fc + 1) * 128, t0:t0 + n], dname="QKNA", extra=None) for fc in range(4)], n, t0, None)
                    fillers = []

                    def xbc_block(c):
                        def f():
                            ps = proj(4 + c)
                            o = self.rot(obf, "obf")
                            if c % 2 == 0:
                                s.op("act", lambda e: e.activation(out=o.t[:, 0:n], in_=ps.t[:, 0:n], func=AF.Copy), reads=[ps], writes=[o])
                            else:
                                s.op("dve", lambda e: e.tensor_copy(out=o.t[:, 0:n], in_=ps.t[:, 0:n]), reads=[ps], writes=[o])
                            s.dma("pool", dr["XBC"][b, c * 128:(c + 1) * 128, t0:t0 + n], o.t[:, 0:n], reads=[o], writes=[drb["XBC"]])
                        return f
                    vs, ds, zs = self.rot(vst, "vst"), self.rot(dst_, "dst"), self.rot(zst, "zst")

                    def tm_block(tc):
                        def f():
                            ps = self.psum()
                            for k in range(8):
                                s.op("pe", lambda e: e.matmul(ps.t[:, 0:272], xm.t[:, k, tc * 128:(tc + 1) * 128], winvdt.t[:, k, :], start=(k == 0), stop=(k == 7)),
                                     reads=[winvdt, xm], writes=[ps])
                            s.op("act", lambda e: e.activation(out=vs.t[:, tc, :], in_=ps.t[:, 0:256], func=AF.Copy), reads=[ps], writes=[vs])
                            s.op("act", lambda e: e.activation(out=ds.t[:, tc, :], in_=ps.t[:, 256:272], func=AF.Copy), reads=[ps], writes=[ds])
                            ps2 = self.psum()
                            for k in range(8):
                                s.op("pe", lambda e: e.matmul(ps2.t[:, :], xm.t[:, k, tc * 128:(tc + 1) * 128], winz.t[:, k, :], start=(k == 0), stop=(k == 7)),
                                     reads=[winz, xm], writes=[ps2])
                            s.op("dve", lambda e: e.tensor_copy(out=zs.t[:, tc, :], in_=ps2.t[:, :]), reads=[ps2], writes=[zs])
                        return f
                    fillers = [xbc_block(c) for c in range(8)] + [tm_block(tc) for tc in range(nt)]
                    us = []
                    for c in range(2):
                        ps = proj(12 + c)
                        u = self.rot(ubuf, "u")
                        s.op("act", lambda e: e.activation(out=u.t[:, 0:n], in_=ps.t[:, 0:n], func=AF.Copy), reads=[ps], writes=[u])
                        q = self.rot(sqb, "sqb")
                        s.op("act", lambda e: e.activation(out=q.t[:, 0:n], in_=ps.t[:, 0:n], func=AF.Square), reads=[ps], writes=[q])
                        us.append((u, q))
                    ps = proj(14)
                    ukv = self.rot(ubuf, "u")
                    s.op("act", lambda e: e.activation(out=ukv.t[:, 0:n], in_=ps.t[:, 0:n], func=AF.Copy), reads=[ps], writes=[ukv])
                    qkv = self.rot(sqb, "sqb")
                    s.op("act", lambda e: e.activation(out=qkv.t[:, 0:n], in_=ps.t[:, 0:n], func=AF.Square), reads=[ps], writes=[qkv])
                    ps = proj(15, 96)
                    s.op("dve", lambda e: e.tensor_copy(out=krp.t[0:96, 0:n], in_=ps.t[0:96, 0:n]), reads=[ps], writes=[krp])
                    r = self.rot(rs2, "rs2")
                    self.fm_rstd([(q, q.t[:, 0:n]) for (u, q) in us], self.ones.t[:], self.ones, 128, n, 256, r)
                    r2 = self.rot(rs2, "rs2")
                    self.fm_rstd([(qkv, qkv.t[:, 0:n])], self.ones.t[:], self.ones, 128, n, 128, r2)
                    for c in range(2):
                        s.op("dve", lambda e: e.scalar_tensor_tensor(out=cqn.t[:, c, 0:n], in0=us[c][0].t[:, 0:n], scalar=self.vec.t[:, 2 + c:3 + c],
                                                                     in1=r.t[:, 0:n], op0=ALU.mult, op1=ALU.mult),
                             reads=[us[c][0], self.vec, r], writes=[cqn])
                    s.op("dve", lambda e: e.scalar_tensor_tensor(out=ckvn.t[:, 0:n], in0=ukv.t[:, 0:n], scalar=self.vec.t[:, 4:5],
                                                                 in1=r2.t[:, 0:n], op0=ALU.mult, op1=ALU.mult),
                         reads=[ukv, self.vec, r2], writes=[ckvn])

                    def uq_mm(h):
                        def f(ps):
                            for c in range(2):
                                s.op("pe", lambda e: e.matmul(ps.t[0:96, 0:n], wuq.t[:, h, c, :], cqn.t[:, c, 0:n], start=(c == 0), stop=(c == 1)),
                                     reads=[wuq, cqn], writes=[ps])
                        return f

                    def uk_mm(h):
                        def f(ps):
                            s.op("pe", lambda e: e.matmul(ps.t[0:96, 0:n], wukvk.t[:, h, :], ckvn.t[:, 0:n], start=True, stop=True),
                                 reads=[wukvk, ckvn], writes=[ps])
                        return f
                    fillers = head_group([dict(mm=uq_mm(h), P=96, wcol=5, lnb=2, dst=dr["QM"][b, h, :, t0:t0 + n], dname="QM", extra=None) for h in range(4)],
                                         n, t0, (cT_, sT_), fillers)
                    fillers = head_group([dict(mm=uk_mm(h), P=96, wcol=6, lnb=None, dst=dr["KM"][b, h, :, t0:t0 + n], dname="KM", extra=krp) for h in range(4)],
                                         n, t0, (cT_, sT_), fillers)
                    for f in fillers:
                        f()
                    s.dma("pool", dr["VNA"][b, t0:t0 + n, :].rearrange("(c p) f -> p c f", p=128), vs.t[:, 0:nt, :], reads=[vs], writes=[drb["VNA"]])
                    s.dma("pool", dr["DTS"][b, t0:t0 + n, :].rearrange("(c p) f -> p c f", p=128), ds.t[:, 0:nt, :], reads=[ds], writes=[drb["DTS"]])
                    s.dma("pool", dr["ZS"][b, t0:t0 + n, :].rearrange("(c p) f -> p c f", p=128), zs.t[:, 0:nt, :], reads=[zs], writes=[drb["ZS"]])
                    vm = self.rot(vmst, "vmst")
                    for tc in range(nt):
                        ps = self.psum()
                        s.op("pe", lambda e: e.matmul(ps.t[:, 0:256], ckvn.t[:, tc * 128:(tc + 1) * 128], wukvv.t[:], start=True, stop=True),
                             reads=[wukvv, ckvn], writes=[ps])
                        s.op("act", lambda e: e.activation(out=vm.t[:, tc, :], in_=ps.t[:, 0:256], func=AF.Copy), reads=[ps], writes=[vm])
                    s.dma("pool", dr["VM"][b, t0:t0 + n, :].rearrange("(c p) f -> p c f", p=128), vm.t[:, 0:nt, :], reads=[vm], writes=[drb["VM"]])

    def phase_att(self, l, last):
        nc, s, NB = self.nc, self.s, self.NB
        dr, drb = self.dr, self.drb
        with ExitStack() as es:
            kT = [self.sb(es, "kT%d" % i, [128, T], BF16) for i in range(2)]
            va = [self.sb(es, "va%d" % i, [128, 18, 128], BF16) for i in range(2)]
            qT = [self.sb(es, "qT%d" % i, [128, 512], BF16) for i in range(3)]
            bias = [self.sb(es, "bias%d" % i, [128, 8, 512], BF16) for i in range(2)]
            pT = [self.sb(es, "pT%d" % i, [128, 2, 512], BF16) for i in range(3)]
            sbias = [self.sb(es, "sbias%d" % i, [128, 2, 512], F32) for i in range(2)]
            osb = [self.sb(es, "osb%d" % i, [128, 512], F32) for i in range(2)]
            rrow = [self.sb(es, "rrow%d" % i, [128, 512], F32) for i in range(2)]
            omix = [self.sb(es, "omix%d" % i, [64, 512], BF16) for i in range(2)]
            for i in range(2):
                s.op("dve", lambda e: e.memset(va[i].t[:, :, 64:128], 1.0), writes=[va[i]])
            accs = [self.ps[4], self.ps[5]]
            wp = self.pp[0:2]
            pbank = [self.ps[6]]
            self.warm(24)
            self.att_tail = []

            def attend(b, d, k_t, v_t, q_src_ap, nq, chunks, bias_t, mix_rows, q_tok0):
                q = self.rot(qT, "qT")
                s.dma("sp", q.t[0:d, 0:nq], q_src_ap, reads=[drb["QKNA"], drb["QM"]], writes=[q])
                acc = self.rot(accs, "acc")
                npair = len(chunks) // 2
                pend = []

                def pv(item):
                    pi, p = item
                    for jj in range(2):
                        ci = 2 * pi + jj
                        kc = chunks[ci][0]
                        s.op("pe", lambda e: e.matmul(acc.t[:, 0:nq], v_t.t[:, kc, :], p.t[:, jj, 0:nq], start=(ci == 0), stop=(ci == len(chunks) - 1)),
                             reads=[v_t, p], writes=[acc])
                for pi in range(npair):
                    ps = self.rot(wp, "wp")
                    psv = ps.t.rearrange("p (j n) -> p j n", j=2)
                    for jj in range(2):
                        kc = chunks[2 * pi + jj][0]
                        s.op("pe", lambda e: e.matmul(ps.t[:, jj * 512:jj * 512 + nq], k_t.t[0:d, kc * 128:(kc + 1) * 128], q.t[0:d, 0:nq], start=True, stop=True),
                             reads=[k_t, q], writes=[ps])
                    bs = chunks[2 * pi][1]
                    p = self.rot(pT, "pT")
                    if bs is not None:
                        sb_ = self.rot(sbias, "sbias")
                        s.op("dve", lambda e: e.tensor_tensor(out=sb_.t[:, :, 0:nq], in0=psv[:, :, 0:nq], in1=bias_t.t[:, bs:bs + 2, 0:nq], op=ALU.add),
                             reads=[ps, bias_t], writes=[sb_])
                        s.op("act", lambda e: e.activation(out=p.t[:, :, 0:nq], in_=sb_.t[:, :, 0:nq], func=AF.Exp), reads=[sb_], writes=[p])
                    else:
                        s.op("act", lambda e: e.activation(out=p.t[:, :, 0:nq], in_=psv[:, :, 0:nq], func=AF.Exp), reads=[ps], writes=[p])
                    pend.append((pi, p))
                    if len(pend) > 1:
                        pv(pend.pop(0))
                    if pi in (0, 3) and self.att_tail:
                        self.att_tail.pop(0)()
                while pend:
                    pv(pend.pop(0))
                while self.att_tail:
                    self.att_tail.pop(0)()
                st = {}
                self.att_tail = [lambda: _tail1(st, nq, acc, d), lambda: _tail2(st, b, nq, mix_rows, q_tok0)]

            def _tail1(st, nq, acc, d=96):
                rr = st["rr"] = self.rot(rrow, "rrow")
                if d == 64:
                    s.op("act", lambda e: e.activation(out=rr.t[64:128, 0:nq], in_=acc.t[64:128, 0:nq], func=AF.Ln), reads=[acc], writes=[rr])
                    s.op("act", lambda e: e.activation(out=rr.t[64:128, 0:nq], in_=rr.t[64:128, 0:nq], func=AF.Exp, scale=-1.0), reads=[rr], writes=[rr])
                else:
                    s.op("dve", lambda e: e.reciprocal(out=rr.t[64:128, 0:nq], in_=acc.t[64:128, 0:nq]), reads=[acc], writes=[rr])
                st["acc"] = acc

            def _tail2(st, b, nq, mix_rows, q_tok0):
                rr, acc = st["rr"], st["acc"]
                om = self.rot(omix, "omix")
                s.op("dve", lambda e: e.tensor_tensor(out=om.t[:, 0:nq], in0=acc.t[0:64, 0:nq], in1=rr.t[64:128, 0:nq], op=ALU.mult),
                     reads=[acc, rr], writes=[om])
                s.dma("pool", dr["MIXT"][b, mix_rows:mix_rows + 64, q_tok0:q_tok0 + nq], om.t[:, 0:nq], reads=[om], writes=[drb["MIXT"]])

            for b in range(NB):
                for h in range(4):
                    k_t, v_t = self.rot(kT, "kT"), self.rot(va, "va")
                    s.dma("sp", k_t.t[0:64, :], dr["QKNA"][b, 256 + h * 64:256 + (h + 1) * 64, :], reads=[drb["QKNA"]], writes=[k_t])
                    s.dma("sp", v_t.t[:, :, 0:64], dr["VNA"][b, :, h * 64:(h + 1) * 64].rearrange("(c p) f -> p c f", p=128),
                          reads=[drb["VNA"]], writes=[v_t])
                    for g in range(4):
                        tok0, nch, bt0 = NA_GROUPS[g]
                        bias_t = self.rot(bias, "bias")
                        s.dma("sp", bias_t.t[:, 0:nch, :], dr["nab_bf"][l, h, bt0:bt0 + nch].rearrange("c p q -> p c q"),
                              reads=[self.wb("nab_bf", l)], writes=[bias_t])
                        chunks = [(tok0 // 128 + i, i) for i in range(nch)] + [(16, None), (17, None)]
                        attend(b, 64, k_t, v_t, dr["QKNA"][b, h * 64:(h + 1) * 64, g * 512:(g + 1) * 512], 512, chunks, bias_t, h * 64, g * 512)
                    if not last:
                        attend(b, 64, k_t, v_t, dr["QKNA"][b, h * 64:(h + 1) * 64, SEQ:T], 256, [(16, None), (17, None)], None, h * 64, SEQ)
                for h in range(4):
                    k_t, v_t = self.rot(kT, "kT"), self.rot(va, "va")
                    s.dma("sp", k_t.t[0:96, :], dr["KM"][b, h], reads=[drb["KM"]], writes=[k_t])
                    s.dma("sp", v_t.t[:, :, 0:64], dr["VM"][b, :, h * 64:(h + 1) * 64].rearrange("(c p) f -> p c f", p=128),
                          reads=[drb["VM"]], writes=[v_t])
                    for g in range(4):
                        attend(b, 96, k_t, v_t, dr["QM"][b, h, :, g * 512:(g + 1) * 512], 512, [(i, None) for i in range(18)], None, 768 + h * 64, g * 512)
                    if not last:
                        attend(b, 96, k_t, v_t, dr["QM"][b, h, :, SEQ:T], 256, [(16, None), (17, None)], None, 768 + h * 64, SEQ)
            while self.att_tail:
                self.att_tail.pop(0)()

    def phase_ssd(self, l, last):
        nc, s, NB = self.nc, self.s, self.NB
        dr, drb = self.dr, self.drb
        W = 2312
        import os
        WA = int(os.environ.get("KWA", "0"))
        WB = int(os.environ.get("KWB", "0"))
        with ExitStack() as es:
            xp = [self.sb(es, "xp%d" % i, [128, W], BF16) for i in range(2)]
            xsfm = self.sb(es, "xsfm", [128, 4, T], BF16)
            BT = self.sb(es, "BT", [128, 2, T], BF16)
            CT = self.sb(es, "CT", [128, 2, T], BF16)
            xstm = self.sb(es, "xstm", [128, 18, 512], BF16)
            Btm = self.sb(es, "Btm", [128, 18, 256], BF16)
            raw = self.sb(es, "dtraw", [128, 18, 16], F32)
            dt = self.sb(es, "dt", [128, 18, 16], F32)
            a = self.sb(es, "a", [128, 18, 16], F32)
            acum = self.sb(es, "acum", [128, 18, 16], F32)
            nacum = self.sb(es, "nacum", [128, 18, 16], F32)
            atot = self.sb(es, "atot", [128, 18, 16], F32)
            eac = self.sb(es, "eac", [128, 18, 16], F32)
            dte = self.sb(es, "dte", [128, 18, 16], F32)
            etot = self.sb(es, "etot", [128, 18, 16], F32)
            aneg = self.sb(es, "aneg", [128, 16], F32)
            dsk = self.sb(es, "dsk", [128, 8, 64], F32)
            yacc = self.sb(es, "yacc", [128, 18, 512], F32)
            zts = [self.sb(es, "zt%d" % i, [128, 512], BF16) for i in range(3)]
            ymix = xsfm
            H = [self.sb(es, "H%d" % i, [128, 512], F32) for i in range(2)]
            Hbf = [self.sb(es, "Hbf%d" % i, [128, 512], BF16) for i in range(2)]
            xd = [self.sb(es, "xd%d" % i, [128, 512], BF16) for i in range(6)]
            xdd = [self.sb(es, "xdd%d" % i, [128, 512], BF16) for i in range(6)]
            abc = [self.sb(es, "abc%d" % i, [128, 8, 128], F32) for i in range(2)]
            T1 = [self.sb(es, "T1_%d" % i, [128, 8, 128], BF16) for i in range(4)]
            MT = [self.sb(es, "MT%d" % i, [128, 8, 128], BF16) for i in range(6)]
            tb = [self.sb(es, "tb%d" % i, [128, 512], F32) for i in range(4)]
            ssq = self.sb(es, "ssq", [128, 16], F32)
            obf = [self.sb(es, "sobf%d" % i, [128, 512], BF16) for i in range(3)]
            szb = [self.sb(es, "szb%d" % i, [128, 512], BF16) for i in range(3)]
            junk = self.sb(es, "junk", [128, 512], BF16)
            for i in range(2):
                s.op("dve", lambda e: e.memset(xp[i].t[:, 0:2], 0.0), writes=[xp[i]])
                s.op("dve", lambda e: e.memset(xp[i].t[:, 2050:2054], 0.0), writes=[xp[i]])
                s.op("dve", lambda e: e.memset(xp[i].t[:, 2310:2312], 0.0), writes=[xp[i]])
            s.op("act", lambda e: e.activation(out=aneg.t[:], in_=self.rowbc.t[:, 16:32], func=AF.Exp), reads=[self.rowbc], writes=[aneg])
            s.op("dve", lambda e: e.tensor_scalar(out=aneg.t[:], in0=aneg.t[:], scalar1=-1.0, scalar2=None, op0=ALU.mult), reads=[aneg], writes=[aneg])
            s.op("dve", lambda e: e.tensor_copy(out=dsk.t[:], in_=self.rowbc.t[:, 32:40].unsqueeze(2).to_broadcast([128, 8, 64])),
                 reads=[self.rowbc], writes=[dsk])
            nwbc = self.rowbc.t[:, 40:552]
            negm4 = [self.sb(es, "negm4_%d" % i, [128, 4, 128], BF16) for i in range(2)]
            for i in range(2):
                s.op("dve", lambda e: e.tensor_copy(out=negm4[i].t[:], in_=self.negmb[i].t[:].unsqueeze(1).to_broadcast([128, 4, 128])),
                     reads=[self.negmb[i]], writes=[negm4[i]])

            def bf16view(ps):
                return ps.t[:].bitcast(BF16)

            for b in range(NB):
                for c in range(8):
                    x_ = self.rot(xp, "xp")
                    s.dma("sp", x_.t[:, 2:2050], dr["XBC"][b, c * 128:(c + 1) * 128, 0:SEQ], reads=[drb["XBC"]], writes=[x_])
                    s.dma("sp", x_.t[:, 2054:2310], dr["XBC"][b, c * 128:(c + 1) * 128, SEQ:T], reads=[drb["XBC"]], writes=[x_])
                    v = yacc
                    vt = yacc.t[:, 0:5, :].rearrange("p c f -> p (c f)")[:, 0:2308]
                    s.op("dve", lambda e: e.tensor_scalar(out=vt, in0=x_.t[:, 0:2308], scalar1=self.convw.t[:, c, 0:1], scalar2=None, op0=ALU.mult),
                         reads=[x_, self.convw], writes=[v])
                    for jj in range(1, 5):
                        s.op("dve", lambda e: e.scalar_tensor_tensor(out=vt, in0=x_.t[:, jj:jj + 2308], scalar=self.convw.t[:, c, jj:jj + 1],
                                                                     in1=vt, op0=ALU.mult, op1=ALU.add),
                             reads=[x_, self.convw, v], writes=[v])
                    if c < 4:
                        dst, dl, dc = xsfm, xsfm.t[:, c, 0:SEQ], xsfm.t[:, c, SEQ:T]
                    elif c < 6:
                        dst, dl, dc = BT, BT.t[:, c - 4, 0:SEQ], BT.t[:, c - 4, SEQ:T]
                    else:
                        dst, dl, dc = CT, CT.t[:, c - 6, 0:SEQ], CT.t[:, c - 6, SEQ:T]
                    s.op("act", lambda e: e.activation(out=dl, in_=vt[:, 0:SEQ], func=AF.Silu, bias=self.vec.t[:, 8 + c:9 + c]),
                         reads=[v, self.vec], writes=[dst])
                    s.op("act", lambda e: e.activation(out=dc, in_=vt[:, 2052:2308], func=AF.Silu, bias=self.vec.t[:, 8 + c:9 + c]),
                         reads=[v, self.vec], writes=[dst])
                for tc in range(18):
                    ps = self.psum()
                    pv = bf16view(ps)
                    for ci in range(4):
                        s.op("pe", lambda e: e.transpose(pv[:, ci * 128:(ci + 1) * 128], xsfm.t[:, ci, tc * 128:(tc + 1) * 128], self.ident.t[:]),
                             reads=[xsfm, self.ident], writes=[ps])
                    for ci in range(2):
                        s.op("pe", lambda e: e.transpose(pv[:, 512 + ci * 128:512 + (ci + 1) * 128], BT.t[:, ci, tc * 128:(tc + 1) * 128], self.ident.t[:]),
                             reads=[BT, self.ident], writes=[ps])
                    s.op("act", lambda e: e.activation(out=xstm.t[:, tc, :], in_=pv[:, 0:512], func=AF.Copy), reads=[ps], writes=[xstm])
                    s.op("dve", lambda e: e.tensor_copy(out=Btm.t[:, tc, :], in_=pv[:, 512:768]), reads=[ps], writes=[Btm])
                s.dma("sp", raw.t[:], dr["DTS"][b].rearrange("(c p) f -> p c f", p=128), reads=[drb["DTS"]], writes=[raw])
                s.op("dve", lambda e: e.tensor_tensor(out=dt.t[:], in0=raw.t[:], in1=self.rowbc.t[:, 0:16].unsqueeze(1).to_broadcast([128, 18, 16]), op=ALU.add),
                     reads=[raw, self.rowbc], writes=[dt])
                s.op("dve", lambda e: e.tensor_scalar(out=dt.t[:], in0=dt.t[:], scalar1=30.0, scalar2=None, op0=ALU.min), reads=[dt], writes=[dt])
                s.op("act", lambda e: e.activation(out=dt.t[:], in_=dt.t[:], func=AF.Exp), reads=[dt], writes=[dt])
                s.op("act", lambda e: e.activation(out=dt.t[:], in_=dt.t[:], func=AF.Ln, bias=self.cst.t[:, 3:4]), reads=[dt, self.cst], writes=[dt])
                s.op("dve", lambda e: e.tensor_tensor(out=a.t[:], in0=dt.t[:], in1=aneg.t[:].unsqueeze(1).to_broadcast([128, 18, 16]), op=ALU.mult),
                     reads=[dt, aneg], writes=[a])
                for d in range(2):
                    ps = self.psum()
                    s.op("pe", lambda e: e.matmul(ps.t[:, 0:144], self.tri[d].t[:], a.t[:, :, d * 8:(d + 1) * 8], start=True, stop=True),
                         reads=[self.tri[d], a], writes=[ps])
                    s.op("dve", lambda e: e.tensor_copy(out=acum.t[:, :, d * 8:(d + 1) * 8], in_=ps.t[:, 0:144].rearrange("p (c h) -> p c h", h=8)),
                         reads=[ps], writes=[acum])
                    ps = self.psum()
                    s.op("pe", lambda e: e.matmul(ps.t[:, 0:144], self.onesf.t[:], a.t[:, :, d * 8:(d + 1) * 8], start=True, stop=True),
                         reads=[self.onesf, a], writes=[ps])
                    s.op("dve", lambda e: e.tensor_copy(out=atot.t[:, :, d * 8:(d + 1) * 8], in_=ps.t[:, 0:144].rearrange("p (c h) -> p c h", h=8)),
                         reads=[ps], writes=[atot])
                s.op("dve", lambda e: e.tensor_scalar(out=nacum.t[:], in0=acum.t[:], scalar1=-1.0, scalar2=None, op0=ALU.mult), reads=[acum], writes=[nacum])
                s.op("act", lambda e: e.activation(out=eac.t[:], in_=acum.t[:], func=AF.Exp), reads=[acum], writes=[eac])
                s.op("act", lambda e: e.activation(out=etot.t[:], in_=atot.t[:], func=AF.Exp), reads=[atot], writes=[etot])
                s.op("dve", lambda e: e.tensor_tensor(out=dte.t[:], in0=atot.t[:], in1=acum.t[:], op=ALU.subtract), reads=[atot, acum], writes=[dte])
                s.op("act", lambda e: e.activation(out=dte.t[:], in_=dte.t[:], func=AF.Exp), reads=[dte], writes=[dte])
                orders = [[16, 17] + list(range(16)), [17, 16] + list(range(15, -1, -1))]
                touched = set()
                for d in range(2):
                    s.op("dve", lambda e: e.memset(H[d].t[:], 0.0), writes=[H[d]])
                    s.op("dve", lambda e: e.memset(Hbf[d].t[:], 0.0), writes=[Hbf[d]])

                def mk(oi, d):
                    c = orders[d][oi]
                    return dict(oi=oi, d=d, c=c, hs=slice(d * 8, (d + 1) * 8), cs=slice(c * 128, (c + 1) * 128),
                                y=not (last and c >= 16), mt=None)

                def stage_a(cxs):
                    for cx in cxs:
                        c, hs = cx["c"], cx["hs"]
                        x1, x2 = cx["x1"], cx["x2"] = self.rot(xd, "xd"), self.rot(xdd, "xdd")
                        s.op("pool", lambda e: e.tensor_tensor(out=x1.t[:].rearrange("p (h q) -> p h q", q=64), in0=xstm.t[:, c, :].rearrange("p (h q) -> p h q", q=64),
                                                               in1=dt.t[:, c, hs].unsqueeze(2).to_broadcast([128, 8, 64]), op=ALU.mult),
                             reads=[xstm, dt], writes=[x1])
                        s.op("pool", lambda e: e.tensor_tensor(out=x2.t[:].rearrange("p (h q) -> p h q", q=64), in0=x1.t[:].rearrange("p (h q) -> p h q", q=64),
                                                               in1=dte.t[:, c, hs].unsqueeze(2).to_broadcast([128, 8, 64]), op=ALU.mult),
                             reads=[x1, dte], writes=[x2])
                    ys = [cx for cx in cxs if cx["y"]]
                    for cx in ys:
                        c, hs, cs_, d = cx["c"], cx["hs"], cx["cs"], cx["d"]
                        pcb = cx["pcb"] = self.psum()
                        for g in range(2):
                            s.op("pe", lambda e: e.matmul(pcb.t[:, g * 128:(g + 1) * 128], BT.t[:, g, cs_], CT.t[:, g, cs_], start=True, stop=True),
                                 reads=[BT, CT], writes=[pcb])
                        ab = cx["ab"] = self.rot(abc, "abc")
                        s.op("dve", lambda e: e.tensor_tensor(out=ab.t[:], in0=self.tri[d].t[:].unsqueeze(1).to_broadcast([128, 8, 128]),
                                                              in1=a.t[:, c, hs].unsqueeze(2).to_broadcast([128, 8, 128]), op=ALU.mult),
                             reads=[a, self.tri[d]], writes=[ab])
                    for cx in ys:
                        d, ab = cx["d"], cx["ab"]
                        pd = cx["pd"] = [self.psum(), self.psum()]
                        for half in range(2):
                            reg = pd[half].t[:, :]
                            s.op("pe", lambda e: e.matmul(reg, self.ident.t[:], negm4[d].t[:].rearrange("p r l -> p (r l)"), start=True, stop=False),
                                 reads=[self.ident, negm4[d]], writes=[pd[half]])
                            s.op("pe", lambda e: e.matmul(reg, self.onesf.t[:], ab.t[:, half * 4:(half + 1) * 4, :].rearrange("p h l -> p (h l)"), start=False, stop=True),
                                 reads=[self.onesf, ab], writes=[pd[half]])
                    for cx in ys:
                        c, d, pd = cx["c"], cx["d"], cx["pd"]
                        t1 = cx["t1"] = self.rot(T1, "T1")
                        for hh in range(8):
                            reg = pd[hh // 4].t[:, (hh % 4) * 128:(hh % 4 + 1) * 128]
                            s.op("act", lambda e: e.activation(out=t1.t[:, hh, :], in_=reg, func=AF.Exp, bias=nacum.t[:, c, d * 8 + hh:d * 8 + hh + 1]),
                                 reads=[pd[hh // 4], nacum], writes=[t1])
                    for cx in ys:
                        t1, pcb = cx["t1"], cx["pcb"]
                        mt = cx["mt"] = self.rot(MT, "MT")
                        for g in range(2):
                            s.op("dve", lambda e: e.tensor_tensor(out=mt.t[:, g * 4:(g + 1) * 4, :], in0=t1.t[:, g * 4:(g + 1) * 4, :],
                                                                  in1=pcb.t[:, g * 128:(g + 1) * 128].unsqueeze(1).to_broadcast([128, 4, 128]), op=ALU.mult),
                                 reads=[t1, pcb], writes=[mt])
                    return cxs

                def stage_b(cxs):
                    ys = [cx for cx in cxs if cx["mt"] is not None]
                    for cx in ys:
                        d, cs_, x1, mt = cx["d"], cx["cs"], cx["x1"], cx["mt"]
                        py = cx["py"] = self.psum()
                        for hh in range(8):
                            s.op("pe", lambda e: e.matmul(py.t[:, hh * 64:(hh + 1) * 64], mt.t[:, hh, :], x1.t[:, hh * 64:(hh + 1) * 64], start=True, stop=True),
                                 reads=[mt, x1], writes=[py])
                        pyo = cx["pyo"] = self.psum()
                        for g in range(2):
                            s.op("pe", lambda e: e.matmul(pyo.t[:, g * 256:(g + 1) * 256], CT.t[:, g, cs_], Hbf[d].t[:, g * 256:(g + 1) * 256], start=True, stop=True),
                                 reads=[CT, Hbf[d]], writes=[pyo])
                    upd = [cx for cx in cxs if cx["oi"] < 17]
                    for cx in upd:
                        d, c, x2, hs = cx["d"], cx["c"], cx["x2"], cx["hs"]
                        pcs = cx["pcs"] = self.psum()
                        for g in range(2):
                            s.op("pe", lambda e: e.matmul(pcs.t[:, g * 256:(g + 1) * 256], Btm.t[:, c, g * 128:(g + 1) * 128], x2.t[:, g * 256:(g + 1) * 256], start=True, stop=True),
                                 reads=[Btm, x2], writes=[pcs])
                        s.op("pool", lambda e: e.tensor_tensor(out=H[d].t[:].rearrange("p (h q) -> p h q", q=64), in0=H[d].t[:].rearrange("p (h q) -> p h q", q=64),
                                                               in1=etot.t[:, c, hs].unsqueeze(2).to_broadcast([128, 8, 64]), op=ALU.mult),
                             reads=[H[d], etot], writes=[H[d]])
                    for cx in ys:
                        c, hs, pyo = cx["c"], cx["hs"], cx["pyo"]
                        t_ = cx["t_"] = self.rot(tb, "tb")
                        s.op("dve", lambda e: e.tensor_tensor(out=t_.t[:].rearrange("p (h q) -> p h q", q=64), in0=pyo.t[:].rearrange("p (h q) -> p h q", q=64),
                                                              in1=eac.t[:, c, hs].unsqueeze(2).to_broadcast([128, 8, 64]), op=ALU.mult),
                             reads=[pyo, eac], writes=[t_])
                    for cx in upd:
                        d, pcs = cx["d"], cx["pcs"]
                        s.op("dve", lambda e: e.tensor_tensor(out=H[d].t[:], in0=H[d].t[:], in1=pcs.t[:], op=ALU.add), reads=[H[d], pcs], writes=[H[d]])
                        s.op("act", lambda e: e.activation(out=Hbf[d].t[:], in_=H[d].t[:], func=AF.Copy), reads=[H[d]], writes=[Hbf[d]])
                    for cx in ys:
                        c, t_, py = cx["c"], cx["t_"], cx["py"]
                        if c not in touched:
                            touched.add(c)
                            s.op("dve", lambda e: e.tensor_tensor(out=yacc.t[:, c, :], in0=t_.t[:], in1=py.t[:], op=ALU.add), reads=[t_, py], writes=[yacc])
                        else:
                            s.op("dve", lambda e: e.tensor_tensor(out=t_.t[:], in0=t_.t[:], in1=py.t[:], op=ALU.add), reads=[t_, py], writes=[t_])
                            s.op("pool", lambda e: e.tensor_tensor(out=yacc.t[:, c, :], in0=yacc.t[:, c, :], in1=t_.t[:], op=ALU.add), reads=[t_, yacc], writes=[yacc])

                AHEAD = 2
                inflight = {}
                for oi in range(18 + AHEAD):
                    if oi < 18:
                        inflight[oi] = stage_a([mk(oi, d) for d in range(2)])
                    if oi >= AHEAD:
                        stage_b(inflight.pop(oi - AHEAD))
                nch = 16 if last else 18
                G = 3
                for c0 in range(0, nch, G):
                    grp = list(range(c0, min(nch, c0 + G)))
                    tt_, szs, zz = {}, {}, {}
                    for c in grp:
                        t_ = tt_[c] = self.rot(tb, "tb")
                        s.op("pool", lambda e: e.tensor_tensor(out=t_.t[:], in0=xstm.t[:, c, :], in1=dsk.t[:].rearrange("p h q -> p (h q)"), op=ALU.mult),
                             reads=[xstm, dsk], writes=[t_])
                        zt = zz[c] = self.rot(zts, "zt")
                        s.dma("sp", zt.t[:], dr["ZS"][b, c * 128:(c + 1) * 128, :], reads=[drb["ZS"]], writes=[zt])
                    for c in grp:
                        t_ = tt_[c]
                        s.op("pool", lambda e: e.tensor_tensor(out=t_.t[:], in0=t_.t[:], in1=yacc.t[:, c, :], op=ALU.add), reads=[t_, yacc], writes=[t_])
                        sz = szs[c] = self.rot(szb, "szb")
                        s.op("act", lambda e: e.activation(out=sz.t[:], in_=zz[c].t[:], func=AF.Silu), reads=[zz[c]], writes=[sz])
                    for c in grp:
                        t_, sz = tt_[c], szs[c]
                        s.op("dve", lambda e: e.tensor_tensor(out=t_.t[:], in0=t_.t[:], in1=sz.t[:], op=ALU.mult), reads=[t_, sz], writes=[t_])
                    for gi, c in enumerate(grp):
                        s.op("act", lambda e: e.activation(out=junk.t[:], in_=tt_[c].t[:], func=AF.Square, accum_out=ssq.t[:, gi:gi + 1]),
                             reads=[tt_[c]], writes=[junk, ssq])
                    ng = len(grp)
                    s.op("act", lambda e: e.activation(out=ssq.t[:, 4:4 + ng], in_=ssq.t[:, 0:ng], func=AF.Ln, scale=1.0 / 512, bias=self.cst.t[:, 0:1]),
                         reads=[ssq, self.cst], writes=[ssq])
                    s.op("act", lambda e: e.activation(out=ssq.t[:, 8:8 + ng], in_=ssq.t[:, 4:4 + ng], func=AF.Exp, scale=-0.5), reads=[ssq], writes=[ssq])
                    oo = {}
                    for gi, c in enumerate(grp):
                        o = oo[c] = self.rot(obf, "sobf")
                        s.op("dve", lambda e: e.scalar_tensor_tensor(out=o.t[:], in0=tt_[c].t[:], scalar=ssq.t[:, 8 + gi:9 + gi], in1=nwbc, op0=ALU.mult, op1=ALU.mult),
                             reads=[tt_[c], ssq, self.rowbc], writes=[o])
                    pss = {}
                    for c in grp:
                        o = oo[c]
                        ps = pss[c] = self.psum()
                        pv = bf16view(ps)
                        for ci in range(4):
                            s.op("pe", lambda e: e.transpose(pv[:, ci * 128:(ci + 1) * 128], o.t[:, ci * 128:(ci + 1) * 128], self.ident.t[:]),
                                 reads=[o, self.ident], writes=[ps])
                    for gi, c in enumerate(grp):
                        ps = pss[c]
                        pv = bf16view(ps)
                        eng = "act" if gi % 2 == 0 else "dve"
                        if eng == "act":
                            s.op("act", lambda e: e.activation(out=ymix.t[:, :, c * 128:(c + 1) * 128], in_=pv[:, 0:512].rearrange("p (k t) -> p k t", t=128), func=AF.Copy),
                                 reads=[ps], writes=[ymix])
                        else:
                            s.op("dve", lambda e: e.tensor_copy(out=ymix.t[:, :, c * 128:(c + 1) * 128], in_=pv[:, 0:512].rearrange("p (k t) -> p k t", t=128)),
                                 reads=[ps], writes=[ymix])
                ntok = SEQ if last else T
                s.dma("pool", dr["MIXT"][b, 256:768, 0:ntok].rearrange("(k p) t -> p k t", p=128), ymix.t[:, :, 0:ntok], reads=[ymix], writes=[drb["MIXT"]])

    def phase_FG(self, l, last):
        nc, s, NB, J = self.nc, self.s, self.NB, self.J
        dr, drb = self.dr, self.drb
        with ExitStack() as es:
            wout = self.sb(es, "wout", [128, 8, 8, 128], BF16)
            s.dma("sp", wout.t[:], dr["wout_bf"][l].rearrange("c p k f -> p c k f"), reads=[self.wb("wout_bf", l)], writes=[wout])
            xts = [self.sb(es, "fxt%d" % i, [128, 8, 512], F32) for i in range(2)]
            mixs = [self.sb(es, "fmix%d" % i, [128, 8, 512], BF16) for i in range(2)]
            sq = self.sb(es, "fsq", [128, 8, 512], BF16)
            xms = [self.sb(es, "fxm%d" % i, [128, 8, 512], BF16) for i in range(2)]
            rstd = self.sb(es, "frstd", [128, 512], F32)
            tmp = [self.sb(es, "ftmp%d" % i, [128, 512], F32) for i in range(3)]
            hT = self.sb(es, "hT", [128, 32, 512], BF16)
            rl = [self.sb(es, "rl%d" % i, [128, 512], BF16) for i in range(3)]
            w1 = [self.sb(es, "w1_%d" % i, [128, 4, 8, 128], BF16) for i in range(2)]
            w2 = [self.sb(es, "w2_%d" % i, [128, 32, 128], BF16) for i in range(2)]
            xsrc = "xT" if l == 0 else "XS"
            tiles = TILES[:4] if last else TILES
            jobs = [(b, ti, t0, n) for b in range(NB) for ti, (t0, n) in enumerate(tiles)]

            def front(job):
                b, ti, t0, n = job
                j = NB if ti == 4 else b
                xt = self.rot(xts, "fxt")
                s.dma("sp", xt.t[:, :, 0:n], dr[xsrc][b].rearrange("(k p) t -> p k t", p=128)[:, :, t0:t0 + n], reads=[drb[xsrc]], writes=[xt])
                mx = self.rot(mixs, "fmix")
                s.dma("sp", mx.t[:, :, 0:n], dr["MIXT"][b].rearrange("(k p) t -> p k t", p=128)[:, :, t0:t0 + n], reads=[drb["MIXT"]], writes=[mx])
                for fc in range(8):
                    ps = self.psum()
                    for k in range(8):
                        s.op("pe", lambda e: e.matmul(ps.t[:, 0:n], wout.t[:, fc, k, :], mx.t[:, k, 0:n], start=(k == 0), stop=(k == 7)),
                             reads=[wout, mx], writes=[ps])
                    s.op("dve", lambda e: e.scalar_tensor_tensor(out=xt.t[:, fc, 0:n], in0=ps.t[:, 0:n], scalar=self.mod.t[:, 16 + fc, j:j + 1],
                                                                 in1=xt.t[:, fc, 0:n], op0=ALU.mult, op1=ALU.add),
                         reads=[ps, self.mod, xt], writes=[xt])
                xm = self.rot(xms, "fxm")
                self.norm_mod(xt, n, self.A2, 24, j, xm, sq, rstd, tmp)
                return dict(job=job, xt=xt, xm=xm, j=j)

            def ffn1(st):
                b, ti, t0, n = st["job"]
                xm = st["xm"]
                for g in range(8):
                    w = self.rot(w1, "w1")
                    s.dma("sp", w.t[:], dr["wff1_bf"][l, g * 4:(g + 1) * 4].rearrange("c p k f -> p c k f"), reads=[self.wb("wff1_bf", l)], writes=[w])
                    for c in range(4):
                        fc = g * 4 + c
                        ps = self.psum()
                        for k in range(8):
                            s.op("pe", lambda e: e.matmul(ps.t[:, 0:n], w.t[:, c, k, :], xm.t[:, k, 0:n], start=(k == 0), stop=(k == 7)),
                                 reads=[w, xm], writes=[ps])
                        r = self.rot(rl, "rl")
                        s.op("act", lambda e: e.activation(out=r.t[:, 0:n], in_=ps.t[:, 0:n], func=AF.Relu), reads=[ps], writes=[r])
                        s.op("pool", lambda e: e.tensor_tensor(out=hT.t[:, fc, 0:n], in0=r.t[:, 0:n], in1=r.t[:, 0:n], op=ALU.mult), reads=[r], writes=[hT])

            def ffn2(st):
                b, ti, t0, n = st["job"]
                xt, j = st["xt"], st["j"]
                for fc in range(8):
                    w = self.rot(w2, "w2")
                    s.dma("sp", w.t[:], dr["wff2_bf"][l, fc], reads=[self.wb("wff2_bf", l)], writes=[w])
                    ps = self.psum()
                    for k in range(32):
                        s.op("pe", lambda e: e.matmul(ps.t[:, 0:n], w.t[:, k, :], hT.t[:, k, 0:n], start=(k == 0), stop=(k == 31)),
                             reads=[w, hT], writes=[ps])
                    s.op("dve", lambda e: e.scalar_tensor_tensor(out=xt.t[:, fc, 0:n], in0=ps.t[:, 0:n], scalar=self.mod.t[:, 40 + fc, j:j + 1],
                                                                 in1=xt.t[:, fc, 0:n], op0=ALU.mult, op1=ALU.add),
                         reads=[ps, self.mod, xt], writes=[xt])
                dst = "out" if last else "XS"
                s.dma("pool", dr[dst][b].rearrange("(k p) t -> p k t", p=128)[:, :, t0:t0 + n], xt.t[:, :, 0:n], reads=[xt], writes=[drb[dst]])

            cur = front(jobs[0])
            for ji in range(len(jobs)):
                ffn1(cur)
                nxt = front(jobs[ji + 1]) if ji + 1 < len(jobs) else None
                ffn2(cur)
                cur = nxt


_CACHE = {}


def _get_prog(NB, L, dbg=()):
    key = (NB, L, tuple(sorted(dbg)))
    if key not in _CACHE:
        p = Prog(NB, L, dbg)
        p.build()
        _CACHE[key] = p
    return _CACHE[key]


def run(inputs, ncores=NCORES, L=None, dbg=(), trace=False):
    x = np.asarray(inputs["x"], np.float32)
    ctx = np.asarray(inputs["ctx"], np.float32)
    c = np.asarray(inputs["c"], np.float32)
    c_ctx = np.asarray(inputs["c_ctx"], np.float32)
    B = x.shape[0]
    NB = B // ncores
    Lw = inputs["w_ada"].shape[0]
    L = Lw if L is None else L
    w = _prep_weights({k: (np.asarray(v)[:L] if np.asarray(v).ndim >= 1 and np.asarray(v).shape[0] == Lw and k not in ("x", "c", "ctx", "c_ctx") else v)
                       for k, v in inputs.items()})
    cst = _consts()
    prog = _get_prog(NB, L, dbg)
    in_maps = []
    for i in range(ncores):
        sl = slice(i * NB, (i + 1) * NB)
        xT = np.concatenate([x[sl].transpose(0, 2, 1), ctx[sl].transpose(0, 2, 1)], axis=2)
        cc = np.concatenate([c[sl], c_ctx[None]], axis=0)
        cT = np.ascontiguousarray(cc.reshape(NB + 1, 8, 128).transpose(2, 1, 0))
        m = {"xT": np.ascontiguousarray(xT), "cT": cT}
        m.update(w)
        m.update(cst)
        in_maps.append(m)
    res = run_bass_kernel_spmd(prog.nc, in_maps, core_ids=list(range(ncores)), **({"trace": True} if trace else {}))
    out = np.concatenate([r["out"].transpose(0, 2, 1) for r in res.results], axis=0)
    return np.ascontiguousarray(out), res


def kernel(**inputs):
    out, _ = run(inputs)
    return out.astype(np.float32)
```

```python
import numpy as np
from contextlib import ExitStack
import concourse.bass as bass
import concourse.mybir as mybir
from concourse.bass_utils import run_bass_kernel_spmd

F32 = mybir.dt.float32
BF16 = mybir.dt.bfloat16
AF = mybir.ActivationFunctionType
ALU = mybir.AluOpType

D = 1024
SEQ = 2048
CTX = 256
T = SEQ + CTX
DFF = 4096
EPS = 1e-6
NEG = -30000.0
NCORES = 8


class Buf:
    __slots__ = ("w", "r")

    def __init__(self):
        self.w = None
        self.r = []


class TT:
    __slots__ = ("t", "b", "ps")

    def __init__(self, t, ps=False):
        self.t = t
        self.b = Buf()
        self.ps = ps


class Sched:
    ND = 40

    def __init__(self, nc, es):
        self.nc = nc
        self.E = dict(pe=nc.tensor, dve=nc.vector, act=nc.scalar, pool=nc.gpsimd, sp=nc.sync)
        self.csem = {e: es.enter_context(nc.semaphore("c_" + e)) for e in ("pe", "dve", "act", "pool")}
        self.ccnt = {e: 0 for e in self.csem}
        self.dsems = [es.enter_context(nc.semaphore("d%d" % i)) for i in range(self.ND)]
        self.dcnt = [0] * self.ND
        self.dnext = 0
        self.dnext_pool = 0
        self.waited = {e: {} for e in self.E}
        self.n_inst = 0

    def _wait(self, eng, ev):
        key, sem, val = ev[1], ev[2], ev[3]
        if self.waited[eng].get(key, 0) >= val:
            return
        self.E[eng].wait_ge(sem, val)
        self.waited[eng][key] = val

    def _deps(self, eng, reads, writes):
        for b in reads:
            if b.w is not None and not (b.w[0] == eng == "pe"):
                self._wait(eng, b.w)
        for b in writes:
            if b.w is not None and not (b.w[0] == eng == "pe"):
                self._wait(eng, b.w)
            for ev in b.r:
                if not (ev[0] == eng == "pe"):
                    self._wait(eng, ev)

    def _update(self, ev, reads, writes):
        for b in reads:
            b.r = [e for e in b.r if e[1] != ev[1]] + [ev]
        for b in writes:
            b.w = ev
            b.r = []

    def op(self, eng, fn, reads=(), writes=()):
        writes = list(writes) + [x for x in reads if isinstance(x, TT) and x.ps]
        reads = [x.b if isinstance(x, TT) else x for x in reads if not (isinstance(x, TT) and x.ps)]
        writes = [x.b if isinstance(x, TT) else x for x in writes]
        self._deps(eng, reads, writes)
        inst = fn(self.E[eng])
        self.ccnt[eng] += 1
        inst.then_inc(self.csem[eng], 1)
        ev = (eng, "c_" + eng, self.csem[eng], self.ccnt[eng])
        self._update(ev, reads, writes)
        self.n_inst += 1

    def dma(self, q, out, in_, reads=(), writes=()):
        reads = [x.b if isinstance(x, TT) else x for x in reads]
        writes = [x.b if isinstance(x, TT) else x for x in writes]
        half = self.ND // 2
        if q == "pool":
            i = half + self.dnext_pool
            self.dnext_pool = (self.dnext_pool + 1) % half
        else:
            i = self.dnext
            self.dnext = (self.dnext + 1) % half
        key = "d%d" % i
        if self.dcnt[i] > 0:
            self._wait(q, ("dma", key, self.dsems[i], self.dcnt[i]))
        self._deps(q, reads, writes)
        inst = self.E[q].dma_start(out=out, in_=in_)
        self.dcnt[i] += 16
        inst.then_inc(self.dsems[i], 16)
        ev = ("dma", key, self.dsems[i], self.dcnt[i])
        self._update(ev, reads, writes)
        self.n_inst += 1

    def barrier(self):
        evs = [(e, "c_" + e, self.csem[e], self.ccnt[e]) for e in self.csem if self.ccnt[e] > 0]
        evs += [("dma", "d%d" % i, self.dsems[i], self.dcnt[i]) for i in range(self.ND) if self.dcnt[i] > 0]
        for eng in self.E:
            for ev in evs:
                if ev[0] != eng:
                    self._wait(eng, ev)


def _lhsT_layout(W, cols_list, mc=128):
    K = W.shape[0]
    nk = K // 128
    out = np.zeros((len(cols_list), 128, nk, mc), np.float32)
    Wr = W.reshape(nk, 128, W.shape[1])
    for i, cols in enumerate(cols_list):
        cols = np.asarray(cols)
        out[i, :, :, :len(cols)] = Wr[:, :, cols].transpose(1, 0, 2)
    return out


def _na_bias_tables(rpb):
    H = 4
    kc = np.arange(64)
    qc = np.arange(64)
    col_start = np.clip(qc - 8, 0, 48)
    col_in = (kc[:, None] >= col_start[None, :]) & (kc[:, None] < col_start[None, :] + 16)
    col_idx = np.clip(kc[:, None] - qc[None, :], -15, 15) + 15
    tiles = []

    def tile_for(qr0, kr0):
        tl = np.full((H, 2, 64, 8, 64), NEG, np.float32)
        for p in range(2):
            kr = kr0 + p
            for j in range(8):
                qr = qr0 + j
                rs = min(max(qr - 4, 0), 24)
                if rs <= kr < rs + 8:
                    ridx = kr - qr + 7
                    blk = rpb[:, ridx][:, col_idx]
                    blk = np.where(col_in[None], blk, np.float32(NEG))
                    tl[:, p, :, j, :] = blk
        return tl.reshape(H, 128, 512)

    for c in range(6):
        tiles.append(tile_for(0, 2 * c))
    for c in range(8):
        tiles.append(tile_for(8, 4 + 2 * c))
    for c in range(6):
        tiles.append(tile_for(24, 20 + 2 * c))
    return np.stack(tiles, axis=1)


NA_GROUPS = [
    (0, 6, 0), (256, 8, 6), (768, 8, 6), (1280, 6, 14)]


def _consts():
    c = {}
    c["ident"] = np.eye(128, dtype=np.float32)
    c["ones"] = np.ones((128, 128), np.float32)
    b = np.zeros((128, 128), np.float32)
    b[:64, :64] = 1
    b[64:, 64:] = 1
    c["blk64"] = b
    pos = np.arange(SEQ)
    axes = np.stack([pos // 64, pos % 64], -1).astype(np.float32)
    inv = (10000.0 ** (-np.arange(8, dtype=np.float32) / 8)).astype(np.float32)
    ang = axes[:, :, None] * inv
    cos = np.cos(ang).astype(np.float32)
    sin = np.sin(ang).astype(np.float32)
    cosT = np.ones((128, T), np.float32)
    sinT = np.zeros((128, T), np.float32)
    for ax in range(2):
        base = 64 + ax * 16
        cosT[base:base + 8, :SEQ] = cos[:, ax].T
        cosT[base + 8:base + 16, :SEQ] = cos[:, ax].T
        sinT[base:base + 8, :SEQ] = sin[:, ax].T
        sinT[base + 8:base + 16, :SEQ] = sin[:, ax].T
    c["cosT"] = cosT
    c["sinT"] = sinT
    P = np.zeros((128, 128), np.float32)
    for ax in range(2):
        base = 64 + ax * 16
        for i in range(8):
            P[base + i, base + 8 + i] = -1.0
            P[base + 8 + i, base + i] = 1.0
    c["prot"] = np.ascontiguousarray(P.T)
    k = np.arange(128)
    c["tri_f"] = (k[:, None] <= k[None, :]).astype(np.float32)
    c["tri_b"] = (k[:, None] >= k[None, :]).astype(np.float32)
    c["negm_f"] = np.where(k[:, None] <= k[None, :], 0.0, NEG).astype(np.float32)
    c["negm_b"] = np.where(k[:, None] >= k[None, :], 0.0, NEG).astype(np.float32)
    sel = np.zeros((128, 64), np.float32)
    sel[64, :] = 1.0
    c["sel64"] = sel
    return c


def _prep_weights(inp):
    L = inp["w_ada"].shape[0]
    f32 = np.float32
    w = {}
    wa = np.asarray(inp["w_ada"], f32)
    w["wada"] = np.ascontiguousarray(wa.reshape(L, 8, 128, 48, 128).transpose(0, 3, 2, 1, 4))
    w["bada"] = np.ascontiguousarray(np.asarray(inp["b_ada"], f32).reshape(L, 48, 128).transpose(0, 2, 1))
    w["nw1"] = np.ascontiguousarray(np.asarray(inp["norm1_w"], f32).reshape(L, 8, 128).transpose(0, 2, 1))
    w["nw2"] = np.ascontiguousarray(np.asarray(inp["norm2_w"], f32).reshape(L, 8, 128).transpose(0, 2, 1))
    win = np.asarray(inp["w_in"], f32)
    fm_cols = [np.arange(0, 128), np.arange(128, 256), np.arange(256, 384), np.arange(384, 512)]
    fm_cols += [np.arange(1280 + 128 * i, 1280 + 128 * (i + 1)) for i in range(8)]
    fm_cols += [np.arange(2320, 2448), np.arange(2448, 2576), np.arange(2576, 2704)]
    winfm = np.zeros((L, 16, 128, 8, 128), f32)
    for l in range(L):
        winfm[l, :15] = _lhsT_layout(win[l], fm_cols)
        kr = win[l][:, 2704:2736].reshape(8, 128, 32).transpose(1, 0, 2)
        winfm[l, 15, :, :, 64:96] = kr
    w["winfm"] = winfm
    w["winz"] = np.ascontiguousarray(win[:, :, 768:1280].reshape(L, 8, 128, 512).transpose(0, 2, 1, 3))
    vdt = np.zeros((L, 128, 8, 272), f32)
    vdt[..., :256] = win[:, :, 512:768].reshape(L, 8, 128, 256).transpose(0, 2, 1, 3)
    vdt[..., 256:272] = win[:, :, 2304:2320].reshape(L, 8, 128, 16).transpose(0, 2, 1, 3)
    w["winvdt"] = vdt
    wuq = np.asarray(inp["mla_w_uq"], f32)
    w["wuq"] = np.ascontiguousarray(wuq.reshape(L, 2, 128, 4, 96).transpose(0, 2, 3, 1, 4))
    wukv = np.asarray(inp["mla_w_ukv"], f32).reshape(L, 128, 4, 128)
    wk = np.zeros((L, 128, 4, 96), f32)
    wk[..., :64] = wukv[..., :64]
    w["wukvk"] = wk
    w["wukvv"] = np.ascontiguousarray(wukv[..., 64:].reshape(L, 128, 256))
    wo = np.asarray(inp["w_out"], f32)
    w["wout"] = np.ascontiguousarray(wo.reshape(L, 8, 128, 8, 128).transpose(0, 3, 2, 1, 4))
    w1 = np.asarray(inp["w_ff1"], f32)
    w["wff1"] = np.ascontiguousarray(w1.reshape(L, 8, 128, 32, 128).transpose(0, 3, 2, 1, 4))
    w2 = np.asarray(inp["w_ff2"], f32)
    w["wff2"] = np.ascontiguousarray(w2.reshape(L, 32, 128, 8, 128).transpose(0, 3, 2, 1, 4))
    vec = np.zeros((L, 128, 32), f32)
    vec[:, :, 0] = np.tile(np.asarray(inp["na_qn_w"], f32), (1, 2))
    vec[:, :, 1] = np.tile(np.asarray(inp["na_kn_w"], f32), (1, 2))
    vec[:, :, 2:4] = np.asarray(inp["mla_cq_norm_w"], f32).reshape(L, 2, 128).transpose(0, 2, 1)
    vec[:, :, 4] = np.asarray(inp["mla_ckv_norm_w"], f32)
    vec[:, :96, 5] = np.asarray(inp["mla_qn_w"], f32)
    vec[:, :96, 6] = np.asarray(inp["mla_kn_w"], f32)
    vec[:, :, 8:16] = np.asarray(inp["ssd_conv_b"], f32).reshape(L, 8, 128).transpose(0, 2, 1)
    w["vec"] = vec
    w["convw"] = np.ascontiguousarray(np.asarray(inp["ssd_conv_w"], f32).reshape(L, 5, 8, 128).transpose(0, 3, 2, 1))
    row = np.zeros((L, 552), f32)
    row[:, 0:16] = np.asarray(inp["ssd_dt_bias"], f32).reshape(L, 16)
    row[:, 16:32] = np.asarray(inp["ssd_a_log"], f32).reshape(L, 16)
    row[:, 32:40] = np.asarray(inp["ssd_d"], f32)
    row[:, 40:552] = np.asarray(inp["ssd_norm_w"], f32)
    w["row"] = row
    rpb = np.asarray(inp["na_rpb"], f32)
    w["nab"] = np.stack([_na_bias_tables(rpb[l]) for l in range(L)], 0)
    return w


BF_W = ["winfm", "winz", "winvdt", "wuq", "wukvk", "wukvv", "wout", "wff1", "wff2", "nab"]


TILES = [(0, 512), (512, 512), (1024, 512), (1536, 512), (2048, 256)]


class Prog:
    def __init__(self, NB, L, dbg=()):
        self.NB, self.L, self.J = NB, L, NB + 1
        self.dbg = set(dbg)
        self.nc = bass.Bass("TRN2", target_bir_lowering=False)
        self.es = ExitStack()
        self.s = Sched(self.nc, self.es)
        self.dr = {}
        self.drb = {}

    def din(self, name, shape, dt=F32):
        self.dr[name] = self.nc.dram_tensor(name, list(shape), dt, kind="ExternalInput").ap()
        self.drb[name] = Buf()

    def dscr(self, name, shape, dt):
        kind = "ExternalOutput" if name in self.dbg else "Internal"
        self.dr[name] = self.nc.dram_tensor(name, list(shape), dt, kind=kind).ap()
        self.drb[name] = Buf()

    def sb(self, es, name, shape, dt):
        self.uid = getattr(self, "uid", 0) + 1
        return TT(es.enter_context(self.nc.sbuf_tensor("sb%d_%s" % (self.uid, name), list(shape), dt)))

    def wb(self, name, l):
        return self.drb.setdefault((name, l), Buf())

    def psum(self):
        p = self.ps[self.psi % 7]
        self.psi += 1
        return p

    def warm(self, n=1, cols=512):
        for _ in range(n):
            self.s.op("pe", lambda e: e.matmul(self.ps[7].t[:, 0:cols], self.ident.t[:], self.wrhs.t[:, 0:cols], start=True, stop=True))

    def rot(self, lst, key):
        i = self.rr.get(key, 0)
        self.rr[key] = i + 1
        return lst[i % len(lst)]

    def build(self):
        nc, s, es, NB, L, J = self.nc, self.s, self.es, self.NB, self.L, self.J
        self.rr = {}
        self.din("xT", [NB, D, T])
        self.din("cT", [128, 8, J])
        wshapes = dict(wada=[48, 128, 8, 128], bada=[128, 48], nw1=[128, 8], nw2=[128, 8],
                       winfm=[16, 128, 8, 128], winz=[128, 8, 512], winvdt=[128, 8, 272],
                       wuq=[128, 4, 2, 96], wukvk=[128, 4, 96], wukvv=[128, 256],
                       wout=[8, 128, 8, 128], wff1=[32, 128, 8, 128], wff2=[8, 128, 32, 128],
                       vec=[128, 32], convw=[128, 8, 5], row=[552], nab=[4, 20, 128, 512])
        self.wshapes = wshapes
        for k, shp in wshapes.items():
            self.din(k, [L] + shp)
        cshapes = dict(ident=[128, 128], ones=[128, 128], blk64=[128, 128], cosT=[128, T], sinT=[128, T],
                       prot=[128, 128], tri_f=[128, 128], tri_b=[128, 128], negm_f=[128, 128], negm_b=[128, 128],
                       sel64=[128, 64])
        for k, shp in cshapes.items():
            self.din(k, shp)
        self.dr["out"] = nc.dram_tensor("out", [NB, D, SEQ], F32, kind="ExternalOutput").ap()
        self.drb["out"] = Buf()
        for k in BF_W:
            self.dscr(k + "_bf", [L] + wshapes[k], BF16)
        self.dscr("XS", [NB, D, T], F32)
        self.dscr("QKNA", [NB, 512, T], BF16)
        self.dscr("VNA", [NB, T, 256], BF16)
        self.dscr("ZS", [NB, T, 512], BF16)
        self.dscr("XBC", [NB, D, T], BF16)
        self.dscr("DTS", [NB, T, 16], F32)
        self.dscr("QM", [NB, 4, 96, T], BF16)
        self.dscr("KM", [NB, 4, 96, T], BF16)
        self.dscr("VM", [NB, T, 256], BF16)
        self.dscr("MIXT", [NB, D, T], BF16)
        dr, drb = self.dr, self.drb

        pairs = [es.enter_context(nc.psum_tensor("pp%d" % i, [128, 1024], F32)) for i in range(4)]
        self.ps = [TT(pairs[i // 2][:, (i % 2) * 512:(i % 2 + 1) * 512], ps=True) for i in range(8)]
        self.pp = [TT(pairs[i][:, :], ps=True) for i in range(4)]
        self.psi = 0

        def convert(l):
            for k in BF_W:
                shp = wshapes[k]
                src, dst = dr[k][l], dr[k + "_bf"][l]
                if len(shp) == 4 and (shp[1] == 128 or k == "nab"):
                    step = max(1, (1 << 20) // (shp[1] * shp[2] * shp[3]))
                    if k == "nab":
                        for h in range(4):
                            for i in range(0, shp[1], 5):
                                s.dma("pool", dst[h, i:i + 5], src[h, i:i + 5], writes=[self.wb(k + "_bf", l)])
                    else:
                        for i in range(0, shp[0], step):
                            s.dma("pool", dst[i:i + step], src[i:i + step], writes=[self.wb(k + "_bf", l)])
                else:
                    s.dma("pool", dst, src, writes=[self.wb(k + "_bf", l)])
        self.convert = convert
        convert(0)

        def cload(name, dt, rows=128):
            t = self.sb(es, "c_" + name, cshapes[name], dt)
            s.dma("pool", t.t[:], dr[name], writes=[t])
            return t
        self.ident = cload("ident", BF16)
        self.ones = cload("ones", BF16)
        self.blk64 = cload("blk64", BF16)
        self.prot = cload("prot", BF16)
        self.onesf = cload("ones", F32) if False else None
        self.tri = [cload("tri_f", F32), cload("tri_b", F32)]
        self.negmb = [cload("negm_f", BF16), cload("negm_b", BF16)]
        self.sel64 = cload("sel64", F32)
        self.onesf = self.sb(es, "onesf", [128, 128], F32)
        s.dma("sp", self.onesf.t[:], dr["ones"], writes=[self.onesf])
        self.wrhs = self.sb(es, "wrhs", [128, 512], BF16)
        s.op("dve", lambda e: e.memset(self.wrhs.t[:], 1.0), writes=[self.wrhs])
        self.cst = self.sb(es, "cst", [128, 8], F32)
        import math
        for i, v in enumerate([EPS, math.log(0.125), math.log(96 ** -0.5), 1.0, 0.0]):
            s.op("dve", lambda e: e.memset(self.cst.t[:, i:i + 1], v), writes=[self.cst])
        self.sc = self.sb(es, "silu_c", [128, 8, J], F32)
        s.dma("sp", self.sc.t[:], dr["cT"], writes=[self.sc])
        s.op("act", lambda e: e.activation(out=self.sc.t[:], in_=self.sc.t[:], func=AF.Silu), reads=[self.sc], writes=[self.sc])
        self.mod = self.sb(es, "mod", [128, 48, J], F32)
        self.A1 = self.sb(es, "A1", [128, 8, J], F32)
        self.A2 = self.sb(es, "A2", [128, 8, J], F32)
        self.vec = self.sb(es, "vec", [128, 32], F32)
        self.nw = self.sb(es, "nw", [128, 16], F32)
        self.bada = self.sb(es, "bada", [128, 48], F32)
        self.rowbc = self.sb(es, "rowbc", [128, 552], F32)
        self.convw = self.sb(es, "convw", [128, 8, 5], F32)
        s.barrier()

        for l in range(L):
            last = (l == L - 1)
            import os
            stop = int(os.environ.get("KSTOP", "9"))
            if stop >= 1:
                self.phase_A(l)
                s.barrier()
            if l + 1 < L:
                self.convert(l + 1)
            if stop >= 2:
                self.phase_B(l)
                s.barrier()
            if stop >= 3:
                self.phase_att(l, last)
                s.barrier()
            if stop >= 4:
                self.phase_ssd(l, last)
                s.barrier()
            if stop >= 5:
                self.phase_FG(l, last)
                s.barrier()
        s.barrier()
        self.es.close()
        return nc

    def phase_A(self, l):
        nc, s, J = self.nc, self.s, self.J
        dr, drb = self.dr, self.drb
        with ExitStack() as es:
            wts = [self.sb(es, "wada%d" % i, [128, 6, 8, 128], F32) for i in range(2)]
            s.dma("sp", self.vec.t[:], dr["vec"][l], writes=[self.vec])
            s.dma("sp", self.nw.t[:, 0:8], dr["nw1"][l], writes=[self.nw])
            s.dma("sp", self.nw.t[:, 8:16], dr["nw2"][l], writes=[self.nw])
            s.dma("sp", self.bada.t[:], dr["bada"][l], writes=[self.bada])
            s.dma("sp", self.rowbc.t[:], dr["row"][l].partition_broadcast(128), writes=[self.rowbc])
            s.dma("sp", self.convw.t[:], dr["convw"][l], writes=[self.convw])
            ps = self.psum()
            for g in range(8):
                wt = wts[g % 2]
                s.dma("sp", wt.t[:], dr["wada"][l, g * 6:(g + 1) * 6].rearrange("c p k f -> p c k f"), writes=[wt])
                for c in range(6):
                    fc = g * 6 + c
                    for k in range(8):
                        s.op("pe", lambda e: e.matmul(ps.t[:, fc * J:(fc + 1) * J], wt.t[:, c, k, :], self.sc.t[:, k, :],
                                                      start=(k == 0), stop=(k == 7)), reads=[wt, self.sc], writes=[ps])
            s.op("dve", lambda e: e.tensor_tensor(out=self.mod.t[:], in0=ps.t[:, 0:48 * J].rearrange("p (c j) -> p c j", j=J),
                                                  in1=self.bada.t[:].unsqueeze(2).to_broadcast([128, 48, J]), op=ALU.add),
                 reads=[ps, self.bada], writes=[self.mod])
            for (A, off, nwo) in ((self.A1, 8, 0), (self.A2, 32, 8)):
                s.op("dve", lambda e: e.scalar_tensor_tensor(out=A.t[:], in0=self.mod.t[:, off:off + 8, :], scalar=1.0,
                                                             in1=self.nw.t[:, nwo:nwo + 8].unsqueeze(2).to_broadcast([128, 8, J]),
                                                             op0=ALU.add, op1=ALU.mult),
                     reads=[self.mod, self.nw], writes=[A])

    def norm_mod(self, xt, n, A, sh_off, j, xm, sq, rstd, tmp):
        s = self.s
        s.op("act", lambda e: e.activation(out=sq.t[:, :, 0:n], in_=xt.t[:, :, 0:n], func=AF.Square), reads=[xt], writes=[sq])
        ps = self.psum()
        for k in range(8):
            s.op("pe", lambda e: e.matmul(ps.t[:, 0:n], self.ones.t[:], sq.t[:, k, 0:n], start=(k == 0), stop=(k == 7)),
                 reads=[sq, self.ones], writes=[ps])
        s.op("act", lambda e: e.activation(out=rstd.t[:, 0:n], in_=ps.t[:, 0:n], func=AF.Ln, scale=1.0 / D, bias=self.cst.t[:, 0:1]),
             reads=[ps, self.cst], writes=[rstd])
        s.op("act", lambda e: e.activation(out=rstd.t[:, 0:n], in_=rstd.t[:, 0:n], func=AF.Exp, scale=-0.5), reads=[rstd], writes=[rstd])
        for k in range(8):
            tm = self.rot(tmp, "nm_tmp")
            s.op("dve", lambda e: e.scalar_tensor_tensor(out=tm.t[:, 0:n], in0=xt.t[:, k, 0:n], scalar=A.t[:, k, j:j + 1],
                                                         in1=rstd.t[:, 0:n], op0=ALU.mult, op1=ALU.mult),
                 reads=[xt, A, rstd], writes=[tm])
            s.op("act", lambda e: e.activation(out=xm.t[:, k, 0:n], in_=tm.t[:, 0:n], func=AF.Identity,
                                               bias=self.mod.t[:, sh_off + k, j:j + 1]),
                 reads=[tm, self.mod], writes=[xm])

    def fm_rstd(self, sqs, ones_ap, ones_tt, P, n, cnt, rstd, lnb=None):
        s = self.s
        ps = self.psum()
        for i, (tt, ap) in enumerate(sqs):
            s.op("pe", lambda e: e.matmul(ps.t[0:P, 0:n], ones_ap, ap, start=(i == 0), stop=(i == len(sqs) - 1)),
                 reads=[tt, ones_tt], writes=[ps])
        s.op("act", lambda e: e.activation(out=rstd.t[0:P, 0:n], in_=ps.t[0:P, 0:n], func=AF.Ln, scale=1.0 / cnt, bias=self.cst.t[0:P, 0:1]),
             reads=[ps, self.cst], writes=[rstd])
        if lnb is None:
            s.op("act", lambda e: e.activation(out=rstd.t[0:P, 0:n], in_=rstd.t[0:P, 0:n], func=AF.Exp, scale=-0.5),
                 reads=[rstd], writes=[rstd])
        else:
            s.op("act", lambda e: e.activation(out=rstd.t[0:P, 0:n], in_=rstd.t[0:P, 0:n], func=AF.Exp, scale=-0.5,
                                               bias=self.cst.t[0:P, lnb:lnb + 1]),
                 reads=[rstd, self.cst], writes=[rstd])

    def phase_B(self, l):
        nc, s, J, NB = self.nc, self.s, self.J, self.NB
        dr, drb = self.dr, self.drb
        with ExitStack() as es:
            winfm = self.sb(es, "winfm", [128, 16, 8, 128], BF16)
            winz = self.sb(es, "winz", [128, 8, 512], BF16)
            winvdt = self.sb(es, "winvdt", [128, 8, 272], BF16)
            wuq = self.sb(es, "wuq", [128, 4, 2, 96], BF16)
            wukvk = self.sb(es, "wukvk", [128, 4, 96], BF16)
            wukvv = self.sb(es, "wukvv", [128, 256], BF16)
            cosb = [self.sb(es, "cosb%d" % i, [128, 512], F32) for i in range(2)]
            sinb = [self.sb(es, "sinb%d" % i, [128, 512], F32) for i in range(2)]
            for i in range(0, 16, 4):
                s.dma("sp", winfm.t[:, i:i + 4], dr["winfm_bf"][l, i:i + 4].rearrange("c p k f -> p c k f"),
                      reads=[self.wb("winfm_bf", l)], writes=[winfm])
            for (tt, nm) in ((winz, "winz_bf"), (winvdt, "winvdt_bf"), (wuq, "wuq_bf"), (wukvk, "wukvk_bf"), (wukvv, "wukvv_bf")):
                s.dma("sp", tt.t[:], dr[nm][l], reads=[self.wb(nm, l)], writes=[tt])
            xts = [self.sb(es, "xt%d" % i, [128, 8, 512], F32) for i in range(2)]
            sq = self.sb(es, "sq", [128, 8, 512], BF16)
            xm = self.sb(es, "xm", [128, 8, 512], BF16)
            rstd = self.sb(es, "rstd", [128, 512], F32)
            tmp = [self.sb(es, "tmp%d" % i, [128, 512], F32) for i in range(8)]
            ubuf = [self.sb(es, "u%d" % i, [128, 512], F32) for i in range(5)]
            sqb = [self.sb(es, "sqb%d" % i, [128, 512], BF16) for i in range(5)]
            rs2 = [self.sb(es, "rs2_%d" % i, [128, 512], F32) for i in range(4)]
            obf = [self.sb(es, "obf%d" % i, [128, 512], BF16) for i in range(8)]
            cqn = self.sb(es, "cqn", [128, 2, 512], BF16)
            ckvn = self.sb(es, "ckvn", [128, 512], BF16)
            krp = self.sb(es, "krp", [128, 512], F32)
            vst = [self.sb(es, "vst%d" % i, [128, 4, 256], BF16) for i in range(2)]
            dst_ = [self.sb(es, "dst%d" % i, [128, 4, 16], F32) for i in range(2)]
            zst = [self.sb(es, "zst%d" % i, [128, 4, 512], BF16) for i in range(2)]
            vmst = [self.sb(es, "vmst%d" % i, [128, 4, 256], BF16) for i in range(2)]
            xsrc = "xT" if l == 0 else "XS"

            def head_group_gen(items, n, t0, cs_t):
                for it in items:
                    it["ps"] = self.psum()
                    it["mm"](it["ps"])
                yield
                for it in items:
                    P = it["P"]
                    u = it["u"] = self.rot(ubuf, "u")
                    q = it["q"] = self.rot(sqb, "sqb")
                    src_ps = it["ps"]
                    if it["extra"] is None:
                        s.op("act", lambda e: e.activation(out=u.t[0:P, 0:n], in_=src_ps.t[0:P, 0:n], func=AF.Copy), reads=[src_ps], writes=[u])
                    else:
                        s.op("dve", lambda e: e.tensor_tensor(out=u.t[0:P, 0:n], in0=src_ps.t[0:P, 0:n], in1=it["extra"].t[0:P, 0:n], op=ALU.add),
                             reads=[src_ps, it["extra"]], writes=[u])
                yield
                for it in items:
                    P, u, q = it["P"], it["u"], it["q"]
                    s.op("act", lambda e: e.activation(out=q.t[0:P, 0:n], in_=u.t[0:P, 0:n], func=AF.Square), reads=[u], writes=[q])
                yield
                for it in items:
                    P, q = it["P"], it["q"]
                    ps = it["ps2"] = self.psum()
                    if P == 128:
                        s.op("pe", lambda e: e.matmul(ps.t[:, 0:n], self.blk64.t[:], q.t[:, 0:n], start=True, stop=True), reads=[q, self.blk64], writes=[ps])
                    else:
                        s.op("pe", lambda e: e.matmul(ps.t[0:P, 0:n], self.ones.t[0:P, 0:P], q.t[0:P, 0:n], start=True, stop=True), reads=[q, self.ones], writes=[ps])
                yield
                for it in items:
                    P, ps = it["P"], it["ps2"]
                    r = it["r"] = self.rot(rs2, "rs2")
                    cnt = 64 if P == 128 else P
                    s.op("act", lambda e: e.activation(out=r.t[0:P, 0:n], in_=ps.t[0:P, 0:n], func=AF.Ln, scale=1.0 / cnt, bias=self.cst.t[0:P, 0:1]),
                         reads=[ps, self.cst], writes=[r])
                yield
                for it in items:
                    P, r, lnb = it["P"], it["r"], it["lnb"]
                    if lnb is None:
                        s.op("act", lambda e: e.activation(out=r.t[0:P, 0:n], in_=r.t[0:P, 0:n], func=AF.Exp, scale=-0.5), reads=[r], writes=[r])
                    else:
                        s.op("act", lambda e: e.activation(out=r.t[0:P, 0:n], in_=r.t[0:P, 0:n], func=AF.Exp, scale=-0.5, bias=self.cst.t[0:P, lnb:lnb + 1]),
                             reads=[r, self.cst], writes=[r])
                yield
                for it in items:
                    P, u, r, wcol = it["P"], it["u"], it["r"], it["wcol"]
                    o = it["o"] = self.rot(obf, "obf")
                    s.op("dve", lambda e: e.scalar_tensor_tensor(out=o.t[0:P, 0:n], in0=u.t[0:P, 0:n], scalar=self.vec.t[0:P, wcol:wcol + 1],
                                                                 in1=r.t[0:P, 0:n], op0=ALU.mult, op1=ALU.mult),
                         reads=[u, self.vec, r], writes=[o])
                if items[0]["P"] == 96:
                    P = 96
                    cT_, sT_ = cs_t
                    yield
                    for it in items:
                        o = it["o"]
                        pr = it["pr"] = self.psum()
                        s.op("pe", lambda e: e.matmul(pr.t[0:P, 0:n], self.prot.t[0:P, 0:P], o.t[0:P, 0:n], start=True, stop=True),
                             reads=[o, self.prot], writes=[pr])
                    yield
                    for it in items:
                        o, pr = it["o"], it["pr"]
                        t1 = it["t1"] = self.rot(tmp, "nm_tmp")
                        t2 = it["t2"] = self.rot(tmp, "nm_tmp")
                        s.op("dve", lambda e: e.tensor_tensor(out=t1.t[0:P, 0:n], in0=pr.t[0:P, 0:n], in1=sT_.t[0:P, 0:n], op=ALU.mult),
                             reads=[pr, sT_], writes=[t1])
                        s.op("pool", lambda e: e.tensor_tensor(out=t2.t[0:P, 0:n], in0=o.t[0:P, 0:n], in1=cT_.t[0:P, 0:n], op=ALU.mult),
                             reads=[o, cT_], writes=[t2])
                    yield
                    for it in items:
                        t1, t2 = it["t1"], it["t2"]
                        o2 = self.rot(obf, "obf")
                        s.op("dve", lambda e: e.tensor_tensor(out=o2.t[0:P, 0:n], in0=t1.t[0:P, 0:n], in1=t2.t[0:P, 0:n], op=ALU.add),
                             reads=[t1, t2], writes=[o2])
                        it["o"] = o2
                yield
                for it in items:
                    P, o = it["P"], it["o"]
                    s.dma("pool", it["dst"], o.t[0:P, 0:n], reads=[o], writes=[drb[it["dname"]]])

            def head_group(items, n, t0, cs_t, fillers=()):
                fillers = list(fillers)
                for _ in head_group_gen(items, n, t0, cs_t):
                    if fillers:
                        fillers.pop(0)()
                return fillers

            for b in range(NB):
                for ti, (t0, n) in enumerate(TILES):
                    j = NB if ti == 4 else b
                    xt = self.rot(xts, "xt")
                    s.dma("sp", xt.t[:, :, 0:n], dr[xsrc][b].rearrange("(k p) t -> p k t", p=128)[:, :, t0:t0 + n],
                          reads=[drb[xsrc]], writes=[xt])
                    cT_, sT_ = self.rot(cosb, "cosb"), self.rot(sinb, "sinb")
                    s.dma("sp", cT_.t[:, 0:n], dr["cosT"][:, t0:t0 + n], writes=[cT_])
                    s.dma("sp", sT_.t[:, 0:n], dr["sinT"][:, t0:t0 + n], writes=[sT_])
                    self.norm_mod(xt, n, self.A1, 0, j, xm, sq, rstd, tmp)
                    nt = n // 128

                    def proj_mm(fc, P=128):
                        def f(ps):
                            for k in range(8):
                                s.op("pe", lambda e: e.matmul(ps.t[0:P, 0:n], winfm.t[:, fc, k, 0:P], xm.t[:, k, 0:n], start=(k == 0), stop=(k == 7)),
                                     reads=[winfm, xm], writes=[ps])
                        return f

                    def proj(fc, P=128):
                        ps = self.psum()
                        proj_mm(fc, P)(ps)
                        return ps
                    fillers = []

                    def xbc_block(c):
                        def f():
                            ps = proj(4 + c)
                            o = self.rot(obf, "obf")
                            if c % 2 == 0:
                                s.op("act", lambda e: e.activation(out=o.t[:, 0:n], in_=ps.t[:, 0:n], func=AF.Copy), reads=[ps], writes=[o])
                            else:
                                s.op("dve", lambda e: e.tensor_copy(out=o.t[:, 0:n], in_=ps.t[:, 0:n]), reads=[ps], writes=[o])
                            s.dma("pool", dr["XBC"][b, c * 128:(c + 1) * 128, t0:t0 + n], o.t[:, 0:n], reads=[o], writes=[drb["XBC"]])
                        return f
                    vs, ds, zs = self.rot(vst, "vst"), self.rot(dst_, "dst"), self.rot(zst, "zst")

                    def tm_block(tc):
                        def f():
                            ps = self.psum()
                            for k in range(8):
                                s.op("pe", lambda e: e.matmul(ps.t[:, 0:272], xm.t[:, k, tc * 128:(tc + 1) * 128], winvdt.t[:, k, :], start=(k == 0), stop=(k == 7)),
                                     reads=[winvdt, xm], writes=[ps])
                            s.op("act", lambda e: e.activation(out=vs.t[:, tc, :], in_=ps.t[:, 0:256], func=AF.Copy), reads=[ps], writes=[vs])
                            s.op("act", lambda e: e.activation(out=ds.t[:, tc, :], in_=ps.t[:, 256:272], func=AF.Copy), reads=[ps], writes=[ds])
                            ps2 = self.psum()
                            for k in range(8):
                                s.op("pe", lambda e: e.matmul(ps2.t[:, :], xm.t[:, k, tc * 128:(tc + 1) * 128], winz.t[:, k, :], start=(k == 0), stop=(k == 7)),
                                     reads=[winz, xm], writes=[ps2])
                            s.op("dve", lambda e: e.tensor_copy(out=zs.t[:, tc, :], in_=ps2.t[:, :]), reads=[ps2], writes=[zs])
                        return f
                    fillers = [xbc_block(c) for c in range(8)] + [tm_block(tc) for tc in range(nt)]
                    fillers = head_group([dict(mm=proj_mm(fc), P=128, wcol=0 if fc < 2 else 1, lnb=1 if fc < 2 else None,
                                     dst=dr["QKNA"][b, fc * 128:(fc + 1) * 128, t0:t0 + n], dname="QKNA", extra=None) for fc in range(4)], n, t0, None, fillers)
                    us = []
                    for c in range(2):
                        ps = proj(12 + c)
                        u = self.rot(ubuf, "u")
                        s.op("act", lambda e: e.activation(out=u.t[:, 0:n], in_=ps.t[:, 0:n], func=AF.Copy), reads=[ps], writes=[u])
                        q = self.rot(sqb, "sqb")
                        s.op("act", lambda e: e.activation(out=q.t[:, 0:n], in_=ps.t[:, 0:n], func=AF.Square), reads=[ps], writes=[q])
                        us.append((u, q))
                    ps = proj(14)
                    ukv = self.rot(ubuf, "u")
                    s.op("act", lambda e: e.activation(out=ukv.t[:, 0:n], in_=ps.t[:, 0:n], func=AF.Copy), reads=[ps], writes=[ukv])
                    qkv = self.rot(sqb, "sqb")
                    s.op("act", lambda e: e.activation(out=qkv.t[:, 0:n], in_=ps.t[:, 0:n], func=AF.Square), reads=[ps], writes=[qkv])
                    ps = proj(15, 96)
                    s.op("dve", lambda e: e.tensor_copy(out=krp.t[0:96, 0:n], in_=ps.t[0:96, 0:n]), reads=[ps], writes=[krp])
                    r = self.rot(rs2, "rs2")
                    self.fm_rstd([(q, q.t[:, 0:n]) for (u, q) in us], self.ones.t[:], self.ones, 128, n, 256, r)
                    r2 = self.rot(rs2, "rs2")
                    self.fm_rstd([(qkv, qkv.t[:, 0:n])], self.ones.t[:], self.ones, 128, n, 128, r2)
                    for c in range(2):
                        s.op("dve", lambda e: e.scalar_tensor_tensor(out=cqn.t[:, c, 0:n], in0=us[c][0].t[:, 0:n], scalar=self.vec.t[:, 2 + c:3 + c],
                                                                     in1=r.t[:, 0:n], op0=ALU.mult, op1=ALU.mult),
                             reads=[us[c][0], self.vec, r], writes=[cqn])
                    s.op("dve", lambda e: e.scalar_tensor_tensor(out=ckvn.t[:, 0:n], in0=ukv.t[:, 0:n], scalar=self.vec.t[:, 4:5],
                                                                 in1=r2.t[:, 0:n], op0=ALU.mult, op1=ALU.mult),
                         reads=[ukv, self.vec, r2], writes=[ckvn])

                    def uq_mm(h):
                        def f(ps):
                            for c in range(2):
                                s.op("pe", lambda e: e.matmul(ps.t[0:96, 0:n], wuq.t[:, h, c, :], cqn.t[:, c, 0:n], start=(c == 0), stop=(c == 1)),
                                     reads=[wuq, cqn], writes=[ps])
                        return f

                    def uk_mm(h):
                        def f(ps):
                            s.op("pe", lambda e: e.matmul(ps.t[0:96, 0:n], wukvk.t[:, h, :], ckvn.t[:, 0:n], start=True, stop=True),
                                 reads=[wukvk, ckvn], writes=[ps])
                        return f
                    fillers = head_group([dict(mm=uq_mm(h), P=96, wcol=5, lnb=2, dst=dr["QM"][b, h, :, t0:t0 + n], dname="QM", extra=None) for h in range(4)],
                                         n, t0, (cT_, sT_), fillers)
                    fillers = head_group([dict(mm=uk_mm(h), P=96, wcol=6, lnb=None, dst=dr["KM"][b, h, :, t0:t0 + n], dname="KM", extra=krp) for h in range(4)],
                                         n, t0, (cT_, sT_), fillers)
                    for f in fillers:
                        f()
                    s.dma("pool", dr["VNA"][b, t0:t0 + n, :].rearrange("(c p) f -> p c f", p=128), vs.t[:, 0:nt, :], reads=[vs], writes=[drb["VNA"]])
                    s.dma("pool", dr["DTS"][b, t0:t0 + n, :].rearrange("(c p) f -> p c f", p=128), ds.t[:, 0:nt, :], reads=[ds], writes=[drb["DTS"]])
                    s.dma("pool", dr["ZS"][b, t0:t0 + n, :].rearrange("(c p) f -> p c f", p=128), zs.t[:, 0:nt, :], reads=[zs], writes=[drb["ZS"]])
                    vm = self.rot(vmst, "vmst")
                    for tc in range(nt):
                        ps = self.psum()
                        s.op("pe", lambda e: e.matmul(ps.t[:, 0:256], ckvn.t[:, tc * 128:(tc + 1) * 128], wukvv.t[:], start=True, stop=True),
                             reads=[wukvv, ckvn], writes=[ps])
                        s.op("act", lambda e: e.activation(out=vm.t[:, tc, :], in_=ps.t[:, 0:256], func=AF.Copy), reads=[ps], writes=[vm])
                    s.dma("pool", dr["VM"][b, t0:t0 + n, :].rearrange("(c p) f -> p c f", p=128), vm.t[:, 0:nt, :], reads=[vm], writes=[drb["VM"]])

    def phase_att(self, l, last):
        nc, s, NB = self.nc, self.s, self.NB
        dr, drb = self.dr, self.drb
        with ExitStack() as es:
            kT = [self.sb(es, "kT%d" % i, [128, T], BF16) for i in range(2)]
            va = [self.sb(es, "va%d" % i, [128, 18, 128], BF16) for i in range(2)]
            qT = [self.sb(es, "qT%d" % i, [128, 512], BF16) for i in range(3)]
            bias = [self.sb(es, "bias%d" % i, [128, 8, 512], BF16) for i in range(2)]
            pT = [self.sb(es, "pT%d" % i, [128, 2, 512], BF16) for i in range(3)]
            sbias = [self.sb(es, "sbias%d" % i, [128, 2, 512], F32) for i in range(2)]
            osb = [self.sb(es, "osb%d" % i, [128, 512], F32) for i in range(2)]
            rrow = [self.sb(es, "rrow%d" % i, [128, 512], F32) for i in range(2)]
            omix = [self.sb(es, "omix%d" % i, [64, 512], BF16) for i in range(2)]
            for i in range(2):
                s.op("dve", lambda e: e.memset(va[i].t[:, :, 64:128], 1.0), writes=[va[i]])
            accs = [self.ps[4], self.ps[5]]
            wp = self.pp[0:2]
            pbank = [self.ps[6]]
            self.warm(24)
            self.att_tail = []

            def attend(b, d, k_t, v_t, q_src_ap, nq, chunks, bias_t, mix_rows, q_tok0):
                q = self.rot(qT, "qT")
                s.dma("sp", q.t[0:d, 0:nq], q_src_ap, reads=[drb["QKNA"], drb["QM"]], writes=[q])
                acc = self.rot(accs, "acc")
                npair = len(chunks) // 2
                pend = []

                def pv(item):
                    pi, p = item
                    for jj in range(2):
                        ci = 2 * pi + jj
                        kc = chunks[ci][0]
                        s.op("pe", lambda e: e.matmul(acc.t[:, 0:nq], v_t.t[:, kc, :], p.t[:, jj, 0:nq], start=(ci == 0), stop=(ci == len(chunks) - 1)),
                             reads=[v_t, p], writes=[acc])
                for pi in range(npair):
                    ps = self.rot(wp, "wp")
                    psv = ps.t.rearrange("p (j n) -> p j n", j=2)
                    for jj in range(2):
                        kc = chunks[2 * pi + jj][0]
                        s.op("pe", lambda e: e.matmul(ps.t[:, jj * 512:jj * 512 + nq], k_t.t[0:d, kc * 128:(kc + 1) * 128], q.t[0:d, 0:nq], start=True, stop=True),
                             reads=[k_t, q], writes=[ps])
                    bs = chunks[2 * pi][1]
                    p = self.rot(pT, "pT")
                    if bs is not None:
                        sb_ = self.rot(sbias, "sbias")
                        s.op("dve", lambda e: e.tensor_tensor(out=sb_.t[:, :, 0:nq], in0=psv[:, :, 0:nq], in1=bias_t.t[:, bs:bs + 2, 0:nq], op=ALU.add),
                             reads=[ps, bias_t], writes=[sb_])
                        s.op("act", lambda e: e.activation(out=p.t[:, :, 0:nq], in_=sb_.t[:, :, 0:nq], func=AF.Exp), reads=[sb_], writes=[p])
                    else:
                        s.op("act", lambda e: e.activation(out=p.t[:, :, 0:nq], in_=psv[:, :, 0:nq], func=AF.Exp), reads=[ps], writes=[p])
                    pend.append((pi, p))
                    if len(pend) > 1:
                        pv(pend.pop(0))
                    if pi in (0, 3) and self.att_tail:
                        self.att_tail.pop(0)()
                while pend:
                    pv(pend.pop(0))
                while self.att_tail:
                    self.att_tail.pop(0)()
                st = {}
                self.att_tail = [lambda: _tail1(st, nq, acc, d), lambda: _tail2(st, b, nq, mix_rows, q_tok0)]

            def _tail1(st, nq, acc, d=96):
                rr = st["rr"] = self.rot(rrow, "rrow")
                if d == 64:
                    s.op("act", lambda e: e.activation(out=rr.t[64:128, 0:nq], in_=acc.t[64:128, 0:nq], func=AF.Ln), reads=[acc], writes=[rr])
                    s.op("act", lambda e: e.activation(out=rr.t[64:128, 0:nq], in_=rr.t[64:128, 0:nq], func=AF.Exp, scale=-1.0), reads=[rr], writes=[rr])
                else:
                    s.op("dve", lambda e: e.reciprocal(out=rr.t[64:128, 0:nq], in_=acc.t[64:128, 0:nq]), reads=[acc], writes=[rr])
                st["acc"] = acc

            def _tail2(st, b, nq, mix_rows, q_tok0):
                rr, acc = st["rr"], st["acc"]
                om = self.rot(omix, "omix")
                s.op("dve", lambda e: e.tensor_tensor(out=om.t[:, 0:nq], in0=acc.t[0:64, 0:nq], in1=rr.t[64:128, 0:nq], op=ALU.mult),
                     reads=[acc, rr], writes=[om])
                s.dma("pool", dr["MIXT"][b, mix_rows:mix_rows + 64, q_tok0:q_tok0 + nq], om.t[:, 0:nq], reads=[om], writes=[drb["MIXT"]])

            for b in range(NB):
                for h in range(4):
                    k_t, v_t = self.rot(kT, "kT"), self.rot(va, "va")
                    s.dma("sp", k_t.t[0:64, :], dr["QKNA"][b, 256 + h * 64:256 + (h + 1) * 64, :], reads=[drb["QKNA"]], writes=[k_t])
                    s.dma("sp", v_t.t[:, :, 0:64], dr["VNA"][b, :, h * 64:(h + 1) * 64].rearrange("(c p) f -> p c f", p=128),
                          reads=[drb["VNA"]], writes=[v_t])
                    for g in range(4):
                        tok0, nch, bt0 = NA_GROUPS[g]
                        bias_t = self.rot(bias, "bias")
                        s.dma("sp", bias_t.t[:, 0:nch, :], dr["nab_bf"][l, h, bt0:bt0 + nch].rearrange("c p q -> p c q"),
                              reads=[self.wb("nab_bf", l)], writes=[bias_t])
                        chunks = [(tok0 // 128 + i, i) for i in range(nch)] + [(16, None), (17, None)]
                        attend(b, 64, k_t, v_t, dr["QKNA"][b, h * 64:(h + 1) * 64, g * 512:(g + 1) * 512], 512, chunks, bias_t, h * 64, g * 512)
                    if not last:
                        attend(b, 64, k_t, v_t, dr["QKNA"][b, h * 64:(h + 1) * 64, SEQ:T], 256, [(16, None), (17, None)], None, h * 64, SEQ)
                for h in range(4):
                    k_t, v_t = self.rot(kT, "kT"), self.rot(va, "va")
                    s.dma("sp", k_t.t[0:96, :], dr["KM"][b, h], reads=[drb["KM"]], writes=[k_t])
                    s.dma("sp", v_t.t[:, :, 0:64], dr["VM"][b, :, h * 64:(h + 1) * 64].rearrange("(c p) f -> p c f", p=128),
                          reads=[drb["VM"]], writes=[v_t])
                    for g in range(4):
                        attend(b, 96, k_t, v_t, dr["QM"][b, h, :, g * 512:(g + 1) * 512], 512, [(i, None) for i in range(18)], None, 768 + h * 64, g * 512)
                    if not last:
                        attend(b, 96, k_t, v_t, dr["QM"][b, h, :, SEQ:T], 256, [(16, None), (17, None)], None, 768 + h * 64, SEQ)
            while self.att_tail:
                self.att_tail.pop(0)()

    def phase_ssd(self, l, last):
        nc, s, NB = self.nc, self.s, self.NB
        dr, drb = self.dr, self.drb
        W = 2312
        import os
        WA = int(os.environ.get("KWA", "0"))
        WB = int(os.environ.get("KWB", "0"))
        with ExitStack() as es:
            xp = [self.sb(es, "xp%d" % i, [128, W], BF16) for i in range(2)]
            xsfm = self.sb(es, "xsfm", [128, 4, T], BF16)
            BT = self.sb(es, "BT", [128, 2, T], BF16)
            CT = self.sb(es, "CT", [128, 2, T], BF16)
            xstm = self.sb(es, "xstm", [128, 18, 512], BF16)
            Btm = self.sb(es, "Btm", [128, 18, 256], BF16)
            raw = self.sb(es, "dtraw", [128, 18, 16], F32)
            dt = self.sb(es, "dt", [128, 18, 16], F32)
            a = self.sb(es, "a", [128, 18, 16], F32)
            acum = self.sb(es, "acum", [128, 18, 16], F32)
            nacum = self.sb(es, "nacum", [128, 18, 16], F32)
            atot = self.sb(es, "atot", [128, 18, 16], F32)
            eac = self.sb(es, "eac", [128, 18, 16], F32)
            dte = self.sb(es, "dte", [128, 18, 16], F32)
            etot = self.sb(es, "etot", [128, 18, 16], F32)
            aneg = self.sb(es, "aneg", [128, 16], F32)
            dsk = self.sb(es, "dsk", [128, 8, 64], F32)
            yacc = self.sb(es, "yacc", [128, 18, 512], F32)
            zts = [self.sb(es, "zt%d" % i, [128, 512], BF16) for i in range(3)]
            ymix = xsfm
            H = [self.sb(es, "H%d" % i, [128, 512], F32) for i in range(2)]
            Hbf = [self.sb(es, "Hbf%d" % i, [128, 512], BF16) for i in range(2)]
            xd = [self.sb(es, "xd%d" % i, [128, 512], BF16) for i in range(6)]
            xdd = [self.sb(es, "xdd%d" % i, [128, 512], BF16) for i in range(6)]
            abc = [self.sb(es, "abc%d" % i, [128, 8, 128], F32) for i in range(2)]
            T1 = [self.sb(es, "T1_%d" % i, [128, 8, 128], BF16) for i in range(4)]
            MT = [self.sb(es, "MT%d" % i, [128, 8, 128], BF16) for i in range(6)]
            tb = [self.sb(es, "tb%d" % i, [128, 512], F32) for i in range(4)]
            ssq = self.sb(es, "ssq", [128, 16], F32)
            obf = [self.sb(es, "sobf%d" % i, [128, 512], BF16) for i in range(3)]
            szb = [self.sb(es, "szb%d" % i, [128, 512], BF16) for i in range(3)]
            junk = self.sb(es, "junk", [128, 512], BF16)
            for i in range(2):
                s.op("dve", lambda e: e.memset(xp[i].t[:, 0:2], 0.0), writes=[xp[i]])
                s.op("dve", lambda e: e.memset(xp[i].t[:, 2050:2054], 0.0), writes=[xp[i]])
                s.op("dve", lambda e: e.memset(xp[i].t[:, 2310:2312], 0.0), writes=[xp[i]])
            s.op("act", lambda e: e.activation(out=aneg.t[:], in_=self.rowbc.t[:, 16:32], func=AF.Exp), reads=[self.rowbc], writes=[aneg])
            s.op("dve", lambda e: e.tensor_scalar(out=aneg.t[:], in0=aneg.t[:], scalar1=-1.0, scalar2=None, op0=ALU.mult), reads=[aneg], writes=[aneg])
            s.op("dve", lambda e: e.tensor_copy(out=dsk.t[:], in_=self.rowbc.t[:, 32:40].unsqueeze(2).to_broadcast([128, 8, 64])),
                 reads=[self.rowbc], writes=[dsk])
            nwbc = self.rowbc.t[:, 40:552]
            negm4 = [self.sb(es, "negm4_%d" % i, [128, 4, 128], BF16) for i in range(2)]
            for i in range(2):
                s.op("dve", lambda e: e.tensor_copy(out=negm4[i].t[:], in_=self.negmb[i].t[:].unsqueeze(1).to_broadcast([128, 4, 128])),
                     reads=[self.negmb[i]], writes=[negm4[i]])

            def bf16view(ps):
                return ps.t[:].bitcast(BF16)

            for b in range(NB):
                for c in range(8):
                    x_ = self.rot(xp, "xp")
                    s.dma("sp", x_.t[:, 2:2050], dr["XBC"][b, c * 128:(c + 1) * 128, 0:SEQ], reads=[drb["XBC"]], writes=[x_])
                    s.dma("sp", x_.t[:, 2054:2310], dr["XBC"][b, c * 128:(c + 1) * 128, SEQ:T], reads=[drb["XBC"]], writes=[x_])
                    v = yacc
                    vt = yacc.t[:, 0:5, :].rearrange("p c f -> p (c f)")[:, 0:2308]
                    s.op("dve", lambda e: e.tensor_scalar(out=vt, in0=x_.t[:, 0:2308], scalar1=self.convw.t[:, c, 0:1], scalar2=None, op0=ALU.mult),
                         reads=[x_, self.convw], writes=[v])
                    for jj in range(1, 5):
                        s.op("dve", lambda e: e.scalar_tensor_tensor(out=vt, in0=x_.t[:, jj:jj + 2308], scalar=self.convw.t[:, c, jj:jj + 1],
                                                                     in1=vt, op0=ALU.mult, op1=ALU.add),
                             reads=[x_, self.convw, v], writes=[v])
                    if c < 4:
                        dst, dl, dc = xsfm, xsfm.t[:, c, 0:SEQ], xsfm.t[:, c, SEQ:T]
                    elif c < 6:
                        dst, dl, dc = BT, BT.t[:, c - 4, 0:SEQ], BT.t[:, c - 4, SEQ:T]
                    else:
                        dst, dl, dc = CT, CT.t[:, c - 6, 0:SEQ], CT.t[:, c - 6, SEQ:T]
                    s.op("act", lambda e: e.activation(out=dl, in_=vt[:, 0:SEQ], func=AF.Silu, bias=self.vec.t[:, 8 + c:9 + c]),
                         reads=[v, self.vec], writes=[dst])
                    s.op("act", lambda e: e.activation(out=dc, in_=vt[:, 2052:2308], func=AF.Silu, bias=self.vec.t[:, 8 + c:9 + c]),
                         reads=[v, self.vec], writes=[dst])
                for tc in range(18):
                    ps = self.psum()
                    pv = bf16view(ps)
                    for ci in range(4):
                        s.op("pe", lambda e: e.transpose(pv[:, ci * 128:(ci + 1) * 128], xsfm.t[:, ci, tc * 128:(tc + 1) * 128], self.ident.t[:]),
                             reads=[xsfm, self.ident], writes=[ps])
                    for ci in range(2):
                        s.op("pe", lambda e: e.transpose(pv[:, 512 + ci * 128:512 + (ci + 1) * 128], BT.t[:, ci, tc * 128:(tc + 1) * 128], self.ident.t[:]),
                             reads=[BT, self.ident], writes=[ps])
                    s.op("act", lambda e: e.activation(out=xstm.t[:, tc, :], in_=pv[:, 0:512], func=AF.Copy), reads=[ps], writes=[xstm])
                    s.op("dve", lambda e: e.tensor_copy(out=Btm.t[:, tc, :], in_=pv[:, 512:768]), reads=[ps], writes=[Btm])
                s.dma("sp", raw.t[:], dr["DTS"][b].rearrange("(c p) f -> p c f", p=128), reads=[drb["DTS"]], writes=[raw])
                s.op("dve", lambda e: e.tensor_tensor(out=dt.t[:], in0=raw.t[:], in1=self.rowbc.t[:, 0:16].unsqueeze(1).to_broadcast([128, 18, 16]), op=ALU.add),
                     reads=[raw, self.rowbc], writes=[dt])
                s.op("dve", lambda e: e.tensor_scalar(out=dt.t[:], in0=dt.t[:], scalar1=30.0, scalar2=None, op0=ALU.min), reads=[dt], writes=[dt])
                s.op("act", lambda e: e.activation(out=dt.t[:], in_=dt.t[:], func=AF.Exp), reads=[dt], writes=[dt])
                s.op("act", lambda e: e.activation(out=dt.t[:], in_=dt.t[:], func=AF.Ln, bias=self.cst.t[:, 3:4]), reads=[dt, self.cst], writes=[dt])
                s.op("dve", lambda e: e.tensor_tensor(out=a.t[:], in0=dt.t[:], in1=aneg.t[:].unsqueeze(1).to_broadcast([128, 18, 16]), op=ALU.mult),
                     reads=[dt, aneg], writes=[a])
                for d in range(2):
                    ps = self.psum()
                    s.op("pe", lambda e: e.matmul(ps.t[:, 0:144], self.tri[d].t[:], a.t[:, :, d * 8:(d + 1) * 8], start=True, stop=True),
                         reads=[self.tri[d], a], writes=[ps])
                    s.op("dve", lambda e: e.tensor_copy(out=acum.t[:, :, d * 8:(d + 1) * 8], in_=ps.t[:, 0:144].rearrange("p (c h) -> p c h", h=8)),
                         reads=[ps], writes=[acum])
                    ps = self.psum()
                    s.op("pe", lambda e: e.matmul(ps.t[:, 0:144], self.onesf.t[:], a.t[:, :, d * 8:(d + 1) * 8], start=True, stop=True),
                         reads=[self.onesf, a], writes=[ps])
                    s.op("dve", lambda e: e.tensor_copy(out=atot.t[:, :, d * 8:(d + 1) * 8], in_=ps.t[:, 0:144].rearrange("p (c h) -> p c h", h=8)),
                         reads=[ps], writes=[atot])
                s.op("dve", lambda e: e.tensor_scalar(out=nacum.t[:], in0=acum.t[:], scalar1=-1.0, scalar2=None, op0=ALU.mult), reads=[acum], writes=[nacum])
                s.op("act", lambda e: e.activation(out=eac.t[:], in_=acum.t[:], func=AF.Exp), reads=[acum], writes=[eac])
                s.op("act", lambda e: e.activation(out=etot.t[:], in_=atot.t[:], func=AF.Exp), reads=[atot], writes=[etot])
                s.op("dve", lambda e: e.tensor_tensor(out=dte.t[:], in0=atot.t[:], in1=acum.t[:], op=ALU.subtract), reads=[atot, acum], writes=[dte])
                s.op("act", lambda e: e.activation(out=dte.t[:], in_=dte.t[:], func=AF.Exp), reads=[dte], writes=[dte])
                orders = [[16, 17] + list(range(16)), [17, 16] + list(range(15, -1, -1))]
                touched = set()
                for d in range(2):
                    s.op("dve", lambda e: e.memset(H[d].t[:], 0.0), writes=[H[d]])
                    s.op("dve", lambda e: e.memset(Hbf[d].t[:], 0.0), writes=[Hbf[d]])

                def mk(oi, d):
                    c = orders[d][oi]
                    return dict(oi=oi, d=d, c=c, hs=slice(d * 8, (d + 1) * 8), cs=slice(c * 128, (c + 1) * 128),
                                y=not (last and c >= 16), mt=None)

                def stage_a(cxs):
                    for cx in cxs:
                        c, hs = cx["c"], cx["hs"]
                        x1, x2 = cx["x1"], cx["x2"] = self.rot(xd, "xd"), self.rot(xdd, "xdd")
                        s.op("pool", lambda e: e.tensor_tensor(out=x1.t[:].rearrange("p (h q) -> p h q", q=64), in0=xstm.t[:, c, :].rearrange("p (h q) -> p h q", q=64),
                                                               in1=dt.t[:, c, hs].unsqueeze(2).to_broadcast([128, 8, 64]), op=ALU.mult),
                             reads=[xstm, dt], writes=[x1])
                        s.op("pool", lambda e: e.tensor_tensor(out=x2.t[:].rearrange("p (h q) -> p h q", q=64), in0=x1.t[:].rearrange("p (h q) -> p h q", q=64),
                                                               in1=dte.t[:, c, hs].unsqueeze(2).to_broadcast([128, 8, 64]), op=ALU.mult),
                             reads=[x1, dte], writes=[x2])
                    ys = [cx for cx in cxs if cx["y"]]
                    for cx in ys:
                        c, hs, cs_, d = cx["c"], cx["hs"], cx["cs"], cx["d"]
                        pcb = cx["pcb"] = self.psum()
                        for g in range(2):
                            s.op("pe", lambda e: e.matmul(pcb.t[:, g * 128:(g + 1) * 128], BT.t[:, g, cs_], CT.t[:, g, cs_], start=True, stop=True),
                                 reads=[BT, CT], writes=[pcb])
                        ab = cx["ab"] = self.rot(abc, "abc")
                        s.op("dve", lambda e: e.tensor_tensor(out=ab.t[:], in0=self.tri[d].t[:].unsqueeze(1).to_broadcast([128, 8, 128]),
                                                              in1=a.t[:, c, hs].unsqueeze(2).to_broadcast([128, 8, 128]), op=ALU.mult),
                             reads=[a, self.tri[d]], writes=[ab])
                    for cx in ys:
                        d, ab = cx["d"], cx["ab"]
                        pd = cx["pd"] = [self.psum(), self.psum()]
                        for half in range(2):
                            reg = pd[half].t[:, :]
                            s.op("pe", lambda e: e.matmul(reg, self.ident.t[:], negm4[d].t[:].rearrange("p r l -> p (r l)"), start=True, stop=False),
                                 reads=[self.ident, negm4[d]], writes=[pd[half]])
                            s.op("pe", lambda e: e.matmul(reg, self.onesf.t[:], ab.t[:, half * 4:(half + 1) * 4, :].rearrange("p h l -> p (h l)"), start=False, stop=True),
                                 reads=[self.onesf, ab], writes=[pd[half]])
                    for cx in ys:
                        c, d, pd = cx["c"], cx["d"], cx["pd"]
                        t1 = cx["t1"] = self.rot(T1, "T1")
                        for hh in range(8):
                            reg = pd[hh // 4].t[:, (hh % 4) * 128:(hh % 4 + 1) * 128]
                            s.op("act", lambda e: e.activation(out=t1.t[:, hh, :], in_=reg, func=AF.Exp, bias=nacum.t[:, c, d * 8 + hh:d * 8 + hh + 1]),
                                 reads=[pd[hh // 4], nacum], writes=[t1])
                    for cx in ys:
                        t1, pcb = cx["t1"], cx["pcb"]
                        mt = cx["mt"] = self.rot(MT, "MT")
                        for g in range(2):
                            s.op("dve", lambda e: e.tensor_tensor(out=mt.t[:, g * 4:(g + 1) * 4, :], in0=t1.t[:, g * 4:(g + 1) * 4, :],
                                                                  in1=pcb.t[:, g * 128:(g + 1) * 128].unsqueeze(1).to_broadcast([128, 4, 128]), op=ALU.mult),
                                 reads=[t1, pcb], writes=[mt])
                    return cxs

                def stage_b(cxs):
                    ys = [cx for cx in cxs if cx["mt"] is not None]
                    for cx in ys:
                        d, cs_, x1, mt = cx["d"], cx["cs"], cx["x1"], cx["mt"]
                        py = cx["py"] = self.psum()
                        for hh in range(8):
                            s.op("pe", lambda e: e.matmul(py.t[:, hh * 64:(hh + 1) * 64], mt.t[:, hh, :], x1.t[:, hh * 64:(hh + 1) * 64], start=True, stop=True),
                                 reads=[mt, x1], writes=[py])
                        pyo = cx["pyo"] = self.psum()
                        for g in range(2):
                            s.op("pe", lambda e: e.matmul(pyo.t[:, g * 256:(g + 1) * 256], CT.t[:, g, cs_], Hbf[d].t[:, g * 256:(g + 1) * 256], start=True, stop=True),
                                 reads=[CT, Hbf[d]], writes=[pyo])
                    upd = [cx for cx in cxs if cx["oi"] < 17]
                    for cx in upd:
                        d, c, x2, hs = cx["d"], cx["c"], cx["x2"], cx["hs"]
                        pcs = cx["pcs"] = self.psum()
                        for g in range(2):
                            s.op("pe", lambda e: e.matmul(pcs.t[:, g * 256:(g + 1) * 256], Btm.t[:, c, g * 128:(g + 1) * 128], x2.t[:, g * 256:(g + 1) * 256], start=True, stop=True),
                                 reads=[Btm, x2], writes=[pcs])
                        s.op("pool", lambda e: e.tensor_tensor(out=H[d].t[:].rearrange("p (h q) -> p h q", q=64), in0=H[d].t[:].rearrange("p (h q) -> p h q", q=64),
                                                               in1=etot.t[:, c, hs].unsqueeze(2).to_broadcast([128, 8, 64]), op=ALU.mult),
                             reads=[H[d], etot], writes=[H[d]])
                    for cx in ys:
                        c, hs, pyo = cx["c"], cx["hs"], cx["pyo"]
                        t_ = cx["t_"] = self.rot(tb, "tb")
                        s.op("dve", lambda e: e.tensor_tensor(out=t_.t[:].rearrange("p (h q) -> p h q", q=64), in0=pyo.t[:].rearrange("p (h q) -> p h q", q=64),
                                                              in1=eac.t[:, c, hs].unsqueeze(2).to_broadcast([128, 8, 64]), op=ALU.mult),
                             reads=[pyo, eac], writes=[t_])
                    for cx in upd:
                        d, pcs = cx["d"], cx["pcs"]
                        s.op("dve", lambda e: e.tensor_tensor(out=H[d].t[:], in0=H[d].t[:], in1=pcs.t[:], op=ALU.add), reads=[H[d], pcs], writes=[H[d]])
                        s.op("act", lambda e: e.activation(out=Hbf[d].t[:], in_=H[d].t[:], func=AF.Copy), reads=[H[d]], writes=[Hbf[d]])
                    for cx in ys:
                        c, t_, py = cx["c"], cx["t_"], cx["py"]
                        if c not in touched:
                            touched.add(c)
                            s.op("dve", lambda e: e.tensor_tensor(out=yacc.t[:, c, :], in0=t_.t[:], in1=py.t[:], op=ALU.add), reads=[t_, py], writes=[yacc])
                        else:
                            s.op("dve", lambda e: e.tensor_tensor(out=t_.t[:], in0=t_.t[:], in1=py.t[:], op=ALU.add), reads=[t_, py], writes=[t_])
                            s.op("pool", lambda e: e.tensor_tensor(out=yacc.t[:, c, :], in0=yacc.t[:, c, :], in1=t_.t[:], op=ALU.add), reads=[t_, yacc], writes=[yacc])

                AHEAD = 2
                inflight = {}
                for oi in range(18 + AHEAD):
                    if oi < 18:
                        inflight[oi] = stage_a([mk(oi, d) for d in range(2)])
                    if oi >= AHEAD:
                        stage_b(inflight.pop(oi - AHEAD))
                nch = 16 if last else 18
                G = 3
                for c0 in range(0, nch, G):
                    grp = list(range(c0, min(nch, c0 + G)))
                    tt_, szs, zz = {}, {}, {}
                    for c in grp:
                        t_ = tt_[c] = self.rot(tb, "tb")
                        s.op("pool", lambda e: e.tensor_tensor(out=t_.t[:], in0=xstm.t[:, c, :], in1=dsk.t[:].rearrange("p h q -> p (h q)"), op=ALU.mult),
                             reads=[xstm, dsk], writes=[t_])
                        zt = zz[c] = self.rot(zts, "zt")
                        s.dma("sp", zt.t[:], dr["ZS"][b, c * 128:(c + 1) * 128, :], reads=[drb["ZS"]], writes=[zt])
                    for c in grp:
                        t_ = tt_[c]
                        s.op("pool", lambda e: e.tensor_tensor(out=t_.t[:], in0=t_.t[:], in1=yacc.t[:, c, :], op=ALU.add), reads=[t_, yacc], writes=[t_])
                        sz = szs[c] = self.rot(szb, "szb")
                        s.op("act", lambda e: e.activation(out=sz.t[:], in_=zz[c].t[:], func=AF.Silu), reads=[zz[c]], writes=[sz])
                    for c in grp:
                        t_, sz = tt_[c], szs[c]
                        s.op("dve", lambda e: e.tensor_tensor(out=t_.t[:], in0=t_.t[:], in1=sz.t[:], op=ALU.mult), reads=[t_, sz], writes=[t_])
                    for gi, c in enumerate(grp):
                        s.op("act", lambda e: e.activation(out=junk.t[:], in_=tt_[c].t[:], func=AF.Square, accum_out=ssq.t[:, gi:gi + 1]),
                             reads=[tt_[c]], writes=[junk, ssq])
                    ng = len(grp)
                    s.op("act", lambda e: e.activation(out=ssq.t[:, 4:4 + ng], in_=ssq.t[:, 0:ng], func=AF.Ln, scale=1.0 / 512, bias=self.cst.t[:, 0:1]),
                         reads=[ssq, self.cst], writes=[ssq])
                    s.op("act", lambda e: e.activation(out=ssq.t[:, 8:8 + ng], in_=ssq.t[:, 4:4 + ng], func=AF.Exp, scale=-0.5), reads=[ssq], writes=[ssq])
                    oo = {}
                    for gi, c in enumerate(grp):
                        o = oo[c] = self.rot(obf, "sobf")
                        s.op("dve", lambda e: e.scalar_tensor_tensor(out=o.t[:], in0=tt_[c].t[:], scalar=ssq.t[:, 8 + gi:9 + gi], in1=nwbc, op0=ALU.mult, op1=ALU.mult),
                             reads=[tt_[c], ssq, self.rowbc], writes=[o])
                    pss = {}
                    for c in grp:
                        o = oo[c]
                        ps = pss[c] = self.psum()
                        pv = bf16view(ps)
                        for ci in range(4):
                            s.op("pe", lambda e: e.transpose(pv[:, ci * 128:(ci + 1) * 128], o.t[:, ci * 128:(ci + 1) * 128], self.ident.t[:]),
                                 reads=[o, self.ident], writes=[ps])
                    for gi, c in enumerate(grp):
                        ps = pss[c]
                        pv = bf16view(ps)
                        eng = "act" if gi % 2 == 0 else "dve"
                        if eng == "act":
                            s.op("act", lambda e: e.activation(out=ymix.t[:, :, c * 128:(c + 1) * 128], in_=pv[:, 0:512].rearrange("p (k t) -> p k t", t=128), func=AF.Copy),
                                 reads=[ps], writes=[ymix])
                        else:
                            s.op("dve", lambda e: e.tensor_copy(out=ymix.t[:, :, c * 128:(c + 1) * 128], in_=pv[:, 0:512].rearrange("p (k t) -> p k t", t=128)),
                                 reads=[ps], writes=[ymix])
                ntok = SEQ if last else T
                s.dma("pool", dr["MIXT"][b, 256:768, 0:ntok].rearrange("(k p) t -> p k t", p=128), ymix.t[:, :, 0:ntok], reads=[ymix], writes=[drb["MIXT"]])

    def phase_FG(self, l, last):
        nc, s, NB, J = self.nc, self.s, self.NB, self.J
        dr, drb = self.dr, self.drb
        with ExitStack() as es:
            wout = self.sb(es, "wout", [128, 8, 8, 128], BF16)
            s.dma("sp", wout.t[:], dr["wout_bf"][l].rearrange("c p k f -> p c k f"), reads=[self.wb("wout_bf", l)], writes=[wout])
            xts = [self.sb(es, "fxt%d" % i, [128, 8, 512], F32) for i in range(2)]
            mixs = [self.sb(es, "fmix%d" % i, [128, 8, 512], BF16) for i in range(2)]
            sq = self.sb(es, "fsq", [128, 8, 512], BF16)
            xms = [self.sb(es, "fxm%d" % i, [128, 8, 512], BF16) for i in range(2)]
            rstd = self.sb(es, "frstd", [128, 512], F32)
            tmp = [self.sb(es, "ftmp%d" % i, [128, 512], F32) for i in range(3)]
            hT = self.sb(es, "hT", [128, 32, 512], BF16)
            rl = [self.sb(es, "rl%d" % i, [128, 512], BF16) for i in range(3)]
            w1 = [self.sb(es, "w1_%d" % i, [128, 4, 8, 128], BF16) for i in range(2)]
            w2 = [self.sb(es, "w2_%d" % i, [128, 32, 128], BF16) for i in range(2)]
            xsrc = "xT" if l == 0 else "XS"
            tiles = TILES[:4] if last else TILES
            jobs = [(b, ti, t0, n) for b in range(NB) for ti, (t0, n) in enumerate(tiles)]

            def front(job):
                b, ti, t0, n = job
                j = NB if ti == 4 else b
                xt = self.rot(xts, "fxt")
                s.dma("sp", xt.t[:, :, 0:n], dr[xsrc][b].rearrange("(k p) t -> p k t", p=128)[:, :, t0:t0 + n], reads=[drb[xsrc]], writes=[xt])
                mx = self.rot(mixs, "fmix")
                s.dma("sp", mx.t[:, :, 0:n], dr["MIXT"][b].rearrange("(k p) t -> p k t", p=128)[:, :, t0:t0 + n], reads=[drb["MIXT"]], writes=[mx])
                for fc in range(8):
                    ps = self.psum()
                    for k in range(8):
                        s.op("pe", lambda e: e.matmul(ps.t[:, 0:n], wout.t[:, fc, k, :], mx.t[:, k, 0:n], start=(k == 0), stop=(k == 7)),
                             reads=[wout, mx], writes=[ps])
                    s.op("dve", lambda e: e.scalar_tensor_tensor(out=xt.t[:, fc, 0:n], in0=ps.t[:, 0:n], scalar=self.mod.t[:, 16 + fc, j:j + 1],
                                                                 in1=xt.t[:, fc, 0:n], op0=ALU.mult, op1=ALU.add),
                         reads=[ps, self.mod, xt], writes=[xt])
                xm = self.rot(xms, "fxm")
                self.norm_mod(xt, n, self.A2, 24, j, xm, sq, rstd, tmp)
                return dict(job=job, xt=xt, xm=xm, j=j)

            def ffn1(st):
                b, ti, t0, n = st["job"]
                xm = st["xm"]
                for g in range(8):
                    w = self.rot(w1, "w1")
                    s.dma("sp", w.t[:], dr["wff1_bf"][l, g * 4:(g + 1) * 4].rearrange("c p k f -> p c k f"), reads=[self.wb("wff1_bf", l)], writes=[w])
                    for c in range(4):
                        fc = g * 4 + c
                        ps = self.psum()
                        for k in range(8):
                            s.op("pe", lambda e: e.matmul(ps.t[:, 0:n], w.t[:, c, k, :], xm.t[:, k, 0:n], start=(k == 0), stop=(k == 7)),
                                 reads=[w, xm], writes=[ps])
                        r = self.rot(rl, "rl")
                        s.op("act", lambda e: e.activation(out=r.t[:, 0:n], in_=ps.t[:, 0:n], func=AF.Relu), reads=[ps], writes=[r])
                        s.op("pool", lambda e: e.tensor_tensor(out=hT.t[:, fc, 0:n], in0=r.t[:, 0:n], in1=r.t[:, 0:n], op=ALU.mult), reads=[r], writes=[hT])

            def ffn2(st):
                b, ti, t0, n = st["job"]
                xt, j = st["xt"], st["j"]
                for fc in range(8):
                    w = self.rot(w2, "w2")
                    s.dma("sp", w.t[:], dr["wff2_bf"][l, fc], reads=[self.wb("wff2_bf", l)], writes=[w])
                    ps = self.psum()
                    for k in range(32):
                        s.op("pe", lambda e: e.matmul(ps.t[:, 0:n], w.t[:, k, :], hT.t[:, k, 0:n], start=(k == 0), stop=(k == 31)),
                             reads=[w, hT], writes=[ps])
                    s.op("dve", lambda e: e.scalar_tensor_tensor(out=xt.t[:, fc, 0:n], in0=ps.t[:, 0:n], scalar=self.mod.t[:, 40 + fc, j:j + 1],
                                                                 in1=xt.t[:, fc, 0:n], op0=ALU.mult, op1=ALU.add),
                         reads=[ps, self.mod, xt], writes=[xt])
                dst = "out" if last else "XS"
                s.dma("pool", dr[dst][b].rearrange("(k p) t -> p k t", p=128)[:, :, t0:t0 + n], xt.t[:, :, 0:n], reads=[xt], writes=[drb[dst]])

            cur = front(jobs[0])
            for ji in range(len(jobs)):
                ffn1(cur)
                nxt = front(jobs[ji + 1]) if ji + 1 < len(jobs) else None
                ffn2(cur)
                cur = nxt


_CACHE = {}


def _get_prog(NB, L, dbg=()):
    key = (NB, L, tuple(sorted(dbg)))
    if key not in _CACHE:
        p = Prog(NB, L, dbg)
        p.build()
        _CACHE[key] = p
    return _CACHE[key]


def run(inputs, ncores=NCORES, L=None, dbg=(), trace=False):
    x = np.asarray(inputs["x"], np.float32)
    ctx = np.asarray(inputs["ctx"], np.float32)
    c = np.asarray(inputs["c"], np.float32)
    c_ctx = np.asarray(inputs["c_ctx"], np.float32)
    B = x.shape[0]
    NB = B // ncores
    Lw = inputs["w_ada"].shape[0]
    L = Lw if L is None else L
    w = _prep_weights({k: (np.asarray(v)[:L] if np.asarray(v).ndim >= 1 and np.asarray(v).shape[0] == Lw and k not in ("x", "c", "ctx", "c_ctx") else v)
                       for k, v in inputs.items()})
    cst = _consts()
    prog = _get_prog(NB, L, dbg)
    in_maps = []
    for i in range(ncores):
        sl = slice(i * NB, (i + 1) * NB)
        xT = np.concatenate([x[sl].transpose(0, 2, 1), ctx[sl].transpose(0, 2, 1)], axis=2)
        cc = np.concatenate([c[sl], c_ctx[None]], axis=0)
        cT = np.ascontiguousarray(cc.reshape(NB + 1, 8, 128).transpose(2, 1, 0))
        m = {"xT": np.ascontiguousarray(xT), "cT": cT}
        m.update(w)
        m.update(cst)
        in_maps.append(m)
    res = run_bass_kernel_spmd(prog.nc, in_maps, core_ids=list(range(ncores)), **({"trace": True} if trace else {}))
    out = np.concatenate([r["out"].transpose(0, 2, 1) for r in res.results], axis=0)
    return np.ascontiguousarray(out), res


def kernel(**inputs):
    out, _ = run(inputs)
    return out.astype(np.float32)
```
